# Optimizing a Trainium2 kernel written in Bass

```python
import jax, jax.numpy as jnp
from jax import lax
import numpy as np

D_MODEL = 1024
BATCH = 2
SEQ = 16384
DEPTH = 2

EPS = 1e-6
GRID_W = 64
D_MIX = D_MODEL
HEAD_DIM = 64
ATT_HEADS = (D_MIX // 2) // HEAD_DIM
ATT_KV_HEADS = 2
ATT_GROUP = ATT_HEADS // ATT_KV_HEADS
ATT_WIDTH = ATT_HEADS * HEAD_DIM
KV_WIDTH = ATT_KV_HEADS * HEAD_DIM
ATT_SCALE = HEAD_DIM ** -0.5
Q_BLOCK = 128
ROPE_THETA = 10000.0
ROPE_AXIS_DIM = HEAD_DIM // 2
HGRN_WIDTH = D_MIX // 4
HGRN_DK = 64
HGRN_DV = 64
HGRN_HEADS = HGRN_WIDTH // HGRN_DV
HGRN_CHUNK = 16
CONV_WIDTH = D_MIX - ATT_WIDTH - HGRN_WIDTH
CONV_KERNEL = 31
CONV_PAD = (CONV_KERNEL - 1) // 2
D_FF = 4 * D_MODEL
IN_COLS = ATT_WIDTH + 2 * KV_WIDTH + 5 * HGRN_WIDTH + 2 * CONV_WIDTH

kernel_name = "hybrid_parallel_heads_encoder"


def _in_split_points():
    sizes = [ATT_WIDTH, KV_WIDTH, KV_WIDTH, HGRN_WIDTH, HGRN_WIDTH, HGRN_WIDTH,
             HGRN_WIDTH, HGRN_WIDTH, CONV_WIDTH, CONV_WIDTH]
    points, acc = [], 0
    for s in sizes[:-1]:
        acc += s
        points.append(acc)
    return points


def rms_norm(x, g):
    xf = x.astype(jnp.float32)
    y = xf * lax.rsqrt(jnp.mean(xf * xf, axis=-1, keepdims=True) + EPS)
    return (y * g.astype(jnp.float32)).astype(x.dtype)


def layer_norm(x, g, b):
    xf = x.astype(jnp.float32)
    mu = jnp.mean(xf, axis=-1, keepdims=True)
    xc = xf - mu
    y = xc * lax.rsqrt(jnp.mean(xc * xc, axis=-1, keepdims=True) + EPS)
    return (y * g.astype(jnp.float32) + b.astype(jnp.float32)).astype(x.dtype)


def _rope_angles(pos):
    inv_freq = ROPE_THETA ** (-jnp.arange(0, ROPE_AXIS_DIM, 2, dtype=jnp.float32) / ROPE_AXIS_DIM)
    ang = pos.astype(jnp.float32)[:, None] * inv_freq[None, :]
    return jnp.cos(ang), jnp.sin(ang)


def _rotate(x, cos, sin):
    half = x.shape[-1] // 2
    x1, x2 = x[..., :half], x[..., half:]
    c, s = cos[None, :, None, :], sin[None, :, None, :]
    return jnp.concatenate([x1 * c - x2 * s, x1 * s + x2 * c], axis=-1)


def apply_axial_rope(x, row_cs, col_cs):
    xf = x.astype(jnp.float32)
    out = jnp.concatenate([_rotate(xf[..., :ROPE_AXIS_DIM], *row_cs),
                           _rotate(xf[..., ROPE_AXIS_DIM:], *col_cs)], axis=-1)
    return out.astype(x.dtype)


def block_attention(q, k, v):
    B, L, H, D = q.shape
    nb = L // Q_BLOCK
    qb = q.reshape(B, nb, Q_BLOCK, ATT_KV_HEADS, ATT_GROUP, D).transpose(1, 0, 2, 3, 4, 5)

    def one_block(q_blk):
        s = jnp.einsum("bqhgd,bkhd->bhgqk", q_blk, k).astype(jnp.float32) * ATT_SCALE
        p = jax.nn.softmax(s, axis=-1)
        return jnp.einsum("bhgqk,bkhd->bqhgd", p.astype(v.dtype), v)

    o = lax.map(one_block, qb)
    return o.transpose(1, 0, 2, 3, 4, 5).reshape(B, L, H * D)


def chunk_gated_recurrence(q, k, v, log_f):
    B, L, H, K = q.shape
    V = v.shape[-1]
    C = HGRN_CHUNK
    N = L // C

    def to_chunks(a):
        return a.reshape(B, N, C, H, a.shape[-1]).transpose(0, 3, 1, 2, 4)

    q, k, v, log_f = (to_chunks(a) for a in (q, k, v, log_f))
    b = jnp.cumsum(log_f, axis=3)
    causal = jnp.tril(jnp.ones((C, C), dtype=bool))[:, :, None]
    diff = b[:, :, :, :, None, :] - b[:, :, :, None, :, :]
    decay = jnp.where(causal, jnp.exp(jnp.where(causal, diff, 0.0)), 0.0)
    scores = jnp.einsum("bhnik,bhnjk,bhnijk->bhnij", q, k, decay)
    o_intra = jnp.einsum("bhnij,bhnjv->bhniv", scores, v)
    b_last = b[:, :, :, -1:, :]
    u = jnp.einsum("bhnjk,bhnjv->bhnkv", k * jnp.exp(b_last - b), v)
    chunk_decay = jnp.exp(b_last[:, :, :, 0, :])

    def step(S, inp):
        dec, uu = inp
        return dec[..., None] * S + uu, S

    S0 = jnp.zeros((B, H, K, V), q.dtype)
    _, S_prev = lax.scan(step, S0, (jnp.moveaxis(chunk_decay, 2, 0), jnp.moveaxis(u, 2, 0)))
    S_prev = jnp.moveaxis(S_prev, 0, 2)
    o_inter = jnp.einsum("bhnik,bhnkv->bhniv", q * jnp.exp(b), S_prev)
    return (o_intra + o_inter).transpose(0, 2, 3, 1, 4).reshape(B, L, H, V)


def _layer_lower_bounds(lb_param):
    p = jax.nn.softmax(lb_param.astype(jnp.float32), axis=0)
    return jnp.cumsum(p, axis=0) - p[0:1]


def _forget_gate(z, lb):
    zf = z.astype(jnp.float32)
    log_f = jax.nn.log_sigmoid(zf) + jnp.log1p(lb * jnp.exp(-zf))
    k = (1.0 - lb) * jax.nn.sigmoid(-zf)
    return log_f, k


def hgrn2_bidirectional(q_h, i_h, f_fw, f_bw, g_h, lb_fw, lb_bw, norm_g):
    B, L, _ = q_h.shape
    heads = lambda a: a.astype(jnp.float32).reshape(B, L, HGRN_HEADS, -1)
    q, v = heads(q_h), heads(i_h)
    lb_fw = lb_fw.reshape(HGRN_HEADS, HGRN_DK)
    lb_bw = lb_bw.reshape(HGRN_HEADS, HGRN_DK)
    logf_fw, k_fw = _forget_gate(heads(f_fw), lb_fw)
    logf_bw, k_bw = _forget_gate(heads(f_bw), lb_bw)
    flip = lambda a: jnp.flip(a, axis=1)
    o_fw = chunk_gated_recurrence(q, k_fw, v, logf_fw)
    o_bw = flip(chunk_gated_recurrence(flip(q), flip(k_bw), flip(v), flip(logf_bw)))
    o = rms_norm(o_fw + o_bw, norm_g.reshape(HGRN_HEADS, HGRN_DV))
    o = o.reshape(B, L, HGRN_WIDTH) * jax.nn.silu(g_h.astype(jnp.float32))
    return o.astype(q_h.dtype)


def conformer_conv(a, gate, w_dw, b_dw, ln_g, ln_b):
    u = a * jax.nn.sigmoid(gate)
    y = lax.conv_general_dilated(u, w_dw[:, None, :], window_strides=(1,),
                                 padding=[(CONV_PAD, CONV_PAD)],
                                 dimension_numbers=("NWC", "WIO", "NWC"),
                                 feature_group_count=CONV_WIDTH)
    y = layer_norm(y + b_dw, ln_g, ln_b)
    return jax.nn.silu(y)


def setup_inputs(seed: int = 0) -> dict:
    key = jax.random.key(seed)
    ks = jax.random.split(key, 16)
    n = lambda k, shape: jax.random.normal(k, shape, jnp.float32)
    return {
        "x": n(ks[0], (BATCH, SEQ, D_MODEL)),
        "w_in": n(ks[1], (DEPTH, D_MODEL, IN_COLS)) * D_MODEL ** -0.5,
        "w_out": n(ks[2], (DEPTH, D_MIX, D_MODEL)) * D_MIX ** -0.5,
        "norm_mix": 1.0 + 0.05 * n(ks[3], (DEPTH, D_MODEL)),
        "norm_mlp": 1.0 + 0.05 * n(ks[4], (DEPTH, D_MODEL)),
        "q_norm": 1.0 + 0.05 * n(ks[5], (DEPTH, HEAD_DIM)),
        "k_norm": 1.0 + 0.05 * n(ks[6], (DEPTH, HEAD_DIM)),
        "hgrn_lb_fwd": 0.5 * n(ks[7], (DEPTH, HGRN_WIDTH)),
        "hgrn_lb_bwd": 0.5 * n(ks[8], (DEPTH, HGRN_WIDTH)),
        "hgrn_norm": 1.0 + 0.05 * n(ks[9], (DEPTH, HGRN_WIDTH)),
        "conv_w": n(ks[10], (DEPTH, CONV_KERNEL, CONV_WIDTH)) * CONV_KERNEL ** -0.5,
        "conv_b": 0.02 * n(ks[11], (DEPTH, CONV_WIDTH)),
        "conv_ln_g": 1.0 + 0.05 * n(ks[12], (DEPTH, CONV_WIDTH)),
        "conv_ln_b": 0.02 * n(ks[13], (DEPTH, CONV_WIDTH)),
        "w_mlp_in": n(ks[14], (DEPTH, D_MODEL, D_FF)) * D_MODEL ** -0.5,
        "w_mlp_out": n(ks[15], (DEPTH, D_FF, D_MODEL)) * D_FF ** -0.5,
    }


def reference(x, w_in, w_out, norm_mix, norm_mlp, q_norm, k_norm, hgrn_lb_fwd, hgrn_lb_bwd,
              hgrn_norm, conv_w, conv_b, conv_ln_g, conv_ln_b, w_mlp_in, w_mlp_out):
    B, L, _ = x.shape
    rows = L // GRID_W
    row_idx = jnp.broadcast_to(jnp.arange(rows)[:, None], (rows, GRID_W)).reshape(L)
    col_idx = jnp.broadcast_to(jnp.arange(GRID_W)[None, :], (rows, GRID_W)).reshape(L)
    row_cs, col_cs = _rope_angles(row_idx), _rope_angles(col_idx)
    lb_fwd = _layer_lower_bounds(hgrn_lb_fwd)
    lb_bwd = _layer_lower_bounds(hgrn_lb_bwd)
    splits = _in_split_points()

    for l in range(DEPTH):
        h = rms_norm(x, norm_mix[l])
        z = h @ w_in[l]
        (q_a, k_a, v_a, q_h, i_h, f_fw, f_bw, g_h, a_c, g_c) = jnp.split(z, splits, axis=-1)

        q_a = rms_norm(q_a.reshape(B, L, ATT_HEADS, HEAD_DIM), q_norm[l])
        k_a = rms_norm(k_a.reshape(B, L, ATT_KV_HEADS, HEAD_DIM), k_norm[l])
        q_a = apply_axial_rope(q_a, row_cs, col_cs)
        k_a = apply_axial_rope(k_a, row_cs, col_cs)
        v_a = v_a.reshape(B, L, ATT_KV_HEADS, HEAD_DIM)
        o_att = block_attention(q_a, k_a, v_a)

        o_hg = hgrn2_bidirectional(q_h, i_h, f_fw, f_bw, g_h,
                                   lb_fwd[l], lb_bwd[l], hgrn_norm[l])

        o_cv = conformer_conv(a_c, g_c, conv_w[l], conv_b[l],
                              conv_ln_g[l], conv_ln_b[l])

        mix = jnp.concatenate([o_att, o_hg.astype(x.dtype), o_cv], axis=-1)
        x = x + mix @ w_out[l]

        h = rms_norm(x, norm_mlp[l])
        x = x + jnp.square(jax.nn.relu(h @ w_mlp_in[l])) @ w_mlp_out[l]
    return x
```

```python
from contextlib import ExitStack
import numpy as np
import concourse.bass as bass
import concourse.mybir as mybir
from concourse.bass_utils import run_bass_kernel_spmd

F32 = mybir.dt.float32
BF16 = mybir.dt.bfloat16
AF = mybir.ActivationFunctionType
ALU = mybir.AluOpType
AX = mybir.AxisListType

D = 1024
DFF = 4096
INC = 2560
EPS = 1e-6
R = 4
NCORES = 8
CH = 32
HALO = 15
KW = 31


class Res:
    __slots__ = ("w", "r", "name", "excl")

    def __init__(self, name="", excl=False):
        self.w = None
        self.r = []
        self.name = name
        self.excl = excl


class Prog:
    ENG = ("pe", "act", "dve", "pool", "sp")

    def __init__(self, nc, es):
        self.nc = nc
        self.es = es
        self.q = {e: [] for e in self.ENG}
        self.sem = {e: es.enter_context(nc.semaphore("sem_" + e)) for e in ("pe", "act", "dve", "pool")}
        self.cnt = {e: 0 for e in self.sem}
        self.seen = {e: {} for e in self.ENG}
        self.dsem = {}
        self.dcnt = {}
        self.drr = {}
        for qn, n in (("sp", 12), ("pool", 6), ("act", 4)):
            self.dsem[qn] = [es.enter_context(nc.semaphore("dma_%s_%d" % (qn, i))) for i in range(n)]
            self.dcnt[qn] = [0] * n
            self.drr[qn] = 0
        self.cctoks = []

    def _waits(self, eng, R_, W_):
        need = {}

        def add(tok):
            if tok is None:
                return
            s, v = tok
            k = id(s)
            if k not in need or need[k][1] < v:
                need[k] = (s, v)

        for r in R_:
            add(r.w)
            if r.excl:
                for t in r.r:
                    add(t)
        for w in W_:
            add(w.w)
            for t in w.r:
                add(t)
        out = []
        own = self.sem.get(eng)
        for k, (s, v) in need.items():
            if eng == "pe" and s is own:
                continue
            if self.seen[eng].get(k, 0) >= v:
                continue
            self.seen[eng][k] = v
            out.append((s, v))
        return out

    def _commit(self, tok, R_, W_):
        for r in R_:
            r.r.append(tok)
        for w in W_:
            w.w = tok
            w.r = []

    def op(self, eng, fn, R_=(), W_=()):
        waits = self._waits(eng, R_, W_)
        self.cnt[eng] += 1
        tok = (self.sem[eng], self.cnt[eng])
        sem = self.sem[eng]

        def c(e):
            for s, v in waits:
                e.wait_ge(s, v)
            fn(e).then_inc(sem, 1)

        self.q[eng].append(c)
        self._commit(tok, R_, W_)
        return tok

    def dma(self, qn, out, in_, R_=(), W_=(), **kw):
        i = self.drr[qn]
        self.drr[qn] = (i + 1) % len(self.dsem[qn])
        s = self.dsem[qn][i]
        prev = self.dcnt[qn][i]
        waits = self._waits(qn, R_, W_)
        k = id(s)
        if prev > 0 and self.seen[qn].get(k, 0) < prev:
            self.seen[qn][k] = prev
            waits.append((s, prev))
        self.dcnt[qn][i] = prev + 16
        tok = (s, prev + 16)

        def c(e):
            for s_, v in waits:
                e.wait_ge(s_, v)
            e.dma_start(out=out, in_=in_, **kw).then_inc(s, 16)

        self.q[qn].append(c)
        self._commit(tok, R_, W_)
        return tok

    def allgather(self, in_t, out_t, R_=(), W_=()):
        waits = self._waits("pool", R_, W_)
        s = self.es.enter_context(self.nc.semaphore("cc_sem%d" % len(self.cctoks)))
        tok = (s, 1)
        self.cctoks.append(tok)
        groups = [[0, 1, 2, 3], [4, 5, 6, 7]]

        def c(e):
            for s_, v in waits:
                e.wait_ge(s_, v)
            e.collective_compute("AllGather", ALU.bypass, replica_groups=groups,
                                 ins=[in_t.ap().opt()], outs=[out_t.ap().opt()]).then_inc(s)

        self.q["pool"].append(c)
        self._commit(tok, R_, W_)
        return tok

    def barrier(self):
        toks = [(self.sem[e], self.cnt[e]) for e in self.sem if self.cnt[e] > 0]
        for qn in self.dsem:
            for s, v in zip(self.dsem[qn], self.dcnt[qn]):
                if v > 0:
                    toks.append((s, v))
        toks.extend(self.cctoks)
        for eng in self.ENG:
            wl = []
            for s, v in toks:
                if self.seen[eng].get(id(s), 0) >= v:
                    continue
                self.seen[eng][id(s)] = v
                wl.append((s, v))

            def c(e, wl=wl):
                for s_, v in wl:
                    e.wait_ge(s_, v)

            self.q[eng].append(c)

    def flush(self):
        nc = self.nc
        q = self.q
        with nc.Block() as block:
            @block.tensor
            def _(e):
                for c in q["pe"]:
                    c(e)

            @block.scalar
            def _(e):
                for c in q["act"]:
                    c(e)

            @block.vector
            def _(e):
                for c in q["dve"]:
                    c(e)

            @block.gpsimd
            def _(e):
                for c in q["pool"]:
                    c(e)

            @block.sync
            def _(e):
                for c in q["sp"]:
                    c(e)
        self.q = {e: [] for e in self.ENG}


class Buf:
    __slots__ = ("t", "r")

    def __init__(self, t):
        self.t = t
        self.r = Res()


class Ring:
    def __init__(self, bufs):
        self.bufs = bufs
        self.i = 0

    def next(self):
        b = self.bufs[self.i]
        self.i = (self.i + 1) % len(self.bufs)
        return b


C_Q, C_K, C_V, C_QH, C_IH, C_FF, C_GH, C_AC, C_GC = 0, 512, 640, 768, 1024, 1280, 1792, 2048, 2304
K_ID, K_MFW, K_MBW, K_SEL, K_SHIFT, K_ONES, K_ONESQ, K_END = 0, 128, 256, 384, 388, 452, 516, 644
PP_NM, PP_NMLP, PP_GN, PP_CW, PP_CB, PP_LNG, PP_LNB, PP_END = 0, 8, 16, 20, 82, 84, 86, 88


def build_program(NT, nlayers=2, dbg=()):
    nc = bass.Bass("TRN2", target_bir_lowering=False)
    L = R * NT
    NBLK = NT // 512
    NSUB = NT // 128
    T = L // 128
    NCH = NT // CH

    def din(name, shape, dt=F32):
        return nc.dram_tensor(name, list(shape), dt, kind="ExternalInput")

    x_in = din("x", [NT, D])
    w_in_d = din("w_in", [2, D, INC])
    w_out_d = din("w_out", [2, D, D])
    w1_d = din("w_mlp_in", [2, D, DFF])
    w2_d = din("w_mlp_out", [2, DFF, D])
    pp_d = din("pp", [2, 128, PP_END])
    qkn_d = din("qkn", [2, 1, 128])
    lbp_d = din("lbp", [1, 1024])
    consts_d = din("consts", [128, K_END])
    ropeC_d = din("ropeC", [128, NT // 128, 64])
    ropeS_d = din("ropeS", [128, NT // 128, 64])
    oh_d = din("oh", [1, 12])
    out_d = nc.dram_tensor("out", [NT, D], F32, kind="ExternalOutput")

    def dscr(name, shape, dt=F32):
        if name in dbg:
            return nc.dram_tensor(name, list(shape), dt, kind="ExternalOutput")
        return nc.dram_tensor(name, list(shape), dt)

    win_bf = dscr("win_bf", [2, D, INC], BF16)
    wout_bf = dscr("wout_bf", [2, D, D], BF16)
    w1_bf = dscr("w1_bf", [2, D, DFF], BF16)
    w2_bf = dscr("w2_bf", [2, DFF, D], BF16)
    qT_d = dscr("qT_d", [128, 4, NT], BF16)
    ag1_in = [dscr("ag1_in%d" % b, [256, 512], BF16) for b in range(NBLK)]
    ag1_out = [[dscr("ag1_out%d_%d" % (l, b), [R * 256, 512], BF16) for b in range(NBLK)] for l in range(2)]
    ag2_in = dscr("ag2_in", [128, 640], F32)
    ag2_out = [dscr("ag2_out%d" % l, [R * 128, 640], F32) for l in range(2)]
    qtT_d = dscr("qtT_d", [64, 2, 4, NT], BF16)
    kt_d = dscr("kt_d", [NT, 512], BF16)
    vh_d = dscr("vh_d", [NT, 256], BF16)
    oiT_d = dscr("oiT_d", [64, 4, NT], F32)
    of_d = dscr("of_d", [64, 4, NT], F32)
    gT_d = dscr("gT_d", [64, 4, NT], BF16)
    uText_d = dscr("uText_d", [128, 2, NT + 2 * HALO], F32)
    mixhg_d = dscr("mixhg_d", [64, 4, NT], BF16)
    mixcv_d = dscr("mixcv_d", [128, 2, NT], BF16)
    mixatt_d = dscr("mixatt_d", [64, 8, NT], BF16)
    x1_d = dscr("x1_d", [NT, D], F32)
    x2_d = dscr("x2_d", [NT, D], F32)

    es = ExitStack()
    P = Prog(nc, es)
    cnt = [0]

    def alloc(stack, kind, shape, dt):
        cnt[0] += 1
        f = nc.sbuf_tensor if kind == "sb" else nc.psum_tensor
        bf_ = Buf(stack.enter_context(f("%s%d" % (kind, cnt[0]), list(shape), dt)))
        bf_.r.excl = (kind == "ps")
        return bf_

    G = es
    consts = alloc(G, "sb", [128, K_END], F32)
    cbf = alloc(G, "sb", [128, 388], BF16)
    pp = alloc(G, "sb", [128, 2, PP_END], F32)
    oh = alloc(G, "sb", [128, 12], F32)
    d_all = alloc(G, "sb", [64, 2, 4, NCH], F32)
    Sst = [alloc(G, "sb", [64, 4, 64], F32) for _ in range(2)]
    Sin = [alloc(G, "sb", [64, 4, 64], F32) for _ in range(2)]

    ident_bf = cbf.t[:, 0:128]
    ident_f = consts.t[:, K_ID:K_ID + 128]
    Mf = [consts.t[:, K_MFW:K_MFW + 128], consts.t[:, K_MBW:K_MBW + 128]]
    sel_f = consts.t[:, K_SEL:K_SEL + 4]
    sel_bf = cbf.t[:, 384:388]
    shiftM = consts.t[:, K_SHIFT:K_SHIFT + 64]
    ones64 = consts.t[0:64, K_ONES:K_ONES + 64]
    onesq = consts.t[:, K_ONESQ:K_ONESQ + 128]

    def TT(eng, out, a, b, op, R_, W_):
        return P.op(eng, lambda e: e.tensor_tensor(out, a, b, op), R_, W_)

    def TS(eng, out, a, s1, s2, op0, op1, R_, W_):
        if s2 is None:
            return P.op(eng, lambda e: e.tensor_scalar(out, a, s1, None, op0), R_, W_)
        return P.op(eng, lambda e: e.tensor_scalar(out, a, s1, s2, op0, op1), R_, W_)

    def STT(eng, out, a, s, b, op0, op1, R_, W_):
        return P.op(eng, lambda e: e.scalar_tensor_tensor(out, a, s, b, op0, op1), R_, W_)

    def CP(eng, out, a, R_, W_):
        if eng == "act":
            return P.op(eng, lambda e: e.copy(out, a), R_, W_)
        return P.op(eng, lambda e: e.tensor_copy(out, a), R_, W_)

    def ACT(out, a, func, R_, W_, **kw):
        return P.op("act", lambda e: e.activation(out, a, func, **kw), R_, W_)

    def RSQRT(out, a, scale, bias, R_, W_):
        ACT(out, a, AF.Sqrt, R_, W_, bias=bias, scale=scale)
        P.op("dve", lambda e: e.reciprocal(out, out), W_, W_)

    def MEMSET(eng, out, v, W_):
        return P.op(eng, lambda e: e.memset(out, v), (), W_)

    def MM(groups, R_, W_):
        def fn(e):
            ins = None
            for (o, l, r, st, sp) in groups:
                ins = e.matmul(o, l, r, start=st, stop=sp)
            return ins
        return P.op("pe", fn, R_, W_)

    def TR(items, R_, W_):
        def fn(e):
            ins = None
            for (o, i, idn) in items:
                ins = e.transpose(o, i, idn)
            return ins
        return P.op("pe", fn, R_, W_)

    def bc(ap, shape, axis):
        return ap.unsqueeze(axis).to_broadcast(list(shape))

    with ExitStack() as ph:
        P.dma("sp", consts.t[:, :], consts_d.ap(), W_=[consts.r])
        P.dma("sp", pp.t[:, :, :], pp_d.ap().rearrange("l p c -> p l c"), W_=[pp.r])
        P.dma("sp", oh.t[:, :], oh_d.ap().partition_broadcast(128), W_=[oh.r])
        CP("dve", cbf.t[:, 0:388], consts.t[:, 0:388], [consts.r], [cbf.r])
        stg = Ring([alloc(ph, "sb", [128, 2048], F32) for _ in range(3)])
        stb = Ring([alloc(ph, "sb", [128, 2048], BF16) for _ in range(3)])
        flip = [0]

        def cast_rows(src, dst, rows, cols, gain_col, l):
            for rc in range(rows // 128):
                for c0 in range(0, cols, 2048):
                    w = min(2048, cols - c0)
                    a = stg.next()
                    b = stb.next()
                    P.dma("sp", a.t[:, 0:w], src[rc * 128:(rc + 1) * 128, c0:c0 + w], W_=[a.r])
                    eng = ("dve", "act", "dve")[flip[0] % 3]
                    flip[0] += 1
                    if gain_col is None:
                        CP(eng, b.t[:, 0:w], a.t[:, 0:w], [a.r], [b.r])
                    elif eng == "act":
                        g = pp.t[:, l, gain_col + rc:gain_col + rc + 1]
                        P.op("act", lambda e, o_=b.t[:, 0:w], i_=a.t[:, 0:w], g_=g: e.mul(o_, i_, g_), [a.r, pp.r], [b.r])
                    else:
                        g = pp.t[:, l, gain_col + rc:gain_col + rc + 1]
                        TS(eng, b.t[:, 0:w], a.t[:, 0:w], g, None, ALU.mult, None, [a.r, pp.r], [b.r])
                    P.dma("pool", dst[rc * 128:(rc + 1) * 128, c0:c0 + w], b.t[:, 0:w], R_=[b.r], W_=[wres])

        wres = Res("weights")
        for l in range(nlayers):
            cast_rows(w_in_d.ap()[l], win_bf.ap()[l], D, INC, PP_NM, l)
            cast_rows(w_out_d.ap()[l], wout_bf.ap()[l], D, D, None, l)
            cast_rows(w1_d.ap()[l], w1_bf.ap()[l], D, DFF, PP_NMLP, l)
            cast_rows(w2_d.ap()[l], w2_bf.ap()[l], DFF, D, None, l)
        P.barrier()
        P.flush()

    x1r = Res("x1_d")
    x2r = Res("x2_d")

    def load_norm_T(ph_bufs, x_src, b, keep=None, src_res=None, skip_load=False):
        xt_ring, junk, st_ring, hb_ring, banks, hT_ring = ph_bufs
        hT = hT_ring.next()
        for s in range(4):
            tok0 = b * 512 + s * 128
            if keep is None:
                xt = xt_ring.next()
                xv = xt.t[:, :]
                xr = xt.r
            else:
                xv = keep.t[:, s, :]
                xr = keep.r
            if not skip_load:
                P.dma("sp", xv, x_src[tok0:tok0 + 128, :], R_=([src_res] if src_res is not None else []), W_=[xr])
            st = st_ring.next()
            ACT(junk.t[:, :], xv, AF.Square, [xr], [junk.r, st.r], accum_out=st.t[:, 0:1])
            RSQRT(st.t[:, 1:2], st.t[:, 0:1], 1.0 / D, EPS, [st.r], [st.r])
            hb = hb_ring.next()
            TS("dve", hb.t[:, :], xv, st.t[:, 1:2], None, ALU.mult, None, [xr, st.r], [hb.r])
            pt = banks.next()
            ptb = pt.t[:, :].bitcast(BF16)
            TR([(ptb[:, c * 128:(c + 1) * 128], hb.t[:, c * 128:(c + 1) * 128], ident_bf) for c in range(8)],
               [hb.r, cbf.r], [pt.r])
            CP("act", hT.t[:, :, s * 128:(s + 1) * 128], ptb.rearrange("p (c t) -> p c t", c=8), [pt.r], [hT.r])
        return hT

    for l in range(nlayers):
        x_cur = x_in.ap() if l == 0 else x2_d.ap()
        x_next = out_d.ap() if l == nlayers - 1 else x2_d.ap()
        winl = win_bf.ap()[l].rearrange("(c p) n -> p c n", p=128)
        ag1r = Res("ag1_in")
        ag2r = Res("ag2_in")
        scr = Res("scratch_l%d" % l)
        uTr = Res("uText")

        with ExitStack() as ph:
            SB = lambda shape, dt=F32: alloc(ph, "sb", shape, dt)
            banks = Ring([alloc(ph, "ps", [128, 512], F32) for _ in range(8)])
            xt_ring = Ring([SB([128, D]) for _ in range(2)])
            junk = SB([128, D], BF16)
            st_ring = Ring([SB([128, 2]) for _ in range(4)])
            hb_ring = Ring([SB([128, D], BF16) for _ in range(2)])
            hT_ring = Ring([SB([128, 8, 512], BF16) for _ in range(1)])
            lnt = (xt_ring, junk, st_ring, hb_ring, banks, hT_ring)
            wq_ring = Ring([SB([128, 8, 512], BF16) for _ in range(2)])
            wf_ring = Ring([SB([128, 8, 768], BF16) for _ in range(1)])
            z = [[SB([128, 1792]) for _ in range(4)] for _ in range(1)]
            gT_blk = Ring([SB([64, 4, 512], BF16) for _ in range(1)])
            uT_blk = Ring([SB([128, 2, 512]) for _ in range(1)])
            sg_ring = Ring([SB([128, 512]) for _ in range(2)])
            qT_blk = Ring([SB([128, 4, 512], BF16) for _ in range(1)])
            kT_blk = Ring([SB([128, 512], BF16) for _ in range(1)])
            QtT_blk = Ring([SB([64, 2, 4, 512], BF16) for _ in range(1)])
            oiT_blk = Ring([SB([64, 4, 512]) for _ in range(1)])
            W2 = lambda shape, dt=F32: Ring([SB(shape, dt) for _ in range(2)])
            W1 = lambda shape, dt=F32: Ring([SB(shape, dt)])
            sq_r, ssq_r, qn_r, t1_r, t2_r = W1([128, 640]), W2([128, 16]), W1([128, 640]), W1([128, 640]), W1([128, 640])
            qr_r, vb_r = W2([128, 640], BF16), W2([128, 128], BF16)
            sgf_r, sgn_r, fg_r, logf_r, kk_r = W1([128, 512]), W1([128, 512]), W1([128, 512]), W1([128, 512]), W1([128, 512])
            eb_r, enb_r = W1([128, 512]), W1([128, 512])
            Qt_r, Kt_r, Vb_r = W2([128, 2, 256], BF16), W2([128, 512], BF16), W2([128, 256], BF16)
            QtT_r, KtT_r = W2([64, 8, 128], BF16), W2([64, 8, 128], BF16)
            AT_r = [W2([128, 4, 128], BF16), W2([128, 4, 128], BF16)]
            TM_GROUPS = [(0, 512), (512, 256), (768, 512), (1280, 512)]
            gqk = SB([128, 2, 10, 64])
            lbt = SB([128, 2, 512])
            oml = SB([128, 2, 512])
            qkb = SB([128, 2, 128])
            P.dma("sp", qkb.t[:, :, :], qkn_d.ap().rearrange("l o c -> o l c").partition_broadcast(128), W_=[qkb.r])
            CP("dve", gqk.t[:, l, 0:8, :], bc(qkb.t[:, l, 0:64], [128, 8, 64], 1), [qkb.r], [gqk.r])
            CP("dve", gqk.t[:, l, 8:10, :], bc(qkb.t[:, l, 64:128], [128, 2, 64], 1), [qkb.r], [gqk.r])
            lbb = SB([128, 1024])
            P.dma("sp", lbb.t[:, :], lbp_d.ap().partition_broadcast(128), W_=[lbb.r])
            lbv = lbb.t[:, :].rearrange("p (d l c) -> p d l c", d=2, l=2)
            dlt = SB([128, 2, 256])
            TT("dve", dlt.t[:, :, :], lbv[:, :, 1, :], lbv[:, :, 0, :], ALU.subtract, [lbb.r], [dlt.r])
            MEMSET("pool", lbt.t[:, 0, :], 0.0, [lbt.r])
            ACT(lbt.t[:, 1, :], dlt.t[:, :, :].rearrange("p d c -> p (d c)"), AF.Sigmoid, [dlt.r], [lbt.r])
            TS("dve", oml.t[:, :, :], lbt.t[:, :, :], -1.0, 1.0, ALU.mult, ALU.add, [lbt.r], [oml.r])
            ropeC = SB([128, NSUB, 64])
            ropeS = SB([128, NSUB, 64])
            P.dma("sp", ropeC.t[:, :, :], ropeC_d.ap(), W_=[ropeC.r])
            P.dma("sp", ropeS.t[:, :, :], ropeS_d.ap(), W_=[ropeS.r])

            for b in range(NBLK):
                hT = load_norm_T(lnt, x_cur, b, src_res=(x2r if l > 0 else None))
                zb = z[0]
                for gi, (c0, ncol) in enumerate(TM_GROUPS):
                    wq = wq_ring.next()
                    P.dma("sp", wq.t[:, :, 0:ncol], winl[:, :, c0:c0 + ncol], R_=[wres], W_=[wq.r])
                    for s in range(4):
                        pz = banks.next()
                        MM([(pz.t[:, 0:ncol], hT.t[:, c, s * 128:(s + 1) * 128], wq.t[:, c, 0:ncol], c == 0, c == 7)
                            for c in range(8)], [hT.r, wq.r], [pz.r])
                        CP(("act", "dve")[s % 2], zb[s].t[:, c0:c0 + ncol], pz.t[:, 0:ncol], [pz.r], [zb[s].r])
                wf = wf_ring.next()
                P.dma("sp", wf.t[:, :, :], winl[:, :, C_GH:INC], R_=[wres], W_=[wf.r])
                gTb = gT_blk.next()
                for h in range(4):
                    pz = banks.next()
                    MM([(pz.t[0:64, :], wf.t[:, c, h * 64:(h + 1) * 64], hT.t[:, c, :], c == 0, c == 7) for c in range(8)],
                       [hT.r, wf.r], [pz.r])
                    ACT(gTb.t[:, h, :], pz.t[0:64, :], AF.Silu, [pz.r], [gTb.r])
                P.dma("sp", gT_d.ap()[:, :, b * 512:(b + 1) * 512], gTb.t[:, :, :], R_=[gTb.r], W_=[scr])
                uTb = uT_blk.next()
                for c2 in range(2):
                    pa = banks.next()
                    pg = banks.next()
                    MM([(pa.t[:, :], wf.t[:, c, 256 + c2 * 128:256 + (c2 + 1) * 128], hT.t[:, c, :], c == 0, c == 7)
                        for c in range(8)], [hT.r, wf.r], [pa.r])
                    MM([(pg.t[:, :], wf.t[:, c, 512 + c2 * 128:512 + (c2 + 1) * 128], hT.t[:, c, :], c == 0, c == 7)
                        for c in range(8)], [hT.r, wf.r], [pg.r])
                    sg = sg_ring.next()
                    ACT(sg.t[:, :], pg.t[:, :], AF.Sigmoid, [pg.r], [sg.r])
                    TT("dve", uTb.t[:, c2, :], pa.t[:, :], sg.t[:, :], ALU.mult, [pa.r, sg.r], [uTb.r])
                P.dma("sp", uText_d.ap()[:, :, HALO + b * 512:HALO + (b + 1) * 512], uTb.t[:, :, :], R_=[uTb.r], W_=[uTr])

                qTb = qT_blk.next()
                kTb = kT_blk.next()
                QtTb = QtT_blk.next()
                oiTb = oiT_blk.next()
                for s in range(4):
                    zs = zb[s]
                    zt = zs.t
                    ts = b * 4 + s
                    tok0 = ts * 128
                    qk = zt[:, 0:640]
                    qk3 = qk.rearrange("p (h d) -> p h d", d=64)
                    sq, ssq, qn, t1, t2, qr, vb = sq_r.next(), ssq_r.next(), qn_r.next(), t1_r.next(), t2_r.next(), qr_r.next(), vb_r.next()
                    TT("dve", sq.t[:, :], qk, qk, ALU.mult, [zs.r], [sq.r])
                    P.op("dve", lambda e, o=ssq.t[:, 0:10], i=sq.t[:, :].rearrange("p (h d) -> p h d", d=64):
                         e.tensor_reduce(o, i, AX.X, ALU.add), [sq.r], [ssq.r])
                    RSQRT(ssq.t[:, 0:10], ssq.t[:, 0:10], 1.0, 64 * EPS, [ssq.r], [ssq.r])
                    qn3 = qn.t[:, :].rearrange("p (h d) -> p h d", d=64)
                    TT("dve", qn3, qk3, bc(ssq.t[:, 0:10], [128, 10, 64], 2), ALU.mult, [zs.r, ssq.r], [qn.r])
                    TT("dve", qn3, qn3, gqk.t[:, l, :, :], ALU.mult, [qn.r, gqk.r], [qn.r])
                    t13 = t1.t[:, :].rearrange("p (h d) -> p h d", d=64)
                    t23 = t2.t[:, :].rearrange("p (h d) -> p h d", d=64)
                    TT("dve", t13, qn3, bc(ropeC.t[:, ts, :], [128, 10, 64], 1), ALU.mult, [qn.r, ropeC.r], [t1.r])
                    for a in range(2):
                        for hf in range(2):
                            o0 = a * 32 + hf * 16
                            i0 = a * 32 + (1 - hf) * 16
                            TT("dve", t23[:, :, o0:o0 + 16], qn3[:, :, i0:i0 + 16],
                               bc(ropeS.t[:, ts, o0:o0 + 16], [128, 10, 16], 1), ALU.mult, [qn.r, ropeS.r], [t2.r])
                    TT("dve", qr.t[:, 0:512].rearrange("p (j g d) -> p g j d", j=4, g=2),
                       t1.t[:, 0:512].rearrange("p (g j d) -> p g j d", g=2, j=4),
                       t2.t[:, 0:512].rearrange("p (g j d) -> p g j d", g=2, j=4), ALU.add, [t1.r, t2.r], [qr.r])
                    TT("dve", qr.t[:, 512:640], t1.t[:, 512:640], t2.t[:, 512:640], ALU.add, [t1.r, t2.r], [qr.r])
                    pt = banks.next()
                    ptb = pt.t[:, :].bitcast(BF16)
                    TR([(ptb[:, j * 128:(j + 1) * 128], qr.t[:, j * 128:(j + 1) * 128], ident_bf) for j in range(4)]
                       + [(ptb[:, 512:640], qr.t[:, 512:640], ident_bf)], [qr.r, cbf.r], [pt.r])
                    CP("act", qTb.t[:, :, s * 128:(s + 1) * 128], ptb[:, 0:512].rearrange("p (j t) -> p j t", j=4), [pt.r], [qTb.r])
                    CP("dve", kTb.t[:, s * 128:(s + 1) * 128], ptb[:, 512:640], [pt.r], [kTb.r])
                    CP("dve", vb.t[:, :], zt[:, C_V:C_V + 128], [zs.r], [vb.r])
                    P.dma("sp", ag1_in[b].ap()[128:256, :].rearrange("a (b c) -> (a b) c", c=128)[s * 128:(s + 1) * 128, :], vb.t[:, :],
                          R_=[vb.r], W_=[ag1r])
                    sgf, sgn, fg, logf, kk = sgf_r.next(), sgn_r.next(), fg_r.next(), logf_r.next(), kk_r.next()
                    eb, enb, Qt, Kt, Vb = eb_r.next(), enb_r.next(), Qt_r.next(), Kt_r.next(), Vb_r.next()
                    ff = zt[:, C_FF:C_FF + 512]
                    ACT(sgf.t[:, :], ff, AF.Sigmoid, [zs.r], [sgf.r])
                    ACT(sgn.t[:, :], ff, AF.Sigmoid, [zs.r], [sgn.r], scale=-1.0)
                    TT("dve", fg.t[:, :], sgf.t[:, :], oml.t[:, l, :], ALU.mult, [sgf.r, oml.r], [fg.r])
                    TT("dve", fg.t[:, :], fg.t[:, :], lbt.t[:, l, :], ALU.add, [fg.r, lbt.r], [fg.r])
                    ACT(logf.t[:, :], fg.t[:, :], AF.Ln, [fg.r], [logf.r])
                    TT("dve", kk.t[:, :], sgn.t[:, :], oml.t[:, l, :], ALU.mult, [sgn.r, oml.r], [kk.r])
                    pc = banks.next()
                    MM([(pc.t[:, 0:256], Mf[0], logf.t[:, 0:256], True, True),
                        (pc.t[:, 256:512], Mf[1], logf.t[:, 256:512], True, True)], [consts.r, logf.r], [pc.r])
                    ACT(eb.t[:, :], pc.t[:, :], AF.Exp, [pc.r], [eb.r])
                    ACT(enb.t[:, :], pc.t[:, :], AF.Exp, [pc.r], [enb.r], scale=-1.0)
                    TT("dve", Qt.t[:, :, :], bc(zt[:, C_QH:C_QH + 256], [128, 2, 256], 1),
                       eb.t[:, :].rearrange("p (d c) -> p d c", d=2), ALU.mult, [zs.r, eb.r], [Qt.r])
                    TT("dve", Kt.t[:, :], kk.t[:, :], enb.t[:, :], ALU.mult, [kk.r, enb.r], [Kt.r])
                    CP("dve", Vb.t[:, :], zt[:, C_IH:C_IH + 256], [zs.r], [Vb.r])
                    P.dma("sp", kt_d.ap()[tok0:tok0 + 128, :], Kt.t[:, :], R_=[Kt.r], W_=[scr])
                    P.dma("sp", vh_d.ap()[tok0:tok0 + 128, :], Vb.t[:, :], R_=[Vb.r], W_=[scr])
                    pd = banks.next()
                    MM([(pd.t[0:64, (dr * 4 + h) * 4:(dr * 4 + h) * 4 + 4], logf.t[:, dr * 256 + h * 64:dr * 256 + (h + 1) * 64], sel_f, True, True)
                        for dr in range(2) for h in range(4)], [logf.r, consts.r], [pd.r])
                    ACT(d_all.t[:, :, :, ts * 4:ts * 4 + 4].rearrange("p d h n -> p (d h) n"),
                        pd.t[0:64, 0:32].rearrange("p (a n) -> p a n", n=4), AF.Exp, [pd.r], [d_all.r])
                    pq = banks.next()
                    pqb = pq.t[:, :].bitcast(BF16)
                    TR([(pqb[0:64, (dr * 4 + h) * 128:(dr * 4 + h + 1) * 128], Qt.t[:, dr, h * 64:(h + 1) * 64], ident_bf)
                        for dr in range(2) for h in range(4)], [Qt.r, cbf.r], [pq.r])
                    QtT, KtT = QtT_r.next(), KtT_r.next()
                    CP("dve", QtT.t[:, :, :], pqb[0:64, :].rearrange("p (a t) -> p a t", a=8), [pq.r], [QtT.r])
                    CP("act", QtTb.t[:, :, :, s * 128:(s + 1) * 128].rearrange("p d h t -> p (d h) t"),
                       pqb[0:64, :].rearrange("p (a t) -> p a t", a=8), [pq.r], [QtTb.r])
                    pk = banks.next()
                    pkb = pk.t[:, :].bitcast(BF16)
                    TR([(pkb[0:64, (dr * 4 + h) * 128:(dr * 4 + h + 1) * 128], Kt.t[:, dr * 256 + h * 64:dr * 256 + (h + 1) * 64], ident_bf)
                        for dr in range(2) for h in range(4)], [Kt.r, cbf.r], [pk.r])
                    CP("dve", KtT.t[:, :, :], pkb[0:64, :].rearrange("p (a t) -> p a t", a=8), [pk.r], [KtT.r])
                    ATs = []
                    for dr in range(2):
                        pa = banks.next()
                        MM([(pa.t[:, h * 128:(h + 1) * 128], KtT.t[:, dr * 4 + h, :], QtT.t[:, dr * 4 + h, :], True, True) for h in range(4)],
                           [KtT.r, QtT.r], [pa.r])
                        AT = AT_r[dr].next()
                        TT("dve", AT.t[:, :, :], pa.t[:, :].rearrange("p (h t) -> p h t", h=4), bc(Mf[dr], [128, 4, 128], 1), ALU.mult,
                           [pa.r, consts.r], [AT.r])
                        ATs.append(AT)
                    po = banks.next()
                    MM([(po.t[0:64, h * 128:(h + 1) * 128], Vb.t[:, h * 64:(h + 1) * 64], ATs[dr].t[:, h, :], dr == 0, dr == 1)
                        for h in range(4) for dr in range(2)], [Vb.r, ATs[0].r, ATs[1].r], [po.r])
                    CP("act", oiTb.t[:, :, s * 128:(s + 1) * 128], po.t[0:64, :].rearrange("p (h t) -> p h t", h=4), [po.r], [oiTb.r])
                P.dma("sp", qT_d.ap()[:, :, b * 512:(b + 1) * 512], qTb.t[:, :, :], R_=[qTb.r], W_=[scr])
                P.dma("sp", ag1_in[b].ap()[0:128, :], kTb.t[:, :], R_=[kTb.r], W_=[ag1r])
                P.dma("sp", qtT_d.ap()[:, :, :, b * 512:(b + 1) * 512], QtTb.t[:, :, :, :], R_=[QtTb.r], W_=[scr])
                P.dma("sp", oiT_d.ap()[:, :, b * 512:(b + 1) * 512], oiTb.t[:, :, :], R_=[oiTb.r], W_=[scr])
            P.barrier()
            P.flush()
        if "stop_s1" in dbg:
            break

        ofr = [Res("of_d%d" % i) for i in range(NSUB)]
        ag1o = Res("ag1_out")
        ag2o = Res("ag2_out")
        mixr = Res("mix_d")

        def hgrn_gen(final, ph, getbank, deep):
            if True:
                SB = lambda shape, dt=F32: alloc(ph, "sb", shape, dt)
                nb = 4 if deep else 2
                Kl_ring = Ring([SB([128, 256], BF16) for _ in range(nb + 2)])
                Vl_ring = Ring([SB([128, 256], BF16) for _ in range(nb + 2)])
                Vx_ring = Ring([SB([128, 4, 4, 64], BF16) for _ in range(nb)])
                Us_ring = Ring([SB([64, 4, 4, 64]) for _ in range(nb)])
                Sp_ring = Ring([SB([64, 4, 4, 64], BF16) for _ in range(nb)])
                Ql_ring = Ring([SB([64, 4, 128], BF16) for _ in range(nb)])
                oi_ring = Ring([SB([64, 4, 128]) for _ in range(nb)])
                o_ring = Ring([SB([64, 512]) for _ in range(nb // 2)])
                sq_ring = Ring([SB([64, 512]) for _ in range(nb // 2)])
                rs_ring = Ring([SB([64, 512]) for _ in range(nb // 2)])
                gl_ring = Ring([SB([64, 4, 128], BF16) for _ in range(2)])
                mx_ring = Ring([SB([64, 4, 128], BF16) for _ in range(2)])
                Salt = [SB([64, 4, 64]) for _ in range(2)]
                dc = SB([64, 8])
                if not final:
                    zt_ = SB([128, 640])
                    MEMSET("pool", zt_.t[:, :], 0.0, [zt_.r])
                    P.dma("sp", ag2_in.ap(), zt_.t[:, :], R_=[zt_.r], W_=[ag2r])
                if final:
                    Gt = SB([64, R, 520])
                    Gh = SB([128, R, 60])
                    Pst = SB([64, 4, 64])
                    hp = SB([128, 2, 15])
                    hn = SB([128, 2, 15])
                    g2 = ag2_out[l].ap().rearrange("(r p) c -> p r c", p=128)
                    P.dma("sp", Gt.t[:, :, :], g2[0:64, :, 0:520], R_=[ag2o], W_=[Gt.r])
                    P.dma("sp", Gh.t[:, :, :], g2[:, :, 520:580], R_=[ag2o], W_=[Gh.r])
                    for dr in range(2):
                        MEMSET("pool", Pst.t[:, :, :], 0.0, [Pst.r])
                        MEMSET("pool", Sin[dr].t[:, :, :], 0.0, [Sin[dr].r])
                        for r in (range(R) if dr == 0 else range(R - 1, -1, -1)):
                            STT("dve", Sin[dr].t[:, :, :], Pst.t[:, :, :], oh.t[0:64, r:r + 1], Sin[dr].t[:, :, :], ALU.mult, ALU.add,
                                [Pst.r, oh.r, Sin[dr].r], [Sin[dr].r])
                            Tr = Gt.t[:, r, dr * 260:dr * 260 + 256].rearrange("p (h v) -> p h v", h=4)
                            Dr = Gt.t[:, r, dr * 260 + 256:dr * 260 + 260]
                            TT("dve", Pst.t[:, :, :], Pst.t[:, :, :], bc(Dr, [64, 4, 64], 2), ALU.mult, [Pst.r, Gt.r], [Pst.r])
                            TT("dve", Pst.t[:, :, :], Pst.t[:, :, :], Tr, ALU.add, [Pst.r, Gt.r], [Pst.r])
                    Gh5 = Gh.t[:, :, :].rearrange("p r (c e k) -> p r c e k", c=2, e=2)
                    for (hb_, e_, o0) in ((hp, 1, 4), (hn, 0, 8)):
                        TS("dve", hb_.t[:, :, :], Gh5[:, 0, :, e_, :], oh.t[:, o0:o0 + 1], None, ALU.mult, None, [Gh.r, oh.r], [hb_.r])
                        for r in range(1, R):
                            STT("dve", hb_.t[:, :, :], Gh5[:, r, :, e_, :], oh.t[:, o0 + r:o0 + r + 1], hb_.t[:, :, :], ALU.mult, ALU.add,
                                [Gh.r, oh.r, hb_.r], [hb_.r])
                    P.dma("sp", uText_d.ap()[:, :, 0:HALO], hp.t[:, :, :], R_=[hp.r], W_=[uTr])
                    P.dma("sp", uText_d.ap()[:, :, HALO + NT:2 * HALO + NT], hn.t[:, :, :], R_=[hn.r], W_=[uTr])

                states = {}
                for dr in range(2):
                    cur, nxt = Sst[dr], Salt[dr]
                    if final:
                        CP("dve", cur.t[:, :, :], Sin[dr].t[:, :, :], [Sin[dr].r], [cur.r])
                    else:
                        MEMSET("pool", cur.t[:, :, :], 0.0, [cur.r])
                    states[dr] = (cur, nxt)
                for i in range(NSUB):
                    for dr in range(2):
                        ts = i if dr == 0 else NSUB - 1 - i
                        first = i < NSUB // 2
                        cur, nxt = states[dr]
                        tok0 = ts * 128
                        Kl, Vl, Vx, Usb = Kl_ring.next(), Vl_ring.next(), Vx_ring.next(), Us_ring.next()
                        P.dma("sp", Kl.t[:, :], kt_d.ap()[tok0:tok0 + 128, dr * 256:(dr + 1) * 256], R_=[scr], W_=[Kl.r])
                        P.dma("sp", Vl.t[:, :], vh_d.ap()[tok0:tok0 + 128, :], R_=[scr], W_=[Vl.r])
                        TT("dve", Vx.t[:, :, :, :],
                           Vl.t[:, :].rearrange("p (h v) -> p h v", h=4).unsqueeze(2).to_broadcast([128, 4, 4, 64]),
                           sel_bf.unsqueeze(1).unsqueeze(3).to_broadcast([128, 4, 4, 64]), ALU.mult, [Vl.r, cbf.r], [Vx.r])
                        for hp2 in range(2):
                            pu = getbank()
                            MM([(pu.t[0:64, hh * 256:(hh + 1) * 256], Kl.t[:, (hp2 * 2 + hh) * 64:(hp2 * 2 + hh + 1) * 64],
                                 Vx.t[:, hp2 * 2 + hh, :, :].rearrange("p n v -> p (n v)"), True, True) for hh in range(2)],
                               [Kl.r, Vx.r], [pu.r])
                            CP("dve", Usb.t[:, hp2 * 2:hp2 * 2 + 2, :, :].rearrange("p h n v -> p h (n v)"),
                               pu.t[0:64, :].rearrange("p (h x) -> p h x", h=2), [pu.r], [Usb.r])
                        Sp = Sp_ring.next() if final else None
                        for n in (range(4) if dr == 0 else range(3, -1, -1)):
                            ch = ts * 4 + n
                            if final:
                                CP("dve", Sp.t[:, n, :, :], cur.t[:, :, :], [cur.r], [Sp.r])
                            TT("dve", nxt.t[:, :, :], cur.t[:, :, :], Usb.t[:, :, n, :], ALU.add, [cur.r, Usb.r], [nxt.r])
                            TT("dve", nxt.t[:, :, :], nxt.t[:, :, :], bc(d_all.t[:, dr, :, ch], [64, 4, 64], 2), ALU.mult,
                               [nxt.r, d_all.r], [nxt.r])
                            cur, nxt = nxt, cur
                        states[dr] = (cur, nxt)
                        if not final:
                            yield
                            continue
                        Ql = Ql_ring.next()
                        P.dma("sp", Ql.t[:, :, :], qtT_d.ap()[:, dr, :, tok0:tok0 + 128], R_=[scr], W_=[Ql.r])
                        po = getbank()
                        MM([(po.t[0:64, h * 128 + n * 32:h * 128 + (n + 1) * 32], Sp.t[:, n, h, :], Ql.t[:, h, n * 32:(n + 1) * 32], True, True)
                            for n in range(4) for h in range(4)], [Sp.r, Ql.r], [po.r])
                        po3 = po.t[0:64, :].rearrange("p (h t) -> p h t", h=4)
                        oi = oi_ring.next()
                        if first:
                            P.dma("sp", oi.t[:, :, :], oiT_d.ap()[:, :, tok0:tok0 + 128], R_=[scr], W_=[oi.r])
                            TT("dve", oi.t[:, :, :], po3, oi.t[:, :, :], ALU.add, [po.r, oi.r], [oi.r])
                            P.dma("sp", of_d.ap()[:, :, tok0:tok0 + 128], oi.t[:, :, :], R_=[oi.r], W_=[ofr[ts]])
                        else:
                            P.dma("sp", oi.t[:, :, :], of_d.ap()[:, :, tok0:tok0 + 128], R_=[ofr[ts]], W_=[oi.r])
                            o, sq, rs, gl, mx = o_ring.next(), sq_ring.next(), rs_ring.next(), gl_ring.next(), mx_ring.next()
                            o3 = o.t[:, :].rearrange("p (h t) -> p h t", h=4)
                            TT("dve", o3, po3, oi.t[:, :, :], ALU.add, [po.r, oi.r], [o.r])
                            ACT(sq.t[:, :], o.t[:, :], AF.Square, [o.r], [sq.r])
                            pn = getbank()
                            MM([(pn.t[0:64, h * 128:(h + 1) * 128], ones64, sq.t[:, h * 128:(h + 1) * 128], True, True) for h in range(4)],
                               [consts.r, sq.r], [pn.r])
                            RSQRT(rs.t[:, :], pn.t[0:64, :], 1.0 / 64, EPS, [pn.r], [rs.r])
                            TT("dve", o.t[:, :], o.t[:, :], rs.t[:, :], ALU.mult, [o.r, rs.r], [o.r])
                            TT("dve", o3, o3, bc(pp.t[0:64, l, PP_GN:PP_GN + 4], [64, 4, 128], 2), ALU.mult, [o.r, pp.r], [o.r])
                            P.dma("sp", gl.t[:, :, :], gT_d.ap()[:, :, tok0:tok0 + 128], R_=[scr], W_=[gl.r])
                            TT("dve", mx.t[:, :, :], o3, gl.t[:, :, :], ALU.mult, [o.r, gl.r], [mx.r])
                            P.dma("sp", mixhg_d.ap()[:, :, tok0:tok0 + 128], mx.t[:, :, :], R_=[mx.r], W_=[mixr])
                        yield
                if not final:
                    for dr in range(2):
                        cur = states[dr][0]
                        P.dma("sp", ag2_in.ap()[0:64, dr * 260:dr * 260 + 256], cur.t[:, :, :].rearrange("p h v -> p (h v)"),
                              R_=[cur.r], W_=[ag2r])
                        P.op("dve", lambda e, o_=dc.t[:, dr * 4:dr * 4 + 4], i_=d_all.t[:, dr, :, :]: e.tensor_reduce(o_, i_, AX.X, ALU.mult),
                             [d_all.r], [dc.r])
                        P.dma("sp", ag2_in.ap()[0:64, dr * 260 + 256:dr * 260 + 260], dc.t[:, dr * 4:dr * 4 + 4], R_=[dc.r], W_=[ag2r])
                if not final:
                    a2 = ag2_in.ap()[:, 520:580].rearrange("p (c e k) -> p c e k", c=2, e=2)
                    P.dma("sp", a2[:, :, 0, :], uText_d.ap()[:, :, HALO:2 * HALO], R_=[uTr], W_=[ag2r])
                    P.dma("sp", a2[:, :, 1, :], uText_d.ap()[:, :, NT:NT + HALO], R_=[uTr], W_=[ag2r])

        with ExitStack() as ph:
            banks = Ring([alloc(ph, "ps", [128, 512], F32) for _ in range(8)])
            for _ in hgrn_gen(False, ph, banks.next, True):
                pass
            P.barrier()
            P.flush()
        for b in range(NBLK):
            P.allgather(ag1_in[b], ag1_out[l][b], R_=[ag1r], W_=[ag1o])
        P.allgather(ag2_in, ag2_out[l], R_=[ag2r], W_=[ag2o])

        def conv_gen(ph, getbank):
            SB = lambda shape, dt=F32: alloc(ph, "sb", shape, dt)
            ub_ring = Ring([SB([128, 2, 512 + 2 * HALO]) for _ in range(1)])
            acc_ring = Ring([SB([128, 2, 512]) for _ in range(1)])
            yc_ring = Ring([SB([128, 2, 512]) for _ in range(1)])
            rs_ring = Ring([SB([128, 512]) for _ in range(1)])
            mc_ring = Ring([SB([128, 2, 512], BF16) for _ in range(1)])
            cw = pp.t[:, l, PP_CW:PP_CW + 62].rearrange("p (c k) -> p c k", c=2)
            for b in range(NBLK):
                ub, acc, yc, rs, mc = ub_ring.next(), acc_ring.next(), yc_ring.next(), rs_ring.next(), mc_ring.next()
                sq = acc
                P.dma("sp", ub.t[:, :, :], uText_d.ap()[:, :, b * 512:b * 512 + 512 + 2 * HALO], R_=[uTr], W_=[ub.r])
                for c in range(2):
                    TS("dve", acc.t[:, c, :], ub.t[:, c, 0:512], cw[:, c, 0:1], pp.t[:, l, PP_CB + c:PP_CB + c + 1], ALU.mult, ALU.add,
                       [ub.r, pp.r], [acc.r])
                    for k in range(1, KW):
                        STT("dve", acc.t[:, c, :], ub.t[:, c, k:k + 512], cw[:, c, k:k + 1], acc.t[:, c, :], ALU.mult, ALU.add,
                            [ub.r, pp.r, acc.r], [acc.r])
                    if c == 0:
                        yield
                pm = getbank()
                MM([(pm.t[:, :], onesq, acc.t[:, c, :], c == 0, c == 1) for c in range(2)], [consts.r, acc.r], [pm.r])
                TT("dve", yc.t[:, :, :], acc.t[:, :, :], bc(pm.t[:, :], [128, 2, 512], 1), ALU.subtract, [acc.r, pm.r], [yc.r])
                ACT(sq.t[:, :, :], yc.t[:, :, :], AF.Square, [yc.r], [sq.r])
                pv = getbank()
                MM([(pv.t[:, :], onesq, sq.t[:, c, :], c == 0, c == 1) for c in range(2)], [consts.r, sq.r], [pv.r])
                RSQRT(rs.t[:, :], pv.t[:, :], 1.0, EPS, [pv.r], [rs.r])
                TT("dve", yc.t[:, :, :], yc.t[:, :, :], bc(rs.t[:, :], [128, 2, 512], 1), ALU.mult, [yc.r, rs.r], [yc.r])
                for c in range(2):
                    ACT(mc.t[:, c, :], yc.t[:, c, :], AF.Silu, [yc.r, pp.r], [mc.r],
                        scale=pp.t[:, l, PP_LNG + c:PP_LNG + c + 1], bias=pp.t[:, l, PP_LNB + c:PP_LNB + c + 1])
                P.dma("sp", mixcv_d.ap()[:, :, b * 512:(b + 1) * 512], mc.t[:, :, :], R_=[mc.r], W_=[mixr])
                yield

        with ExitStack() as ph:
            SB = lambda shape, dt=F32: alloc(ph, "sb", shape, dt)
            st_ring = Ring([alloc(ph, "ps", [128, 1024], F32) for _ in range(3)])
            obank = [alloc(ph, "ps", [128, 512], F32) for _ in range(2)]
            tbank = st_ring

            class View:
                def __init__(self, t, r):
                    self.t = t
                    self.r = r

            pending = set()

            def take_slot():
                for _ in range(len(st_ring.bufs)):
                    sl_ = st_ring.next()
                    if id(sl_) not in pending:
                        return sl_
                raise RuntimeError("no free score slot")

            def halfbank():
                sl_ = take_slot()
                return View(sl_.t[:, 0:512], sl_.r)

            kT_all = SB([128, L], BF16)
            v_all = SB([128, T, 2, 128], BF16)
            QT_ring = Ring([SB([128, 4, 512], BF16) for _ in range(2)])
            pT_ring = Ring([SB([128, 1024], BF16) for _ in range(3)])
            osb_ring = Ring([SB([128, 512]) for _ in range(2)])
            rec_ring = Ring([SB([64, 512]) for _ in range(1)])
            mixA_ring = Ring([SB([64, 8, 512], BF16) for _ in range(1)])
            MEMSET("pool", v_all.t[:, :, :, 64:128], 1.0, [v_all.r])
            for r in range(R):
                for b in range(NBLK):
                    go = ag1_out[l][b].ap()
                    P.dma("sp", kT_all.t[:, r * NT + b * 512:r * NT + (b + 1) * 512], go[r * 256:r * 256 + 128, :], R_=[ag1o], W_=[kT_all.r])
                    vsrc = go[r * 256 + 128:r * 256 + 256, :].rearrange("a (b c) -> (a b) c", c=128)
                    vsrc = vsrc.rearrange("(t p) (g d) -> p t g d", p=128, g=2)
                    t0 = r * NSUB + b * 4
                    for g in range(2):
                        P.dma("sp", v_all.t[:, t0:t0 + 4, g, 0:64], vsrc[:, :, g, :], R_=[ag1o], W_=[v_all.r])
            hgen = hgrn_gen(True, ph, halfbank, False)
            cgen = conv_gen(ph, halfbank)
            side = {"h": True, "c": True}

            def step(which):
                g_ = hgen if which == "h" else cgen
                if side[which]:
                    try:
                        next(g_)
                    except StopIteration:
                        side[which] = False

            nsteps = NBLK * 4 * T
            h_every = max(1, nsteps // (2 * NSUB + 2))
            c_every = max(1, nsteps // (2 * NBLK + 1))
            step("h")
            k = 0
            for b in range(NBLK):
                QT = QT_ring.next()
                P.dma("sp", QT.t[:, :, :], qT_d.ap()[:, :, b * 512:(b + 1) * 512], R_=[scr], W_=[QT.r])
                mixA = mixA_ring.next()
                for j in range(4):
                    def S_mm(t, j=j, QT=QT):
                        st = take_slot()
                        pending.add(id(st))
                        MM([(st.t[:, 0:512], kT_all.t[0:64, t * 128:(t + 1) * 128], QT.t[0:64, j, :], True, True),
                            (st.t[:, 512:1024], kT_all.t[64:128, t * 128:(t + 1) * 128], QT.t[64:128, j, :], True, True)],
                           [kT_all.r, QT.r], [st.r])
                        return st
                    pend = [S_mm(0)]
                    if T > 1:
                        pend.append(S_mm(1))
                    for t in range(T):
                        st = pend.pop(0)
                        pT = pT_ring.next()
                        ACT(pT.t[:, :], st.t[:, :], AF.Exp, [st.r], [pT.r], scale=8.0)
                        pending.discard(id(st))
                        if t + 2 < T:
                            pend.append(S_mm(t + 2))
                        MM([(obank[0].t[:, :], v_all.t[:, t, 0, :], pT.t[:, 0:512], t == 0, t == T - 1),
                            (obank[1].t[:, :], v_all.t[:, t, 1, :], pT.t[:, 512:1024], t == 0, t == T - 1)],
                           [v_all.r, pT.r], [obank[0].r, obank[1].r])
                        k += 1
                        if k % h_every == 0:
                            step("h")
                        if k % c_every == 0:
                            step("c")
                    for g in range(2):
                        osb, rec = osb_ring.next(), rec_ring.next()
                        CP("dve", osb.t[:, :], obank[g].t[:, :], [obank[g].r], [osb.r])
                        pr = take_slot()
                        MM([(pr.t[0:64, 0:512], shiftM, osb.t[:, :], True, True)], [consts.r, osb.r], [pr.r])
                        P.op("dve", lambda e, o_=rec.t[:, :], i_=pr.t[0:64, 0:512]: e.reciprocal(o_, i_), [pr.r], [rec.r])
                        TT("dve", mixA.t[:, g * 4 + j, :], osb.t[0:64, :], rec.t[:, :], ALU.mult, [osb.r, rec.r], [mixA.r])
                P.dma("sp", mixatt_d.ap()[:, :, b * 512:(b + 1) * 512], mixA.t[:, :, :], R_=[mixA.r], W_=[mixr])
            while side["h"]:
                step("h")
            while side["c"]:
                step("c")
            P.barrier()
            P.flush()
        if "stop_s6" in dbg:
            break

        with ExitStack() as ph:
            SB = lambda shape, dt=F32: alloc(ph, "sb", shape, dt)
            banks = Ring([alloc(ph, "ps", [128, 512], F32) for _ in range(4)])
            accb = [alloc(ph, "ps", [128, 512], F32) for _ in range(4)]
            junk = SB([128, D], BF16)
            st_ring = Ring([SB([128, 2]) for _ in range(4)])
            hb_ring = Ring([SB([128, D], BF16) for _ in range(2)])
            hT_ring = Ring([SB([128, 8, 512], BF16) for _ in range(1)])
            lnt = (None, junk, st_ring, hb_ring, banks, hT_ring)
            xb_ring = Ring([SB([128, 4, D]) for _ in range(2)])
            aT = SB([128, 32, 512], BF16)
            w1q_ring = Ring([SB([128, 8, 512], BF16) for _ in range(2)])
            w2q_ring = Ring([SB([128, 4, 512], BF16) for _ in range(3)])
            rl_ring = Ring([SB([128, 512]) for _ in range(2)])
            w1l = w1_bf.ap()[l].rearrange("(c p) n -> p c n", p=128)
            w2l = w2_bf.ap()[l].rearrange("(j p) n -> p j n", p=128)
            wo_att = SB([64, 8, D], BF16)
            wo_hg = SB([64, 4, D], BF16)
            wo_cv = SB([128, 2, D], BF16)
            ma_ring = Ring([SB([64, 8, 512], BF16) for _ in range(1)])
            mh_ring = Ring([SB([64, 4, 512], BF16) for _ in range(1)])
            mcv_ring = Ring([SB([128, 2, 512], BF16) for _ in range(1)])
            wol = wout_bf.ap()[l]
            P.dma("sp", wo_att.t[:, :, :], wol[0:512, :].rearrange("(h d) n -> d h n", d=64), R_=[wres], W_=[wo_att.r])
            P.dma("sp", wo_hg.t[:, :, :], wol[512:768, :].rearrange("(h d) n -> d h n", d=64), R_=[wres], W_=[wo_hg.r])
            P.dma("sp", wo_cv.t[:, :, :], wol[768:1024, :].rearrange("(c p) n -> p c n", p=128), R_=[wres], W_=[wo_cv.r])
            for b in range(NBLK):
                xb = xb_ring.next()
                ma, mh, mcv = ma_ring.next(), mh_ring.next(), mcv_ring.next()
                P.dma("sp", ma.t[:, :, :], mixatt_d.ap()[:, :, b * 512:(b + 1) * 512], R_=[mixr], W_=[ma.r])
                P.dma("sp", mh.t[:, :, :], mixhg_d.ap()[:, :, b * 512:(b + 1) * 512], R_=[mixr], W_=[mh.r])
                P.dma("sp", mcv.t[:, :, :], mixcv_d.ap()[:, :, b * 512:(b + 1) * 512], R_=[mixr], W_=[mcv.r])
                for s in range(4):
                    P.dma("sp", xb.t[:, s, :], x_cur[b * 512 + s * 128:b * 512 + (s + 1) * 128, :], W_=[xb.r])
                for s in range(4):
                    for n2 in range(2):
                        pw = banks.next()
                        sl = slice(s * 128, (s + 1) * 128)
                        nl = slice(n2 * 512, (n2 + 1) * 512)
                        grp = [(pw.t[:, :], ma.t[:, h, sl], wo_att.t[:, h, nl], h == 0, False) for h in range(8)]
                        grp += [(pw.t[:, :], mh.t[:, h, sl], wo_hg.t[:, h, nl], False, False) for h in range(4)]
                        grp += [(pw.t[:, :], mcv.t[:, c, sl], wo_cv.t[:, c, nl], False, c == 1) for c in range(2)]
                        MM(grp, [ma.r, mh.r, mcv.r, wo_att.r, wo_hg.r, wo_cv.r], [pw.r])
                        TT("dve", xb.t[:, s, nl], pw.t[:, :], xb.t[:, s, nl], ALU.add, [pw.r, xb.r], [xb.r])
                hT = load_norm_T(lnt, None, b, keep=xb, skip_load=True)
                for fg in range(8):
                    w1q = w1q_ring.next()
                    P.dma("sp", w1q.t[:, :, :], w1l[:, :, fg * 512:(fg + 1) * 512], R_=[wres], W_=[w1q.r])
                    for jj in range(4):
                        pa = banks.next()
                        MM([(pa.t[:, :], w1q.t[:, c, jj * 128:(jj + 1) * 128], hT.t[:, c, :], c == 0, c == 7) for c in range(8)],
                           [w1q.r, hT.r], [pa.r])
                        rl = rl_ring.next()
                        ACT(rl.t[:, :], pa.t[:, :], AF.Relu, [pa.r], [rl.r])
                        TT("dve", aT.t[:, fg * 4 + jj, :], rl.t[:, :], rl.t[:, :], ALU.mult, [rl.r], [aT.r])
                for n2 in range(2):
                    nl = slice(n2 * 512, (n2 + 1) * 512)
                    for pc in range(8):
                        w2q = w2q_ring.next()
                        P.dma("sp", w2q.t[:, :, :], w2l[:, pc * 4:(pc + 1) * 4, nl], R_=[wres], W_=[w2q.r])
                        for s in range(4):
                            MM([(accb[s].t[:, :], aT.t[:, pc * 4 + jj, s * 128:(s + 1) * 128], w2q.t[:, jj, :],
                                 pc == 0 and jj == 0, pc == 7 and jj == 3) for jj in range(4)], [aT.r, w2q.r], [accb[s].r])
                    for s in range(4):
                        TT("dve", xb.t[:, s, nl], accb[s].t[:, :], xb.t[:, s, nl], ALU.add, [accb[s].r, xb.r], [xb.r])
                for s in range(4):
                    P.dma("sp", x_next[b * 512 + s * 128:b * 512 + (s + 1) * 128, :], xb.t[:, s, :], R_=[xb.r], W_=[x2r])
            P.barrier()
            P.flush()

    P.barrier()
    P.flush()
    es.close()
    return nc


def _consts():
    c = np.zeros((128, K_END), np.float32)
    j = np.arange(128)[:, None]
    i = np.arange(128)[None, :]
    c[:, K_ID:K_ID + 128] = (j == i)
    same = (j // CH) == (i // CH)
    c[:, K_MFW:K_MFW + 128] = same & (j <= i)
    c[:, K_MBW:K_MBW + 128] = same & (j >= i)
    c[:, K_SEL:K_SEL + 4] = (j // CH) == np.arange(4)[None, :]
    c[64, K_SHIFT:K_SHIFT + 64] = 1.0
    c[:, K_ONES:K_ONES + 64] = 1.0
    c[:, K_ONESQ:K_ONESQ + 128] = 1.0 / 256.0
    return c


def _rope_tables(pos):
    inv = (10000.0 ** (-np.arange(0, 32, 2, dtype=np.float32) / 32.0)).astype(np.float32)
    row = (pos // 64).astype(np.float32)[:, None] * inv[None, :]
    col = (pos % 64).astype(np.float32)[:, None] * inv[None, :]
    cr, sr, cc, sc = np.cos(row), np.sin(row), np.cos(col), np.sin(col)
    C = np.concatenate([cr, cr, cc, cc], 1).astype(np.float32)
    S = np.concatenate([-sr, sr, -sc, sc], 1).astype(np.float32)
    return C, S


def prepare(inputs, NT):
    f = lambda a: np.ascontiguousarray(np.asarray(a, dtype=np.float32))
    x = f(inputs["x"])
    B = x.shape[0]
    pp = np.zeros((2, 128, PP_END), np.float32)
    for l in range(2):
        pp[l, :, PP_NM:PP_NM + 8] = f(inputs["norm_mix"])[l].reshape(8, 128).T
        pp[l, :, PP_NMLP:PP_NMLP + 8] = f(inputs["norm_mlp"])[l].reshape(8, 128).T
        pp[l, 0:64, PP_GN:PP_GN + 4] = f(inputs["hgrn_norm"])[l].reshape(4, 64).T
        pp[l, :, PP_CW:PP_CW + 62] = f(inputs["conv_w"])[l].T.reshape(2, 128, 31).transpose(1, 0, 2).reshape(128, 62)
        pp[l, :, PP_CB:PP_CB + 2] = f(inputs["conv_b"])[l].reshape(2, 128).T
        pp[l, :, PP_LNG:PP_LNG + 2] = f(inputs["conv_ln_g"])[l].reshape(2, 128).T
        pp[l, :, PP_LNB:PP_LNB + 2] = f(inputs["conv_ln_b"])[l].reshape(2, 128).T
    qkn = np.concatenate([f(inputs["q_norm"]), f(inputs["k_norm"])], 1).reshape(2, 1, 128)
    lbp = np.concatenate([f(inputs["hgrn_lb_fwd"]).reshape(-1), f(inputs["hgrn_lb_bwd"]).reshape(-1)]).reshape(1, 1024)
    consts = _consts()
    shared = {"w_in": f(inputs["w_in"]), "w_out": f(inputs["w_out"]), "w_mlp_in": f(inputs["w_mlp_in"]),
              "w_mlp_out": f(inputs["w_mlp_out"]), "pp": pp, "qkn": np.ascontiguousarray(qkn), "lbp": lbp, "consts": consts}
    maps = []
    for c in range(B * R):
        b, r = c // R, c % R
        pos = np.arange(r * NT, (r + 1) * NT)
        C, S = _rope_tables(pos)
        oh = np.zeros((1, 12), np.float32)
        oh[0, r] = 1.0
        if r > 0:
            oh[0, 4 + r - 1] = 1.0
        if r < R - 1:
            oh[0, 8 + r + 1] = 1.0
        m = dict(shared)
        C = np.ascontiguousarray(C.reshape(NT // 128, 128, 64).transpose(1, 0, 2))
        S = np.ascontiguousarray(S.reshape(NT // 128, 128, 64).transpose(1, 0, 2))
        m.update({"x": np.ascontiguousarray(x[b, r * NT:(r + 1) * NT]), "ropeC": C, "ropeS": S, "oh": oh})
        maps.append(m)
    return maps


_CACHE = {}


def kernel(**inputs):
    x = np.asarray(inputs["x"])
    B, Lfull, _ = x.shape
    NT = Lfull // R
    if NT not in _CACHE:
        _CACHE[NT] = build_program(NT)
    nc = _CACHE[NT]
    maps = prepare(inputs, NT)
    res = run_bass_kernel_spmd(nc, maps, core_ids=list(range(B * R)))
    out = np.empty((B, Lfull, D), np.float32)
    for c in range(B * R):
        out[c // R, (c % R) * NT:(c % R + 1) * NT] = np.asarray(res.results[c]["out"], dtype=np.float32)
    return out
```

```python
from contextlib import ExitStack
import numpy as np
import concourse.bass as bass
import concourse.mybir as mybir
from concourse.bass_utils import run_bass_kernel_spmd

F32 = mybir.dt.float32
BF16 = mybir.dt.bfloat16
AF = mybir.ActivationFunctionType
ALU = mybir.AluOpType
AX = mybir.AxisListType

D = 1024
DFF = 4096
INC = 2560
EPS = 1e-6
R = 4
NCORES = 8
CH = 32
HALO = 15
KW = 31


class Res:
    __slots__ = ("w", "r", "name", "excl")

    def __init__(self, name="", excl=False):
        self.w = None
        self.r = []
        self.name = name
        self.excl = excl


class Prog:
    ENG = ("pe", "act", "dve", "pool", "sp")

    def __init__(self, nc, es):
        self.nc = nc
        self.es = es
        self.q = {e: [] for e in self.ENG}
        self.sem = {e: es.enter_context(nc.semaphore("sem_" + e)) for e in ("pe", "act", "dve", "pool")}
        self.cnt = {e: 0 for e in self.sem}
        self.seen = {e: {} for e in self.ENG}
        self.dsem = {}
        self.dcnt = {}
        self.drr = {}
        for qn, n in (("sp", 12), ("pool", 6), ("act", 4)):
            self.dsem[qn] = [es.enter_context(nc.semaphore("dma_%s_%d" % (qn, i))) for i in range(n)]
            self.dcnt[qn] = [0] * n
            self.drr[qn] = 0
        self.cctoks = []

    def _waits(self, eng, R_, W_):
        need = {}

        def add(tok):
            if tok is None:
                return
            s, v = tok
            k = id(s)
            if k not in need or need[k][1] < v:
                need[k] = (s, v)

        for r in R_:
            add(r.w)
            if r.excl:
                for t in r.r:
                    add(t)
        for w in W_:
            add(w.w)
            for t in w.r:
                add(t)
        out = []
        own = self.sem.get(eng)
        for k, (s, v) in need.items():
            if eng == "pe" and s is own:
                continue
            if self.seen[eng].get(k, 0) >= v:
                continue
            self.seen[eng][k] = v
            out.append((s, v))
        return out

    def _commit(self, tok, R_, W_):
        for r in R_:
            r.r.append(tok)
        for w in W_:
            w.w = tok
            w.r = []

    def op(self, eng, fn, R_=(), W_=()):
        waits = self._waits(eng, R_, W_)
        self.cnt[eng] += 1
        tok = (self.sem[eng], self.cnt[eng])
        sem = self.sem[eng]

        def c(e):
            for s, v in waits:
                e.wait_ge(s, v)
            fn(e).then_inc(sem, 1)

        self.q[eng].append(c)
        self._commit(tok, R_, W_)
        return tok

    def dma(self, qn, out, in_, R_=(), W_=(), **kw):
        i = self.drr[qn]
        self.drr[qn] = (i + 1) % len(self.dsem[qn])
        s = self.dsem[qn][i]
        prev = self.dcnt[qn][i]
        waits = self._waits(qn, R_, W_)
        k = id(s)
        if prev > 0 and self.seen[qn].get(k, 0) < prev:
            self.seen[qn][k] = prev
            waits.append((s, prev))
        self.dcnt[qn][i] = prev + 16
        tok = (s, prev + 16)

        def c(e):
            for s_, v in waits:
                e.wait_ge(s_, v)
            e.dma_start(out=out, in_=in_, **kw).then_inc(s, 16)

        self.q[qn].append(c)
        self._commit(tok, R_, W_)
        return tok

    def allgather(self, in_t, out_t, R_=(), W_=()):
        waits = self._waits("pool", R_, W_)
        s = self.es.enter_context(self.nc.semaphore("cc_sem%d" % len(self.cctoks)))
        tok = (s, 1)
        self.cctoks.append(tok)
        groups = [[0, 1, 2, 3], [4, 5, 6, 7]]

        def c(e):
            for s_, v in waits:
                e.wait_ge(s_, v)
            e.collective_compute("AllGather", ALU.bypass, replica_groups=groups,
                                 ins=[in_t.ap().opt()], outs=[out_t.ap().opt()]).then_inc(s)

        self.q["pool"].append(c)
        self._commit(tok, R_, W_)
        return tok

    def barrier(self):
        toks = [(self.sem[e], self.cnt[e]) for e in self.sem if self.cnt[e] > 0]
        for qn in self.dsem:
            for s, v in zip(self.dsem[qn], self.dcnt[qn]):
                if v > 0:
                    toks.append((s, v))
        toks.extend(self.cctoks)
        for eng in self.ENG:
            wl = []
            for s, v in toks:
                if self.seen[eng].get(id(s), 0) >= v:
                    continue
                self.seen[eng][id(s)] = v
                wl.append((s, v))

            def c(e, wl=wl):
                for s_, v in wl:
                    e.wait_ge(s_, v)

            self.q[eng].append(c)

    def flush(self):
        nc = self.nc
        q = self.q
        with nc.Block() as block:
            @block.tensor
            def _(e):
                for c in q["pe"]:
                    c(e)

            @block.scalar
            def _(e):
                for c in q["act"]:
                    c(e)

            @block.vector
            def _(e):
                for c in q["dve"]:
                    c(e)

            @block.gpsimd
            def _(e):
                for c in q["pool"]:
                    c(e)

            @block.sync
            def _(e):
                for c in q["sp"]:
                    c(e)
        self.q = {e: [] for e in self.ENG}


class Buf:
    __slots__ = ("t", "r")

    def __init__(self, t):
        self.t = t
        self.r = Res()


class Ring:
    def __init__(self, bufs):
        self.bufs = bufs
        self.i = 0

    def next(self):
        b = self.bufs[self.i]
        self.i = (self.i + 1) % len(self.bufs)
        return b


C_Q, C_K, C_V, C_QH, C_IH, C_FF, C_GH, C_AC, C_GC = 0, 512, 640, 768, 1024, 1280, 1792, 2048, 2304
K_ID, K_MFW, K_MBW, K_SEL, K_SHIFT, K_ONES, K_ONESQ, K_END = 0, 128, 256, 384, 388, 452, 516, 644
PP_NM, PP_NMLP, PP_GN, PP_CW, PP_CB, PP_LNG, PP_LNB, PP_END = 0, 8, 16, 20, 82, 84, 86, 88


def build_program(NT, nlayers=2, dbg=()):
    nc = bass.Bass("TRN2", target_bir_lowering=False)
    L = R * NT
    NBLK = NT // 512
    NSUB = NT // 128
    T = L // 128
    NCH = NT // CH

    def din(name, shape, dt=F32):
        return nc.dram_tensor(name, list(shape), dt, kind="ExternalInput")

    x_in = din("x", [NT, D])
    w_in_d = din("w_in", [2, D, INC])
    w_out_d = din("w_out", [2, D, D])
    w1_d = din("w_mlp_in", [2, D, DFF])
    w2_d = din("w_mlp_out", [2, DFF, D])
    pp_d = din("pp", [2, 128, PP_END])
    qkn_d = din("qkn", [2, 1, 128])
    lbp_d = din("lbp", [1, 1024])
    consts_d = din("consts", [128, K_END])
    ropeC_d = din("ropeC", [128, NT // 128, 64])
    ropeS_d = din("ropeS", [128, NT // 128, 64])
    oh_d = din("oh", [1, 12])
    out_d = nc.dram_tensor("out", [NT, D], F32, kind="ExternalOutput")

    def dscr(name, shape, dt=F32):
        if name in dbg:
            return nc.dram_tensor(name, list(shape), dt, kind="ExternalOutput")
        return nc.dram_tensor(name, list(shape), dt)

    win_bf = dscr("win_bf", [2, D, INC], BF16)
    wout_bf = dscr("wout_bf", [2, D, D], BF16)
    w1_bf = dscr("w1_bf", [2, D, DFF], BF16)
    w2_bf = dscr("w2_bf", [2, DFF, D], BF16)
    qT_d = dscr("qT_d", [128, 4, NT], BF16)
    ag1_in = [dscr("ag1_in%d" % b, [256, 512], BF16) for b in range(NBLK)]
    ag1_out = [[dscr("ag1_out%d_%d" % (l, b), [R * 256, 512], BF16) for b in range(NBLK)] for l in range(2)]
    ag2_in = dscr("ag2_in", [128, 640], F32)
    ag2_out = [dscr("ag2_out%d" % l, [R * 128, 640], F32) for l in range(2)]
    qtT_d = dscr("qtT_d", [64, 2, 4, NT], BF16)
    kt_d = dscr("kt_d", [NT, 512], BF16)
    vh_d = dscr("vh_d", [NT, 256], BF16)
    oiT_d = dscr("oiT_d", [64, 4, NT], F32)
    of_d = dscr("of_d", [64, 4, NT], F32)
    gT_d = dscr("gT_d", [64, 4, NT], BF16)
    uText_d = dscr("uText_d", [128, 2, NT + 2 * HALO], F32)
    mixhg_d = dscr("mixhg_d", [64, 4, NT], BF16)
    mixcv_d = dscr("mixcv_d", [128, 2, NT], BF16)
    mixatt_d = dscr("mixatt_d", [64, 8, NT], BF16)
    x1_d = dscr("x1_d", [NT, D], F32)
    x2_d = dscr("x2_d", [NT, D], F32)

    es = ExitStack()
    P = Prog(nc, es)
    cnt = [0]

    def alloc(stack, kind, shape, dt):
        cnt[0] += 1
        f = nc.sbuf_tensor if kind == "sb" else nc.psum_tensor
        bf_ = Buf(stack.enter_context(f("%s%d" % (kind, cnt[0]), list(shape), dt)))
        bf_.r.excl = (kind == "ps")
        return bf_

    G = es
    consts = alloc(G, "sb", [128, K_END], F32)
    cbf = alloc(G, "sb", [128, 388], BF16)
    pp = alloc(G, "sb", [128, 2, PP_END], F32)
    oh = alloc(G, "sb", [128, 12], F32)
    d_all = alloc(G, "sb", [64, 2, 4, NCH], F32)
    Sst = [alloc(G, "sb", [64, 4, 64], F32) for _ in range(2)]
    Sin = [alloc(G, "sb", [64, 4, 64], F32) for _ in range(2)]

    ident_bf = cbf.t[:, 0:128]
    ident_f = consts.t[:, K_ID:K_ID + 128]
    Mf = [consts.t[:, K_MFW:K_MFW + 128], consts.t[:, K_MBW:K_MBW + 128]]
    sel_f = consts.t[:, K_SEL:K_SEL + 4]
    sel_bf = cbf.t[:, 384:388]
    shiftM = consts.t[:, K_SHIFT:K_SHIFT + 64]
    ones64 = consts.t[0:64, K_ONES:K_ONES + 64]
    onesq = consts.t[:, K_ONESQ:K_ONESQ + 128]

    def TT(eng, out, a, b, op, R_, W_):
        return P.op(eng, lambda e: e.tensor_tensor(out, a, b, op), R_, W_)

    def TS(eng, out, a, s1, s2, op0, op1, R_, W_):
        if s2 is None:
            return P.op(eng, lambda e: e.tensor_scalar(out, a, s1, None, op0), R_, W_)
        return P.op(eng, lambda e: e.tensor_scalar(out, a, s1, s2, op0, op1), R_, W_)

    def STT(eng, out, a, s, b, op0, op1, R_, W_):
        return P.op(eng, lambda e: e.scalar_tensor_tensor(out, a, s, b, op0, op1), R_, W_)

    def CP(eng, out, a, R_, W_):
        if eng == "act":
            return P.op(eng, lambda e: e.copy(out, a), R_, W_)
        return P.op(eng, lambda e: e.tensor_copy(out, a), R_, W_)

    def ACT(out, a, func, R_, W_, **kw):
        return P.op("act", lambda e: e.activation(out, a, func, **kw), R_, W_)

    def RSQRT(out, a, scale, bias, R_, W_):
        ACT(out, a, AF.Sqrt, R_, W_, bias=bias, scale=scale)
        P.op("dve", lambda e: e.reciprocal(out, out), W_, W_)

    def MEMSET(eng, out, v, W_):
        return P.op(eng, lambda e: e.memset(out, v), (), W_)

    def MM(groups, R_, W_):
        def fn(e):
            ins = None
            for (o, l, r, st, sp) in groups:
                ins = e.matmul(o, l, r, start=st, stop=sp)
            return ins
        return P.op("pe", fn, R_, W_)

    def TR(items, R_, W_):
        def fn(e):
            ins = None
            for (o, i, idn) in items:
                ins = e.transpose(o, i, idn)
            return ins
        return P.op("pe", fn, R_, W_)

    def bc(ap, shape, axis):
        return ap.unsqueeze(axis).to_broadcast(list(shape))

    with ExitStack() as ph:
        P.dma("sp", consts.t[:, :], consts_d.ap(), W_=[consts.r])
        P.dma("sp", pp.t[:, :, :], pp_d.ap().rearrange("l p c -> p l c"), W_=[pp.r])
        P.dma("sp", oh.t[:, :], oh_d.ap().partition_broadcast(128), W_=[oh.r])
        CP("dve", cbf.t[:, 0:388], consts.t[:, 0:388], [consts.r], [cbf.r])
        stg = Ring([alloc(ph, "sb", [128, 2048], F32) for _ in range(3)])
        stb = Ring([alloc(ph, "sb", [128, 2048], BF16) for _ in range(3)])
        flip = [0]

        def cast_rows(src, dst, rows, cols, gain_col, l):
            for rc in range(rows // 128):
                for c0 in range(0, cols, 2048):
                    w = min(2048, cols - c0)
                    a = stg.next()
                    b = stb.next()
                    P.dma("sp", a.t[:, 0:w], src[rc * 128:(rc + 1) * 128, c0:c0 + w], W_=[a.r])
                    eng = ("dve", "act", "dve")[flip[0] % 3]
                    flip[0] += 1
                    if gain_col is None:
                        CP(eng, b.t[:, 0:w], a.t[:, 0:w], [a.r], [b.r])
                    elif eng == "act":
                        g = pp.t[:, l, gain_col + rc:gain_col + rc + 1]
                        P.op("act", lambda e, o_=b.t[:, 0:w], i_=a.t[:, 0:w], g_=g: e.mul(o_, i_, g_), [a.r, pp.r], [b.r])
                    else:
                        g = pp.t[:, l, gain_col + rc:gain_col + rc + 1]
                        TS(eng, b.t[:, 0:w], a.t[:, 0:w], g, None, ALU.mult, None, [a.r, pp.r], [b.r])
                    P.dma("pool", dst[rc * 128:(rc + 1) * 128, c0:c0 + w], b.t[:, 0:w], R_=[b.r], W_=[wres])

        wres = Res("weights")
        for l in range(nlayers):
            cast_rows(w_in_d.ap()[l], win_bf.ap()[l], D, INC, PP_NM, l)
            cast_rows(w_out_d.ap()[l], wout_bf.ap()[l], D, D, None, l)
            cast_rows(w1_d.ap()[l], w1_bf.ap()[l], D, DFF, PP_NMLP, l)
            cast_rows(w2_d.ap()[l], w2_bf.ap()[l], DFF, D, None, l)
        P.barrier()
        P.flush()

    x1r = Res("x1_d")
    x2r = Res("x2_d")

    def load_norm_T(ph_bufs, x_src, b, keep=None, src_res=None, skip_load=False):
        xt_ring, junk, st_ring, hb_ring, banks, hT_ring = ph_bufs
        hT = hT_ring.next()
        for s in range(4):
            tok0 = b * 512 + s * 128
            if keep is None:
                xt = xt_ring.next()
                xv = xt.t[:, :]
                xr = xt.r
            else:
                xv = keep.t[:, s, :]
                xr = keep.r
            if not skip_load:
                P.dma("sp", xv, x_src[tok0:tok0 + 128, :], R_=([src_res] if src_res is not None else []), W_=[xr])
            st = st_ring.next()
            ACT(junk.t[:, :], xv, AF.Square, [xr], [junk.r, st.r], accum_out=st.t[:, 0:1])
            RSQRT(st.t[:, 1:2], st.t[:, 0:1], 1.0 / D, EPS, [st.r], [st.r])
            hb = hb_ring.next()
            TS("dve", hb.t[:, :], xv, st.t[:, 1:2], None, ALU.mult, None, [xr, st.r], [hb.r])
            pt = banks.next()
            ptb = pt.t[:, :].bitcast(BF16)
            TR([(ptb[:, c * 128:(c + 1) * 128], hb.t[:, c * 128:(c + 1) * 128], ident_bf) for c in range(8)],
               [hb.r, cbf.r], [pt.r])
            CP("act", hT.t[:, :, s * 128:(s + 1) * 128], ptb.rearrange("p (c t) -> p c t", c=8), [pt.r], [hT.r])
        return hT

    for l in range(nlayers):
        x_cur = x_in.ap() if l == 0 else x2_d.ap()
        x_next = out_d.ap() if l == nlayers - 1 else x2_d.ap()
        winl = win_bf.ap()[l].rearrange("(c p) n -> p c n", p=128)
        ag1r = Res("ag1_in")
        ag2r = Res("ag2_in")
        scr = Res("scratch_l%d" % l)
        uTr = Res("uText")

        with ExitStack() as ph:
            SB = lambda shape, dt=F32: alloc(ph, "sb", shape, dt)
            banks = Ring([alloc(ph, "ps", [128, 512], F32) for _ in range(8)])
            xt_ring = Ring([SB([128, D]) for _ in range(2)])
            junk = SB([128, D], BF16)
            st_ring = Ring([SB([128, 2]) for _ in range(4)])
            hb_ring = Ring([SB([128, D], BF16) for _ in range(2)])
            hT_ring = Ring([SB([128, 8, 512], BF16) for _ in range(1)])
            lnt = (xt_ring, junk, st_ring, hb_ring, banks, hT_ring)
            wq_ring = Ring([SB([128, 8, 512], BF16) for _ in range(2)])
            wf_ring = Ring([SB([128, 8, 768], BF16) for _ in range(1)])
            z = [[SB([128, 1792]) for _ in range(4)] for _ in range(1)]
            gT_blk = Ring([SB([64, 4, 512], BF16) for _ in range(1)])
            uT_blk = Ring([SB([128, 2, 512]) for _ in range(1)])
            sg_ring = Ring([SB([128, 512]) for _ in range(2)])
            qT_blk = Ring([SB([128, 4, 512], BF16) for _ in range(1)])
            kT_blk = Ring([SB([128, 512], BF16) for _ in range(1)])
            QtT_blk = Ring([SB([64, 2, 4, 512], BF16) for _ in range(1)])
            oiT_blk = Ring([SB([64, 4, 512]) for _ in range(1)])
            W2 = lambda shape, dt=F32: Ring([SB(shape, dt) for _ in range(2)])
            W1 = lambda shape, dt=F32: Ring([SB(shape, dt)])
            sq_r, ssq_r, qn_r, t1_r, t2_r = W1([128, 640]), W2([128, 16]), W1([128, 640]), W1([128, 640]), W1([128, 640])
            qr_r, vb_r = W2([128, 640], BF16), W2([128, 128], BF16)
            sgf_r, sgn_r, fg_r, logf_r, kk_r = W1([128, 512]), W1([128, 512]), W1([128, 512]), W1([128, 512]), W1([128, 512])
            eb_r, enb_r = W1([128, 512]), W1([128, 512])
            Qt_r, Kt_r, Vb_r = W2([128, 2, 256], BF16), W2([128, 512], BF16), W2([128, 256], BF16)
            QtT_r, KtT_r = W2([64, 8, 128], BF16), W2([64, 8, 128], BF16)
            AT_r = [W2([128, 4, 128], BF16), W2([128, 4, 128], BF16)]
            TM_GROUPS = [(0, 512), (512, 256), (768, 512), (1280, 512)]
            gqk = SB([128, 2, 10, 64])
            lbt = SB([128, 2, 512])
            oml = SB([128, 2, 512])
            qkb = SB([128, 2, 128])
            P.dma("sp", qkb.t[:, :, :], qkn_d.ap().rearrange("l o c -> o l c").partition_broadcast(128), W_=[qkb.r])
            CP("dve", gqk.t[:, l, 0:8, :], bc(qkb.t[:, l, 0:64], [128, 8, 64], 1), [qkb.r], [gqk.r])
            CP("dve", gqk.t[:, l, 8:10, :], bc(qkb.t[:, l, 64:128], [128, 2, 64], 1), [qkb.r], [gqk.r])
            lbb = SB([128, 1024])
            P.dma("sp", lbb.t[:, :], lbp_d.ap().partition_broadcast(128), W_=[lbb.r])
            lbv = lbb.t[:, :].rearrange("p (d l c) -> p d l c", d=2, l=2)
            dlt = SB([128, 2, 256])
            TT("dve", dlt.t[:, :, :], lbv[:, :, 1, :], lbv[:, :, 0, :], ALU.subtract, [lbb.r], [dlt.r])
            MEMSET("pool", lbt.t[:, 0, :], 0.0, [lbt.r])
            ACT(lbt.t[:, 1, :], dlt.t[:, :, :].rearrange("p d c -> p (d c)"), AF.Sigmoid, [dlt.r], [lbt.r])
            TS("dve", oml.t[:, :, :], lbt.t[:, :, :], -1.0, 1.0, ALU.mult, ALU.add, [lbt.r], [oml.r])
            ropeC = SB([128, NSUB, 64])
            ropeS = SB([128, NSUB, 64])
            P.dma("sp", ropeC.t[:, :, :], ropeC_d.ap(), W_=[ropeC.r])
            P.dma("sp", ropeS.t[:, :, :], ropeS_d.ap(), W_=[ropeS.r])

            for b in range(NBLK):
                hT = load_norm_T(lnt, x_cur, b, src_res=(x2r if l > 0 else None))
                zb = z[0]
                for gi, (c0, ncol) in enumerate(TM_GROUPS):
                    wq = wq_ring.next()
                    P.dma("sp", wq.t[:, :, 0:ncol], winl[:, :, c0:c0 + ncol], R_=[wres], W_=[wq.r])
                    for s in range(4):
                        pz = banks.next()
                        MM([(pz.t[:, 0:ncol], hT.t[:, c, s * 128:(s + 1) * 128], wq.t[:, c, 0:ncol], c == 0, c == 7)
                            for c in range(8)], [hT.r, wq.r], [pz.r])
                        CP(("act", "dve")[s % 2], zb[s].t[:, c0:c0 + ncol], pz.t[:, 0:ncol], [pz.r], [zb[s].r])
                wf = wf_ring.next()
                P.dma("sp", wf.t[:, :, :], winl[:, :, C_GH:INC], R_=[wres], W_=[wf.r])
                gTb = gT_blk.next()
                for h in range(4):
                    pz = banks.next()
                    MM([(pz.t[0:64, :], wf.t[:, c, h * 64:(h + 1) * 64], hT.t[:, c, :], c == 0, c == 7) for c in range(8)],
                       [hT.r, wf.r], [pz.r])
                    ACT(gTb.t[:, h, :], pz.t[0:64, :], AF.Silu, [pz.r], [gTb.r])
                P.dma("sp", gT_d.ap()[:, :, b * 512:(b + 1) * 512], gTb.t[:, :, :], R_=[gTb.r], W_=[scr])
                uTb = uT_blk.next()
                for c2 in range(2):
                    pa = banks.next()
                    pg = banks.next()
                    MM([(pa.t[:, :], wf.t[:, c, 256 + c2 * 128:256 + (c2 + 1) * 128], hT.t[:, c, :], c == 0, c == 7)
                        for c in range(8)], [hT.r, wf.r], [pa.r])
                    MM([(pg.t[:, :], wf.t[:, c, 512 + c2 * 128:512 + (c2 + 1) * 128], hT.t[:, c, :], c == 0, c == 7)
                        for c in range(8)], [hT.r, wf.r], [pg.r])
                    sg = sg_ring.next()
                    ACT(sg.t[:, :], pg.t[:, :], AF.Sigmoid, [pg.r], [sg.r])
                    TT("dve", uTb.t[:, c2, :], pa.t[:, :], sg.t[:, :], ALU.mult, [pa.r, sg.r], [uTb.r])
                P.dma("sp", uText_d.ap()[:, :, HALO + b * 512:HALO + (b + 1) * 512], uTb.t[:, :, :], R_=[uTb.r], W_=[uTr])

                qTb = qT_blk.next()
                kTb = kT_blk.next()
                QtTb = QtT_blk.next()
                oiTb = oiT_blk.next()
                for s in range(4):
                    zs = zb[s]
                    zt = zs.t
                    ts = b * 4 + s
                    tok0 = ts * 128
                    qk = zt[:, 0:640]
                    qk3 = qk.rearrange("p (h d) -> p h d", d=64)
                    sq, ssq, qn, t1, t2, qr, vb = sq_r.next(), ssq_r.next(), qn_r.next(), t1_r.next(), t2_r.next(), qr_r.next(), vb_r.next()
                    TT("dve", sq.t[:, :], qk, qk, ALU.mult, [zs.r], [sq.r])
                    P.op("dve", lambda e, o=ssq.t[:, 0:10], i=sq.t[:, :].rearrange("p (h d) -> p h d", d=64):
                         e.tensor_reduce(o, i, AX.X, ALU.add), [sq.r], [ssq.r])
                    RSQRT(ssq.t[:, 0:10], ssq.t[:, 0:10], 1.0, 64 * EPS, [ssq.r], [ssq.r])
                    qn3 = qn.t[:, :].rearrange("p (h d) -> p h d", d=64)
                    TT("dve", qn3, qk3, bc(ssq.t[:, 0:10], [128, 10, 64], 2), ALU.mult, [zs.r, ssq.r], [qn.r])
                    TT("dve", qn3, qn3, gqk.t[:, l, :, :], ALU.mult, [qn.r, gqk.r], [qn.r])
                    t13 = t1.t[:, :].rearrange("p (h d) -> p h d", d=64)
                    t23 = t2.t[:, :].rearrange("p (h d) -> p h d", d=64)
                    TT("dve", t13, qn3, bc(ropeC.t[:, ts, :], [128, 10, 64], 1), ALU.mult, [qn.r, ropeC.r], [t1.r])
                    for a in range(2):
                        for hf in range(2):
                            o0 = a * 32 + hf * 16
                            i0 = a * 32 + (1 - hf) * 16
                            TT("dve", t23[:, :, o0:o0 + 16], qn3[:, :, i0:i0 + 16],
                               bc(ropeS.t[:, ts, o0:o0 + 16], [128, 10, 16], 1), ALU.mult, [qn.r, ropeS.r], [t2.r])
                    TT("dve", qr.t[:, 0:512].rearrange("p (j g d) -> p g j d", j=4, g=2),
                       t1.t[:, 0:512].rearrange("p (g j d) -> p g j d", g=2, j=4),
                       t2.t[:, 0:512].rearrange("p (g j d) -> p g j d", g=2, j=4), ALU.add, [t1.r, t2.r], [qr.r])
                    TT("dve", qr.t[:, 512:640], t1.t[:, 512:640], t2.t[:, 512:640], ALU.add, [t1.r, t2.r], [qr.r])
                    pt = banks.next()
                    ptb = pt.t[:, :].bitcast(BF16)
                    TR([(ptb[:, j * 128:(j + 1) * 128], qr.t[:, j * 128:(j + 1) * 128], ident_bf) for j in range(4)]
                       + [(ptb[:, 512:640], qr.t[:, 512:640], ident_bf)], [qr.r, cbf.r], [pt.r])
                    CP("act", qTb.t[:, :, s * 128:(s + 1) * 128], ptb[:, 0:512].rearrange("p (j t) -> p j t", j=4), [pt.r], [qTb.r])
                    CP("dve", kTb.t[:, s * 128:(s + 1) * 128], ptb[:, 512:640], [pt.r], [kTb.r])
                    CP("dve", vb.t[:, :], zt[:, C_V:C_V + 128], [zs.r], [vb.r])
                    P.dma("sp", ag1_in[b].ap()[128:256, :].rearrange("a (b c) -> (a b) c", c=128)[s * 128:(s + 1) * 128, :], vb.t[:, :],
                          R_=[vb.r], W_=[ag1r])
                    sgf, sgn, fg, logf, kk = sgf_r.next(), sgn_r.next(), fg_r.next(), logf_r.next(), kk_r.next()
                    eb, enb, Qt, Kt, Vb = eb_r.next(), enb_r.next(), Qt_r.next(), Kt_r.next(), Vb_r.next()
                    ff = zt[:, C_FF:C_FF + 512]
                    ACT(sgf.t[:, :], ff, AF.Sigmoid, [zs.r], [sgf.r])
                    ACT(sgn.t[:, :], ff, AF.Sigmoid, [zs.r], [sgn.r], scale=-1.0)
                    TT("dve", fg.t[:, :], sgf.t[:, :], oml.t[:, l, :], ALU.mult, [sgf.r, oml.r], [fg.r])
                    TT("dve", fg.t[:, :], fg.t[:, :], lbt.t[:, l, :], ALU.add, [fg.r, lbt.r], [fg.r])
                    ACT(logf.t[:, :], fg.t[:, :], AF.Ln, [fg.r], [logf.r])
                    TT("dve", kk.t[:, :], sgn.t[:, :], oml.t[:, l, :], ALU.mult, [sgn.r, oml.r], [kk.r])
                    pc = banks.next()
                    MM([(pc.t[:, 0:256], Mf[0], logf.t[:, 0:256], True, True),
                        (pc.t[:, 256:512], Mf[1], logf.t[:, 256:512], True, True)], [consts.r, logf.r], [pc.r])
                    ACT(eb.t[:, :], pc.t[:, :], AF.Exp, [pc.r], [eb.r])
                    ACT(enb.t[:, :], pc.t[:, :], AF.Exp, [pc.r], [enb.r], scale=-1.0)
                    TT("dve", Qt.t[:, :, :], bc(zt[:, C_QH:C_QH + 256], [128, 2, 256], 1),
                       eb.t[:, :].rearrange("p (d c) -> p d c", d=2), ALU.mult, [zs.r, eb.r], [Qt.r])
                    TT("dve", Kt.t[:, :], kk.t[:, :], enb.t[:, :], ALU.mult, [kk.r, enb.r], [Kt.r])
                    CP("dve", Vb.t[:, :], zt[:, C_IH:C_IH + 256], [zs.r], [Vb.r])
                    P.dma("sp", kt_d.ap()[tok0:tok0 + 128, :], Kt.t[:, :], R_=[Kt.r], W_=[scr])
                    P.dma("sp", vh_d.ap()[tok0:tok0 + 128, :], Vb.t[:, :], R_=[Vb.r], W_=[scr])
                    pd = banks.next()
                    MM([(pd.t[0:64, (dr * 4 + h) * 4:(dr * 4 + h) * 4 + 4], logf.t[:, dr * 256 + h * 64:dr * 256 + (h + 1) * 64], sel_f, True, True)
                        for dr in range(2) for h in range(4)], [logf.r, consts.r], [pd.r])
                    ACT(d_all.t[:, :, :, ts * 4:ts * 4 + 4].rearrange("p d h n -> p (d h) n"),
                        pd.t[0:64, 0:32].rearrange("p (a n) -> p a n", n=4), AF.Exp, [pd.r], [d_all.r])
                    pq = banks.next()
                    pqb = pq.t[:, :].bitcast(BF16)
                    TR([(pqb[0:64, (dr * 4 + h) * 128:(dr * 4 + h + 1) * 128], Qt.t[:, dr, h * 64:(h + 1) * 64], ident_bf)
                        for dr in range(2) for h in range(4)], [Qt.r, cbf.r], [pq.r])
                    QtT, KtT = QtT_r.next(), KtT_r.next()
                    CP("dve", QtT.t[:, :, :], pqb[0:64, :].rearrange("p (a t) -> p a t", a=8), [pq.r], [QtT.r])
                    CP("act", QtTb.t[:, :, :, s * 128:(s + 1) * 128].rearrange("p d h t -> p (d h) t"),
                       pqb[0:64, :].rearrange("p (a t) -> p a t", a=8), [pq.r], [QtTb.r])
                    pk = banks.next()
                    pkb = pk.t[:, :].bitcast(BF16)
                    TR([(pkb[0:64, (dr * 4 + h) * 128:(dr * 4 + h + 1) * 128], Kt.t[:, dr * 256 + h * 64:dr * 256 + (h + 1) * 64], ident_bf)
                        for dr in range(2) for h in range(4)], [Kt.r, cbf.r], [pk.r])
                    CP("dve", KtT.t[:, :, :], pkb[0:64, :].rearrange("p (a t) -> p a t", a=8), [pk.r], [KtT.r])
                    ATs = []
                    for dr in range(2):
                        pa = banks.next()
                        MM([(pa.t[:, h * 128:(h + 1) * 128], KtT.t[:, dr * 4 + h, :], QtT.t[:, dr * 4 + h, :], True, True) for h in range(4)],
                           [KtT.r, QtT.r], [pa.r])
                        AT = AT_r[dr].next()
                        TT("dve", AT.t[:, :, :], pa.t[:, :].rearrange("p (h t) -> p h t", h=4), bc(Mf[dr], [128, 4, 128], 1), ALU.mult,
                           [pa.r, consts.r], [AT.r])
                        ATs.append(AT)
                    po = banks.next()
                    MM([(po.t[0:64, h * 128:(h + 1) * 128], Vb.t[:, h * 64:(h + 1) * 64], ATs[dr].t[:, h, :], dr == 0, dr == 1)
                        for h in range(4) for dr in range(2)], [Vb.r, ATs[0].r, ATs[1].r], [po.r])
                    CP("act", oiTb.t[:, :, s * 128:(s + 1) * 128], po.t[0:64, :].rearrange("p (h t) -> p h t", h=4), [po.r], [oiTb.r])
                P.dma("sp", qT_d.ap()[:, :, b * 512:(b + 1) * 512], qTb.t[:, :, :], R_=[qTb.r], W_=[scr])
                P.dma("sp", ag1_in[b].ap()[0:128, :], kTb.t[:, :], R_=[kTb.r], W_=[ag1r])
                P.dma("sp", qtT_d.ap()[:, :, :, b * 512:(b + 1) * 512], QtTb.t[:, :, :, :], R_=[QtTb.r], W_=[scr])
                P.dma("sp", oiT_d.ap()[:, :, b * 512:(b + 1) * 512], oiTb.t[:, :, :], R_=[oiTb.r], W_=[scr])
            P.barrier()
            P.flush()
        if "stop_s1" in dbg:
            break

        ofr = [Res("of_d%d" % i) for i in range(NSUB)]
        ag1o = Res("ag1_out")
        ag2o = Res("ag2_out")
        mixr = Res("mix_d")

        def hgrn_gen(final, ph, getbank, deep):
            if True:
                SB = lambda shape, dt=F32: alloc(ph, "sb", shape, dt)
                nb = 4 if deep else 2
                Kl_ring = Ring([SB([128, 256], BF16) for _ in range(nb + 2)])
                Vl_ring = Ring([SB([128, 256], BF16) for _ in range(nb + 2)])
                Vx_ring = Ring([SB([128, 4, 4, 64], BF16) for _ in range(nb)])
                Us_ring = Ring([SB([64, 4, 4, 64]) for _ in range(nb)])
                Sp_ring = Ring([SB([64, 4, 4, 64], BF16) for _ in range(nb)])
                Ql_ring = Ring([SB([64, 4, 128], BF16) for _ in range(nb)])
                oi_ring = Ring([SB([64, 4, 128]) for _ in range(nb)])
                o_ring = Ring([SB([64, 512]) for _ in range(nb // 2)])
                sq_ring = Ring([SB([64, 512]) for _ in range(nb // 2)])
                rs_ring = Ring([SB([64, 512]) for _ in range(nb // 2)])
                gl_ring = Ring([SB([64, 4, 128], BF16) for _ in range(2)])
                mx_ring = Ring([SB([64, 4, 128], BF16) for _ in range(2)])
                Salt = [SB([64, 4, 64]) for _ in range(2)]
                dc = SB([64, 8])
                if not final:
                    zt_ = SB([128, 640])
                    MEMSET("pool", zt_.t[:, :], 0.0, [zt_.r])
                    P.dma("sp", ag2_in.ap(), zt_.t[:, :], R_=[zt_.r], W_=[ag2r])
                if final:
                    Gt = SB([64, R, 520])
                    Gh = SB([128, R, 60])
                    Pst = SB([64, 4, 64])
                    hp = SB([128, 2, 15])
                    hn = SB([128, 2, 15])
                    g2 = ag2_out[l].ap().rearrange("(r p) c -> p r c", p=128)
                    P.dma("sp", Gt.t[:, :, :], g2[0:64, :, 0:520], R_=[ag2o], W_=[Gt.r])
                    P.dma("sp", Gh.t[:, :, :], g2[:, :, 520:580], R_=[ag2o], W_=[Gh.r])
                    for dr in range(2):
                        MEMSET("pool", Pst.t[:, :, :], 0.0, [Pst.r])
                        MEMSET("pool", Sin[dr].t[:, :, :], 0.0, [Sin[dr].r])
                        for r in (range(R) if dr == 0 else range(R - 1, -1, -1)):
                            STT("dve", Sin[dr].t[:, :, :], Pst.t[:, :, :], oh.t[0:64, r:r + 1], Sin[dr].t[:, :, :], ALU.mult, ALU.add,
                                [Pst.r, oh.r, Sin[dr].r], [Sin[dr].r])
                            Tr = Gt.t[:, r, dr * 260:dr * 260 + 256].rearrange("p (h v) -> p h v", h=4)
                            Dr = Gt.t[:, r, dr * 260 + 256:dr * 260 + 260]
                            TT("dve", Pst.t[:, :, :], Pst.t[:, :, :], bc(Dr, [64, 4, 64], 2), ALU.mult, [Pst.r, Gt.r], [Pst.r])
                            TT("dve", Pst.t[:, :, :], Pst.t[:, :, :], Tr, ALU.add, [Pst.r, Gt.r], [Pst.r])
                    Gh5 = Gh.t[:, :, :].rearrange("p r (c e k) -> p r c e k", c=2, e=2)
                    for (hb_, e_, o0) in ((hp, 1, 4), (hn, 0, 8)):
                        TS("dve", hb_.t[:, :, :], Gh5[:, 0, :, e_, :], oh.t[:, o0:o0 + 1], None, ALU.mult, None, [Gh.r, oh.r], [hb_.r])
                        for r in range(1, R):
                            STT("dve", hb_.t[:, :, :], Gh5[:, r, :, e_, :], oh.t[:, o0 + r:o0 + r + 1], hb_.t[:, :, :], ALU.mult, ALU.add,
                                [Gh.r, oh.r, hb_.r], [hb_.r])
                    P.dma("sp", uText_d.ap()[:, :, 0:HALO], hp.t[:, :, :], R_=[hp.r], W_=[uTr])
                    P.dma("sp", uText_d.ap()[:, :, HALO + NT:2 * HALO + NT], hn.t[:, :, :], R_=[hn.r], W_=[uTr])

                states = {}
                for dr in range(2):
                    cur, nxt = Sst[dr], Salt[dr]
                    if final:
                        CP("dve", cur.t[:, :, :], Sin[dr].t[:, :, :], [Sin[dr].r], [cur.r])
                    else:
                        MEMSET("pool", cur.t[:, :, :], 0.0, [cur.r])
                    states[dr] = (cur, nxt)
                for i in range(NSUB):
                    for dr in range(2):
                        ts = i if dr == 0 else NSUB - 1 - i
                        first = i < NSUB // 2
                        cur, nxt = states[dr]
                        tok0 = ts * 128
                        Kl, Vl, Vx, Usb = Kl_ring.next(), Vl_ring.next(), Vx_ring.next(), Us_ring.next()
                        P.dma("sp", Kl.t[:, :], kt_d.ap()[tok0:tok0 + 128, dr * 256:(dr + 1) * 256], R_=[scr], W_=[Kl.r])
                        P.dma("sp", Vl.t[:, :], vh_d.ap()[tok0:tok0 + 128, :], R_=[scr], W_=[Vl.r])
                        TT("dve", Vx.t[:, :, :, :],
                           Vl.t[:, :].rearrange("p (h v) -> p h v", h=4).unsqueeze(2).to_broadcast([128, 4, 4, 64]),
                           sel_bf.unsqueeze(1).unsqueeze(3).to_broadcast([128, 4, 4, 64]), ALU.mult, [Vl.r, cbf.r], [Vx.r])
                        if final:
                            Ql = Ql_ring.next()
                            P.dma("sp", Ql.t[:, :, :], qtT_d.ap()[:, dr, :, tok0:tok0 + 128], R_=[scr], W_=[Ql.r])
                        yield
                        for hp2 in range(2):
                            pu = getbank()
                            MM([(pu.t[0:64, hh * 256:(hh + 1) * 256], Kl.t[:, (hp2 * 2 + hh) * 64:(hp2 * 2 + hh + 1) * 64],
                                 Vx.t[:, hp2 * 2 + hh, :, :].rearrange("p n v -> p (n v)"), True, True) for hh in range(2)],
                               [Kl.r, Vx.r], [pu.r])
                            CP("dve", Usb.t[:, hp2 * 2:hp2 * 2 + 2, :, :].rearrange("p h n v -> p h (n v)"),
                               pu.t[0:64, :].rearrange("p (h x) -> p h x", h=2), [pu.r], [Usb.r])
                        Sp = Sp_ring.next() if final else None
                        for n in (range(4) if dr == 0 else range(3, -1, -1)):
                            ch = ts * 4 + n
                            if final:
                                CP("dve", Sp.t[:, n, :, :], cur.t[:, :, :], [cur.r], [Sp.r])
                            TT("dve", nxt.t[:, :, :], cur.t[:, :, :], Usb.t[:, :, n, :], ALU.add, [cur.r, Usb.r], [nxt.r])
                            TT("dve", nxt.t[:, :, :], nxt.t[:, :, :], bc(d_all.t[:, dr, :, ch], [64, 4, 64], 2), ALU.mult,
                               [nxt.r, d_all.r], [nxt.r])
                            cur, nxt = nxt, cur
                        states[dr] = (cur, nxt)
                        yield
                        if not final:
                            continue
                        po = getbank()
                        MM([(po.t[0:64, h * 128 + n * 32:h * 128 + (n + 1) * 32], Sp.t[:, n, h, :], Ql.t[:, h, n * 32:(n + 1) * 32], True, True)
                            for n in range(4) for h in range(4)], [Sp.r, Ql.r], [po.r])
                        po3 = po.t[0:64, :].rearrange("p (h t) -> p h t", h=4)
                        oi = oi_ring.next()
                        if first:
                            P.dma("sp", oi.t[:, :, :], oiT_d.ap()[:, :, tok0:tok0 + 128], R_=[scr], W_=[oi.r])
                            TT("dve", oi.t[:, :, :], po3, oi.t[:, :, :], ALU.add, [po.r, oi.r], [oi.r])
                            P.dma("sp", of_d.ap()[:, :, tok0:tok0 + 128], oi.t[:, :, :], R_=[oi.r], W_=[ofr[ts]])
                        else:
                            P.dma("sp", oi.t[:, :, :], of_d.ap()[:, :, tok0:tok0 + 128], R_=[ofr[ts]], W_=[oi.r])
                            o, sq, rs, gl, mx = o_ring.next(), sq_ring.next(), rs_ring.next(), gl_ring.next(), mx_ring.next()
                            o3 = o.t[:, :].rearrange("p (h t) -> p h t", h=4)
                            TT("dve", o3, po3, oi.t[:, :, :], ALU.add, [po.r, oi.r], [o.r])
                            ACT(sq.t[:, :], o.t[:, :], AF.Square, [o.r], [sq.r])
                            pn = getbank()
                            MM([(pn.t[0:64, h * 128:(h + 1) * 128], ones64, sq.t[:, h * 128:(h + 1) * 128], True, True) for h in range(4)],
                               [consts.r, sq.r], [pn.r])
                            RSQRT(rs.t[:, :], pn.t[0:64, :], 1.0 / 64, EPS, [pn.r], [rs.r])
                            TT("dve", o.t[:, :], o.t[:, :], rs.t[:, :], ALU.mult, [o.r, rs.r], [o.r])
                            TT("dve", o3, o3, bc(pp.t[0:64, l, PP_GN:PP_GN + 4], [64, 4, 128], 2), ALU.mult, [o.r, pp.r], [o.r])
                            P.dma("sp", gl.t[:, :, :], gT_d.ap()[:, :, tok0:tok0 + 128], R_=[scr], W_=[gl.r])
                            TT("dve", mx.t[:, :, :], o3, gl.t[:, :, :], ALU.mult, [o.r, gl.r], [mx.r])
                            P.dma("sp", mixhg_d.ap()[:, :, tok0:tok0 + 128], mx.t[:, :, :], R_=[mx.r], W_=[mixr])
                        yield
                if not final:
                    for dr in range(2):
                        cur = states[dr][0]
                        P.dma("sp", ag2_in.ap()[0:64, dr * 260:dr * 260 + 256], cur.t[:, :, :].rearrange("p h v -> p (h v)"),
                              R_=[cur.r], W_=[ag2r])
                        P.op("dve", lambda e, o_=dc.t[:, dr * 4:dr * 4 + 4], i_=d_all.t[:, dr, :, :]: e.tensor_reduce(o_, i_, AX.X, ALU.mult),
                             [d_all.r], [dc.r])
                        P.dma("sp", ag2_in.ap()[0:64, dr * 260 + 256:dr * 260 + 260], dc.t[:, dr * 4:dr * 4 + 4], R_=[dc.r], W_=[ag2r])
                if not final:
                    a2 = ag2_in.ap()[:, 520:580].rearrange("p (c e k) -> p c e k", c=2, e=2)
                    P.dma("sp", a2[:, :, 0, :], uText_d.ap()[:, :, HALO:2 * HALO], R_=[uTr], W_=[ag2r])
                    P.dma("sp", a2[:, :, 1, :], uText_d.ap()[:, :, NT:NT + HALO], R_=[uTr], W_=[ag2r])

        for b in range(NBLK):
            P.allgather(ag1_in[b], ag1_out[l][b], R_=[ag1r], W_=[ag1o])
        with ExitStack() as ph:
            banks = Ring([alloc(ph, "ps", [128, 512], F32) for _ in range(8)])
            for _ in hgrn_gen(False, ph, banks.next, True):
                pass
            P.barrier()
            P.flush()
        P.allgather(ag2_in, ag2_out[l], R_=[ag2r], W_=[ag2o])

        def conv_gen(ph, getbank):
            SB = lambda shape, dt=F32: alloc(ph, "sb", shape, dt)
            ub_ring = Ring([SB([128, 2, 512 + 2 * HALO]) for _ in range(1)])
            acc_ring = Ring([SB([128, 2, 512]) for _ in range(1)])
            yc_ring = Ring([SB([128, 2, 512]) for _ in range(1)])
            rs_ring = Ring([SB([128, 512]) for _ in range(1)])
            mc_ring = Ring([SB([128, 2, 512], BF16) for _ in range(1)])
            cw = pp.t[:, l, PP_CW:PP_CW + 62].rearrange("p (c k) -> p c k", c=2)
            for b in range(NBLK):
                ub, acc, yc, rs, mc = ub_ring.next(), acc_ring.next(), yc_ring.next(), rs_ring.next(), mc_ring.next()
                sq = acc
                P.dma("sp", ub.t[:, :, :], uText_d.ap()[:, :, b * 512:b * 512 + 512 + 2 * HALO], R_=[uTr], W_=[ub.r])
                for c in range(2):
                    TS("dve", acc.t[:, c, :], ub.t[:, c, 0:512], cw[:, c, 0:1], pp.t[:, l, PP_CB + c:PP_CB + c + 1], ALU.mult, ALU.add,
                       [ub.r, pp.r], [acc.r])
                    for k in range(1, KW):
                        STT("dve", acc.t[:, c, :], ub.t[:, c, k:k + 512], cw[:, c, k:k + 1], acc.t[:, c, :], ALU.mult, ALU.add,
                            [ub.r, pp.r, acc.r], [acc.r])
                    yield
                pm = getbank()
                MM([(pm.t[:, :], onesq, acc.t[:, c, :], c == 0, c == 1) for c in range(2)], [consts.r, acc.r], [pm.r])
                TT("dve", yc.t[:, :, :], acc.t[:, :, :], bc(pm.t[:, :], [128, 2, 512], 1), ALU.subtract, [acc.r, pm.r], [yc.r])
                ACT(sq.t[:, :, :], yc.t[:, :, :], AF.Square, [yc.r], [sq.r])
                yield
                pv = getbank()
                MM([(pv.t[:, :], onesq, sq.t[:, c, :], c == 0, c == 1) for c in range(2)], [consts.r, sq.r], [pv.r])
                RSQRT(rs.t[:, :], pv.t[:, :], 1.0, EPS, [pv.r], [rs.r])
                TT("dve", yc.t[:, :, :], yc.t[:, :, :], bc(rs.t[:, :], [128, 2, 512], 1), ALU.mult, [yc.r, rs.r], [yc.r])
                for c in range(2):
                    ACT(mc.t[:, c, :], yc.t[:, c, :], AF.Silu, [yc.r, pp.r], [mc.r],
                        scale=pp.t[:, l, PP_LNG + c:PP_LNG + c + 1], bias=pp.t[:, l, PP_LNB + c:PP_LNB + c + 1])
                P.dma("sp", mixcv_d.ap()[:, :, b * 512:(b + 1) * 512], mc.t[:, :, :], R_=[mc.r], W_=[mixr])
                yield

        with ExitStack() as ph:
            SB = lambda shape, dt=F32: alloc(ph, "sb", shape, dt)
            st_ring = Ring([alloc(ph, "ps", [128, 1024], F32) for _ in range(3)])
            obank = [alloc(ph, "ps", [128, 512], F32) for _ in range(2)]
            tbank = st_ring

            class View:
                def __init__(self, t, r):
                    self.t = t
                    self.r = r

            pending = set()

            def take_slot():
                for _ in range(len(st_ring.bufs)):
                    sl_ = st_ring.next()
                    if id(sl_) not in pending:
                        return sl_
                raise RuntimeError("no free score slot")

            def halfbank():
                sl_ = take_slot()
                return View(sl_.t[:, 0:512], sl_.r)

            kT_all = SB([128, L], BF16)
            v_all = SB([128, T, 2, 128], BF16)
            QT_ring = Ring([SB([128, 4, 512], BF16) for _ in range(2)])
            pT_ring = Ring([SB([128, 1024], BF16) for _ in range(3)])
            osb_ring = Ring([SB([128, 512]) for _ in range(2)])
            rec_ring = Ring([SB([64, 512]) for _ in range(1)])
            mixA_ring = Ring([SB([64, 8, 512], BF16) for _ in range(1)])
            MEMSET("pool", v_all.t[:, :, :, 64:128], 1.0, [v_all.r])
            for r in range(R):
                for b in range(NBLK):
                    go = ag1_out[l][b].ap()
                    P.dma("sp", kT_all.t[:, r * NT + b * 512:r * NT + (b + 1) * 512], go[r * 256:r * 256 + 128, :], R_=[ag1o], W_=[kT_all.r])
                    vsrc = go[r * 256 + 128:r * 256 + 256, :].rearrange("a (b c) -> (a b) c", c=128)
                    vsrc = vsrc.rearrange("(t p) (g d) -> p t g d", p=128, g=2)
                    t0 = r * NSUB + b * 4
                    for g in range(2):
                        P.dma("sp", v_all.t[:, t0:t0 + 4, g, 0:64], vsrc[:, :, g, :], R_=[ag1o], W_=[v_all.r])
            hgen = hgrn_gen(True, ph, halfbank, False)
            cgen = conv_gen(ph, halfbank)
            side = {"h": True, "c": True}

            def step(which):
                g_ = hgen if which == "h" else cgen
                if side[which]:
                    try:
                        next(g_)
                    except StopIteration:
                        side[which] = False

            nsteps = NBLK * 4 * T
            h_every = max(1, nsteps // (6 * NSUB + 8))
            c_every = max(1, nsteps // (4 * NBLK + 2))
            step("h")
            k = 0
            for b in range(NBLK):
                QT = QT_ring.next()
                P.dma("sp", QT.t[:, :, :], qT_d.ap()[:, :, b * 512:(b + 1) * 512], R_=[scr], W_=[QT.r])
                mixA = mixA_ring.next()
                for j in range(4):
                    def S_mm(t, j=j, QT=QT):
                        st = take_slot()
                        pending.add(id(st))
                        MM([(st.t[:, 0:512], kT_all.t[0:64, t * 128:(t + 1) * 128], QT.t[0:64, j, :], True, True),
                            (st.t[:, 512:1024], kT_all.t[64:128, t * 128:(t + 1) * 128], QT.t[64:128, j, :], True, True)],
                           [kT_all.r, QT.r], [st.r])
                        return st
                    pend = [S_mm(0)]
                    if T > 1:
                        pend.append(S_mm(1))
                    for t in range(T):
                        st = pend.pop(0)
                        pT = pT_ring.next()
                        ACT(pT.t[:, :], st.t[:, :], AF.Exp, [st.r], [pT.r], scale=8.0)
                        pending.discard(id(st))
                        if t + 2 < T:
                            pend.append(S_mm(t + 2))
                        MM([(obank[0].t[:, :], v_all.t[:, t, 0, :], pT.t[:, 0:512], t == 0, t == T - 1),
                            (obank[1].t[:, :], v_all.t[:, t, 1, :], pT.t[:, 512:1024], t == 0, t == T - 1)],
                           [v_all.r, pT.r], [obank[0].r, obank[1].r])
                        k += 1
                        if k % h_every == 0:
                            step("h")
                        if k % c_every == 0:
                            step("c")
                    for g in range(2):
                        osb, rec = osb_ring.next(), rec_ring.next()
                        CP("dve", osb.t[:, :], obank[g].t[:, :], [obank[g].r], [osb.r])
                        pr = take_slot()
                        MM([(pr.t[0:64, 0:512], shiftM, osb.t[:, :], True, True)], [consts.r, osb.r], [pr.r])
                        P.op("dve", lambda e, o_=rec.t[:, :], i_=pr.t[0:64, 0:512]: e.reciprocal(o_, i_), [pr.r], [rec.r])
                        TT("dve", mixA.t[:, g * 4 + j, :], osb.t[0:64, :], rec.t[:, :], ALU.mult, [osb.r, rec.r], [mixA.r])
                P.dma("sp", mixatt_d.ap()[:, :, b * 512:(b + 1) * 512], mixA.t[:, :, :], R_=[mixA.r], W_=[mixr])
            while side["h"]:
                step("h")
            while side["c"]:
                step("c")
            P.barrier()
            P.flush()
        if "stop_s6" in dbg:
            break

        with ExitStack() as ph:
            SB = lambda shape, dt=F32: alloc(ph, "sb", shape, dt)
            banks = Ring([alloc(ph, "ps", [128, 512], F32) for _ in range(4)])
            accb = [alloc(ph, "ps", [128, 512], F32) for _ in range(4)]
            junk = SB([128, D], BF16)
            st_ring = Ring([SB([128, 2]) for _ in range(8)])
            hb_ring = Ring([SB([128, D], BF16) for _ in range(4)])
            hT_ring = Ring([SB([128, 8, 512], BF16) for _ in range(2)])
            xb_ring = Ring([SB([128, 4, D]) for _ in range(2)])
            aT = SB([128, 32, 512], BF16)
            w1q_ring = Ring([SB([128, 8, 512], BF16) for _ in range(2)])
            w2q_ring = Ring([SB([128, 4, 512], BF16) for _ in range(3)])
            rl_ring = Ring([SB([128, 512]) for _ in range(2)])
            w1l = w1_bf.ap()[l].rearrange("(c p) n -> p c n", p=128)
            w2l = w2_bf.ap()[l].rearrange("(j p) n -> p j n", p=128)
            wo_att = SB([64, 8, D], BF16)
            wo_hg = SB([64, 4, D], BF16)
            wo_cv = SB([128, 2, D], BF16)
            ma_ring = Ring([SB([64, 8, 512], BF16) for _ in range(1)])
            mh_ring = Ring([SB([64, 4, 512], BF16) for _ in range(1)])
            mcv_ring = Ring([SB([128, 2, 512], BF16) for _ in range(1)])
            wol = wout_bf.ap()[l]
            P.dma("sp", wo_att.t[:, :, :], wol[0:512, :].rearrange("(h d) n -> d h n", d=64), R_=[wres], W_=[wo_att.r])
            P.dma("sp", wo_hg.t[:, :, :], wol[512:768, :].rearrange("(h d) n -> d h n", d=64), R_=[wres], W_=[wo_hg.r])
            P.dma("sp", wo_cv.t[:, :, :], wol[768:1024, :].rearrange("(c p) n -> p c n", p=128), R_=[wres], W_=[wo_cv.r])
            def wo_norm(b):
                xb = xb_ring.next()
                ma, mh, mcv = ma_ring.next(), mh_ring.next(), mcv_ring.next()
                P.dma("sp", ma.t[:, :, :], mixatt_d.ap()[:, :, b * 512:(b + 1) * 512], R_=[mixr], W_=[ma.r])
                P.dma("sp", mh.t[:, :, :], mixhg_d.ap()[:, :, b * 512:(b + 1) * 512], R_=[mixr], W_=[mh.r])
                P.dma("sp", mcv.t[:, :, :], mixcv_d.ap()[:, :, b * 512:(b + 1) * 512], R_=[mixr], W_=[mcv.r])
                for s in range(4):
                    P.dma("sp", xb.t[:, s, :], x_cur[b * 512 + s * 128:b * 512 + (s + 1) * 128, :], W_=[xb.r])
                hbs = []
                for s in range(4):
                    for n2 in range(2):
                        pw = banks.next()
                        sl = slice(s * 128, (s + 1) * 128)
                        nl = slice(n2 * 512, (n2 + 1) * 512)
                        grp = [(pw.t[:, :], ma.t[:, h, sl], wo_att.t[:, h, nl], h == 0, False) for h in range(8)]
                        grp += [(pw.t[:, :], mh.t[:, h, sl], wo_hg.t[:, h, nl], False, False) for h in range(4)]
                        grp += [(pw.t[:, :], mcv.t[:, c, sl], wo_cv.t[:, c, nl], False, c == 1) for c in range(2)]
                        MM(grp, [ma.r, mh.r, mcv.r, wo_att.r, wo_hg.r, wo_cv.r], [pw.r])
                        TT("dve", xb.t[:, s, nl], pw.t[:, :], xb.t[:, s, nl], ALU.add, [pw.r, xb.r], [xb.r])
                    st = st_ring.next()
                    ACT(junk.t[:, :], xb.t[:, s, :], AF.Square, [xb.r], [junk.r, st.r], accum_out=st.t[:, 0:1])
                    RSQRT(st.t[:, 1:2], st.t[:, 0:1], 1.0 / D, EPS, [st.r], [st.r])
                    hb = hb_ring.next()
                    TS("dve", hb.t[:, :], xb.t[:, s, :], st.t[:, 1:2], None, ALU.mult, None, [xb.r, st.r], [hb.r])
                    hbs.append(hb)
                return xb, hbs

            def transposes(hbs):
                hT = hT_ring.next()
                for s, hb in enumerate(hbs):
                    pt = banks.next()
                    ptb = pt.t[:, :].bitcast(BF16)
                    TR([(ptb[:, c * 128:(c + 1) * 128], hb.t[:, c * 128:(c + 1) * 128], ident_bf) for c in range(8)],
                       [hb.r, cbf.r], [pt.r])
                    CP("act", hT.t[:, :, s * 128:(s + 1) * 128], ptb.rearrange("p (c t) -> p c t", c=8), [pt.r], [hT.r])
                return hT

            nxt_blk = wo_norm(0)
            nxt_hT = transposes(nxt_blk[1])
            for b in range(NBLK):
                xb, hT = nxt_blk[0], nxt_hT
                if b + 1 < NBLK:
                    nxt_blk = wo_norm(b + 1)
                for fg in range(8):
                    w1q = w1q_ring.next()
                    P.dma("sp", w1q.t[:, :, :], w1l[:, :, fg * 512:(fg + 1) * 512], R_=[wres], W_=[w1q.r])
                    for jj in range(4):
                        pa = banks.next()
                        MM([(pa.t[:, :], w1q.t[:, c, jj * 128:(jj + 1) * 128], hT.t[:, c, :], c == 0, c == 7) for c in range(8)],
                           [w1q.r, hT.r], [pa.r])
                        rl = rl_ring.next()
                        ACT(rl.t[:, :], pa.t[:, :], AF.Relu, [pa.r], [rl.r])
                        TT("dve", aT.t[:, fg * 4 + jj, :], rl.t[:, :], rl.t[:, :], ALU.mult, [rl.r], [aT.r])
                if b + 1 < NBLK:
                    nxt_hT = transposes(nxt_blk[1])
                for n2 in range(2):
                    nl = slice(n2 * 512, (n2 + 1) * 512)
                    for pc in range(8):
                        w2q = w2q_ring.next()
                        P.dma("sp", w2q.t[:, :, :], w2l[:, pc * 4:(pc + 1) * 4, nl], R_=[wres], W_=[w2q.r])
                        for s in range(4):
                            MM([(accb[s].t[:, :], aT.t[:, pc * 4 + jj, s * 128:(s + 1) * 128], w2q.t[:, jj, :],
                                 pc == 0 and jj == 0, pc == 7 and jj == 3) for jj in range(4)], [aT.r, w2q.r], [accb[s].r])
                    for s in range(4):
                        TT("dve", xb.t[:, s, nl], accb[s].t[:, :], xb.t[:, s, nl], ALU.add, [accb[s].r, xb.r], [xb.r])
                for s in range(4):
                    P.dma("sp", x_next[b * 512 + s * 128:b * 512 + (s + 1) * 128, :], xb.t[:, s, :], R_=[xb.r], W_=[x2r])
            P.barrier()
            P.flush()

    P.barrier()
    P.flush()
    es.close()
    return nc


def _consts():
    c = np.zeros((128, K_END), np.float32)
    j = np.arange(128)[:, None]
    i = np.arange(128)[None, :]
    c[:, K_ID:K_ID + 128] = (j == i)
    same = (j // CH) == (i // CH)
    c[:, K_MFW:K_MFW + 128] = same & (j <= i)
    c[:, K_MBW:K_MBW + 128] = same & (j >= i)
    c[:, K_SEL:K_SEL + 4] = (j // CH) == np.arange(4)[None, :]
    c[64, K_SHIFT:K_SHIFT + 64] = 1.0
    c[:, K_ONES:K_ONES + 64] = 1.0
    c[:, K_ONESQ:K_ONESQ + 128] = 1.0 / 256.0
    return c


def _rope_tables(pos):
    inv = (10000.0 ** (-np.arange(0, 32, 2, dtype=np.float32) / 32.0)).astype(np.float32)
    row = (pos // 64).astype(np.float32)[:, None] * inv[None, :]
    col = (pos % 64).astype(np.float32)[:, None] * inv[None, :]
    cr, sr, cc, sc = np.cos(row), np.sin(row), np.cos(col), np.sin(col)
    C = np.concatenate([cr, cr, cc, cc], 1).astype(np.float32)
    S = np.concatenate([-sr, sr, -sc, sc], 1).astype(np.float32)
    return C, S


def prepare(inputs, NT):
    f = lambda a: np.ascontiguousarray(np.asarray(a, dtype=np.float32))
    x = f(inputs["x"])
    B = x.shape[0]
    pp = np.zeros((2, 128, PP_END), np.float32)
    for l in range(2):
        pp[l, :, PP_NM:PP_NM + 8] = f(inputs["norm_mix"])[l].reshape(8, 128).T
        pp[l, :, PP_NMLP:PP_NMLP + 8] = f(inputs["norm_mlp"])[l].reshape(8, 128).T
        pp[l, 0:64, PP_GN:PP_GN + 4] = f(inputs["hgrn_norm"])[l].reshape(4, 64).T
        pp[l, :, PP_CW:PP_CW + 62] = f(inputs["conv_w"])[l].T.reshape(2, 128, 31).transpose(1, 0, 2).reshape(128, 62)
        pp[l, :, PP_CB:PP_CB + 2] = f(inputs["conv_b"])[l].reshape(2, 128).T
        pp[l, :, PP_LNG:PP_LNG + 2] = f(inputs["conv_ln_g"])[l].reshape(2, 128).T
        pp[l, :, PP_LNB:PP_LNB + 2] = f(inputs["conv_ln_b"])[l].reshape(2, 128).T
    qkn = np.concatenate([f(inputs["q_norm"]), f(inputs["k_norm"])], 1).reshape(2, 1, 128)
    lbp = np.concatenate([f(inputs["hgrn_lb_fwd"]).reshape(-1), f(inputs["hgrn_lb_bwd"]).reshape(-1)]).reshape(1, 1024)
    consts = _consts()
    shared = {"w_in": f(inputs["w_in"]), "w_out": f(inputs["w_out"]), "w_mlp_in": f(inputs["w_mlp_in"]),
              "w_mlp_out": f(inputs["w_mlp_out"]), "pp": pp, "qkn": np.ascontiguousarray(qkn), "lbp": lbp, "consts": consts}
    maps = []
    for c in range(B * R):
        b, r = c // R, c % R
        pos = np.arange(r * NT, (r + 1) * NT)
        C, S = _rope_tables(pos)
        oh = np.zeros((1, 12), np.float32)
        oh[0, r] = 1.0
        if r > 0:
            oh[0, 4 + r - 1] = 1.0
        if r < R - 1:
            oh[0, 8 + r + 1] = 1.0
        m = dict(shared)
        C = np.ascontiguousarray(C.reshape(NT // 128, 128, 64).transpose(1, 0, 2))
        S = np.ascontiguousarray(S.reshape(NT // 128, 128, 64).transpose(1, 0, 2))
        m.update({"x": np.ascontiguousarray(x[b, r * NT:(r + 1) * NT]), "ropeC": C, "ropeS": S, "oh": oh})
        maps.append(m)
    return maps


_CACHE = {}


def kernel(**inputs):
    x = np.asarray(inputs["x"])
    B, Lfull, _ = x.shape
    NT = Lfull // R
    if NT not in _CACHE:
        _CACHE[NT] = build_program(NT)
    nc = _CACHE[NT]
    maps = prepare(inputs, NT)
    res = run_bass_kernel_spmd(nc, maps, core_ids=list(range(B * R)))
    out = np.empty((B, Lfull, D), np.float32)
    for c in range(B * R):
        out[c // R, (c % R) * NT:(c % R + 1) * NT] = np.asarray(res.results[c]["out"], dtype=np.float32)
    return out
```

```python
from contextlib import ExitStack
import numpy as np
import concourse.bass as bass
import concourse.mybir as mybir
from concourse.bass_utils import run_bass_kernel_spmd

F32 = mybir.dt.float32
BF16 = mybir.dt.bfloat16
AF = mybir.ActivationFunctionType
ALU = mybir.AluOpType
AX = mybir.AxisListType

D = 1024
DFF = 4096
INC = 2560
EPS = 1e-6
R = 4
NCORES = 8
CH = 32
HALO = 15
KW = 31


class Res:
    __slots__ = ("w", "r", "name", "excl")

    def __init__(self, name="", excl=False):
        self.w = None
        self.r = []
        self.name = name
        self.excl = excl


class Prog:
    ENG = ("pe", "act", "dve", "pool", "sp")

    def __init__(self, nc, es):
        self.nc = nc
        self.es = es
        self.q = {e: [] for e in self.ENG}
        self.sem = {e: es.enter_context(nc.semaphore("sem_" + e)) for e in ("pe", "act", "dve", "pool")}
        self.cnt = {e: 0 for e in self.sem}
        self.seen = {e: {} for e in self.ENG}
        self.dsem = {}
        self.dcnt = {}
        self.drr = {}
        for qn, n in (("sp", 12), ("pool", 6), ("act", 4)):
            self.dsem[qn] = [es.enter_context(nc.semaphore("dma_%s_%d" % (qn, i))) for i in range(n)]
            self.dcnt[qn] = [0] * n
            self.drr[qn] = 0
        self.cctoks = []

    def _waits(self, eng, R_, W_):
        need = {}

        def add(tok):
            if tok is None:
                return
            s, v = tok
            k = id(s)
            if k not in need or need[k][1] < v:
                need[k] = (s, v)

        for r in R_:
            add(r.w)
            if r.excl:
                for t in r.r:
                    add(t)
        for w in W_:
            add(w.w)
            for t in w.r:
                add(t)
        out = []
        own = self.sem.get(eng)
        for k, (s, v) in need.items():
            if eng == "pe" and s is own:
                continue
            if self.seen[eng].get(k, 0) >= v:
                continue
            self.seen[eng][k] = v
            out.append((s, v))
        return out

    def _commit(self, tok, R_, W_):
        for r in R_:
            r.r.append(tok)
        for w in W_:
            w.w = tok
            w.r = []

    def op(self, eng, fn, R_=(), W_=()):
        waits = self._waits(eng, R_, W_)
        self.cnt[eng] += 1
        tok = (self.sem[eng], self.cnt[eng])
        sem = self.sem[eng]

        def c(e):
            for s, v in waits:
                e.wait_ge(s, v)
            fn(e).then_inc(sem, 1)

        self.q[eng].append(c)
        self._commit(tok, R_, W_)
        return tok

    def dma(self, qn, out, in_, R_=(), W_=(), **kw):
        i = self.drr[qn]
        self.drr[qn] = (i + 1) % len(self.dsem[qn])
        s = self.dsem[qn][i]
        prev = self.dcnt[qn][i]
        waits = self._waits(qn, R_, W_)
        k = id(s)
        if prev > 0 and self.seen[qn].get(k, 0) < prev:
            self.seen[qn][k] = prev
            waits.append((s, prev))
        self.dcnt[qn][i] = prev + 16
        tok = (s, prev + 16)

        def c(e):
            for s_, v in waits:
                e.wait_ge(s_, v)
            e.dma_start(out=out, in_=in_, **kw).then_inc(s, 16)

        self.q[qn].append(c)
        self._commit(tok, R_, W_)
        return tok

    def allgather(self, in_t, out_t, R_=(), W_=()):
        waits = self._waits("pool", R_, W_)
        s = self.es.enter_context(self.nc.semaphore("cc_sem%d" % len(self.cctoks)))
        tok = (s, 1)
        self.cctoks.append(tok)
        groups = [[0, 1, 2, 3], [4, 5, 6, 7]]

        def c(e):
            for s_, v in waits:
                e.wait_ge(s_, v)
            e.collective_compute("AllGather", ALU.bypass, replica_groups=groups,
                                 ins=[in_t.ap().opt()], outs=[out_t.ap().opt()]).then_inc(s)

        self.q["pool"].append(c)
        self._commit(tok, R_, W_)
        return tok

    def barrier(self):
        toks = [(self.sem[e], self.cnt[e]) for e in self.sem if self.cnt[e] > 0]
        for qn in self.dsem:
            for s, v in zip(self.dsem[qn], self.dcnt[qn]):
                if v > 0:
                    toks.append((s, v))
        toks.extend(self.cctoks)
        for eng in self.ENG:
            wl = []
            for s, v in toks:
                if self.seen[eng].get(id(s), 0) >= v:
                    continue
                self.seen[eng][id(s)] = v
                wl.append((s, v))

            def c(e, wl=wl):
                for s_, v in wl:
                    e.wait_ge(s_, v)

            self.q[eng].append(c)

    def flush(self):
        nc = self.nc
        q = self.q
        with nc.Block() as block:
            @block.tensor
            def _(e):
                for c in q["pe"]:
                    c(e)

            @block.scalar
            def _(e):
                for c in q["act"]:
                    c(e)

            @block.vector
            def _(e):
                for c in q["dve"]:
                    c(e)

            @block.gpsimd
            def _(e):
                for c in q["pool"]:
                    c(e)

            @block.sync
            def _(e):
                for c in q["sp"]:
                    c(e)
        self.q = {e: [] for e in self.ENG}


class Buf:
    __slots__ = ("t", "r")

    def __init__(self, t):
        self.t = t
        self.r = Res()


class Ring:
    def __init__(self, bufs):
        self.bufs = bufs
        self.i = 0

    def next(self):
        b = self.bufs[self.i]
        self.i = (self.i + 1) % len(self.bufs)
        return b


C_Q, C_K, C_V, C_QH, C_IH, C_FF, C_GH, C_AC, C_GC = 0, 512, 640, 768, 1024, 1280, 1792, 2048, 2304
K_ID, K_MFW, K_MBW, K_SEL, K_SHIFT, K_ONES, K_ONESQ, K_END = 0, 128, 256, 384, 388, 452, 516, 644
PP_NM, PP_NMLP, PP_GN, PP_CW, PP_CB, PP_LNG, PP_LNB, PP_END = 0, 8, 16, 20, 82, 84, 86, 88


def build_program(NT, nlayers=2, dbg=()):
    nc = bass.Bass("TRN2", target_bir_lowering=False)
    L = R * NT
    NBLK = NT // 512
    NSUB = NT // 128
    T = L // 128
    NCH = NT // CH

    def din(name, shape, dt=F32):
        return nc.dram_tensor(name, list(shape), dt, kind="ExternalInput")

    x_in = din("x", [NT, D])
    w_in_d = din("w_in", [2, D, INC])
    w_out_d = din("w_out", [2, D, D])
    w1_d = din("w_mlp_in", [2, D, DFF])
    w2_d = din("w_mlp_out", [2, DFF, D])
    pp_d = din("pp", [2, 128, PP_END])
    qkn_d = din("qkn", [2, 1, 128])
    lbp_d = din("lbp", [1, 1024])
    consts_d = din("consts", [128, K_END])
    ropeC_d = din("ropeC", [128, NT // 128, 64])
    ropeS_d = din("ropeS", [128, NT // 128, 64])
    oh_d = din("oh", [1, 12])
    out_d = nc.dram_tensor("out", [NT, D], F32, kind="ExternalOutput")

    def dscr(name, shape, dt=F32):
        if name in dbg:
            return nc.dram_tensor(name, list(shape), dt, kind="ExternalOutput")
        return nc.dram_tensor(name, list(shape), dt)

    win_bf = dscr("win_bf", [2, D, INC], BF16)
    wout_bf = dscr("wout_bf", [2, D, D], BF16)
    w1_bf = dscr("w1_bf", [2, D, DFF], BF16)
    w2_bf = dscr("w2_bf", [2, DFF, D], BF16)
    qT_d = dscr("qT_d", [128, 4, NT], BF16)
    ag1_in = [dscr("ag1_in%d" % b, [256, 512], BF16) for b in range(NBLK)]
    ag1_out = [[dscr("ag1_out%d_%d" % (l, b), [R * 256, 512], BF16) for b in range(NBLK)] for l in range(2)]
    ag2_in = dscr("ag2_in", [128, 640], F32)
    ag2_out = [dscr("ag2_out%d" % l, [R * 128, 640], F32) for l in range(2)]
    qtT_d = dscr("qtT_d", [64, 2, 4, NT], BF16)
    kt_d = dscr("kt_d", [NT, 512], BF16)
    vh_d = dscr("vh_d", [NT, 256], BF16)
    oiT_d = dscr("oiT_d", [64, 4, NT], F32)
    of_d = dscr("of_d", [64, 4, NT], F32)
    gT_d = dscr("gT_d", [64, 4, NT], BF16)
    uText_d = dscr("uText_d", [128, 2, NT + 2 * HALO], F32)
    mixhg_d = dscr("mixhg_d", [64, 4, NT], BF16)
    mixcv_d = dscr("mixcv_d", [128, 2, NT], BF16)
    mixatt_d = dscr("mixatt_d", [64, 8, NT], BF16)
    x1_d = dscr("x1_d", [NT, D], F32)
    x2_d = dscr("x2_d", [NT, D], F32)

    es = ExitStack()
    P = Prog(nc, es)
    cnt = [0]

    def alloc(stack, kind, shape, dt):
        cnt[0] += 1
        f = nc.sbuf_tensor if kind == "sb" else nc.psum_tensor
        bf_ = Buf(stack.enter_context(f("%s%d" % (kind, cnt[0]), list(shape), dt)))
        bf_.r.excl = (kind == "ps")
        return bf_

    G = es
    consts = alloc(G, "sb", [128, K_END], F32)
    cbf = alloc(G, "sb", [128, 388], BF16)
    pp = alloc(G, "sb", [128, 2, PP_END], F32)
    oh = alloc(G, "sb", [128, 12], F32)
    d_all = alloc(G, "sb", [64, 2, 4, NCH], F32)
    Sst = [alloc(G, "sb", [64, 4, 64], F32) for _ in range(2)]
    Sin = [alloc(G, "sb", [64, 4, 64], F32) for _ in range(2)]

    ident_bf = cbf.t[:, 0:128]
    ident_f = consts.t[:, K_ID:K_ID + 128]
    Mf = [consts.t[:, K_MFW:K_MFW + 128], consts.t[:, K_MBW:K_MBW + 128]]
    sel_f = consts.t[:, K_SEL:K_SEL + 4]
    sel_bf = cbf.t[:, 384:388]
    shiftM = consts.t[:, K_SHIFT:K_SHIFT + 64]
    ones64 = consts.t[0:64, K_ONES:K_ONES + 64]
    onesq = consts.t[:, K_ONESQ:K_ONESQ + 128]

    def TT(eng, out, a, b, op, R_, W_):
        return P.op(eng, lambda e: e.tensor_tensor(out, a, b, op), R_, W_)

    def TS(eng, out, a, s1, s2, op0, op1, R_, W_):
        if s2 is None:
            return P.op(eng, lambda e: e.tensor_scalar(out, a, s1, None, op0), R_, W_)
        return P.op(eng, lambda e: e.tensor_scalar(out, a, s1, s2, op0, op1), R_, W_)

    def STT(eng, out, a, s, b, op0, op1, R_, W_):
        return P.op(eng, lambda e: e.scalar_tensor_tensor(out, a, s, b, op0, op1), R_, W_)

    def CP(eng, out, a, R_, W_):
        if eng == "act":
            return P.op(eng, lambda e: e.copy(out, a), R_, W_)
        return P.op(eng, lambda e: e.tensor_copy(out, a), R_, W_)

    def ACT(out, a, func, R_, W_, **kw):
        return P.op("act", lambda e: e.activation(out, a, func, **kw), R_, W_)

    def RSQRT(out, a, scale, bias, R_, W_):
        ACT(out, a, AF.Sqrt, R_, W_, bias=bias, scale=scale)
        P.op("dve", lambda e: e.reciprocal(out, out), W_, W_)

    def MEMSET(eng, out, v, W_):
        return P.op(eng, lambda e: e.memset(out, v), (), W_)

    def MM(groups, R_, W_):
        def fn(e):
            ins = None
            for (o, l, r, st, sp) in groups:
                ins = e.matmul(o, l, r, start=st, stop=sp)
            return ins
        return P.op("pe", fn, R_, W_)

    def TR(items, R_, W_):
        def fn(e):
            ins = None
            for (o, i, idn) in items:
                ins = e.transpose(o, i, idn)
            return ins
        return P.op("pe", fn, R_, W_)

    def bc(ap, shape, axis):
        return ap.unsqueeze(axis).to_broadcast(list(shape))

    with ExitStack() as ph:
        P.dma("sp", consts.t[:, :], consts_d.ap(), W_=[consts.r])
        P.dma("sp", pp.t[:, :, :], pp_d.ap().rearrange("l p c -> p l c"), W_=[pp.r])
        P.dma("sp", oh.t[:, :], oh_d.ap().partition_broadcast(128), W_=[oh.r])
        CP("dve", cbf.t[:, 0:388], consts.t[:, 0:388], [consts.r], [cbf.r])
        stg = Ring([alloc(ph, "sb", [128, 2048], F32) for _ in range(3)])
        stb = Ring([alloc(ph, "sb", [128, 2048], BF16) for _ in range(3)])
        flip = [0]

        def cast_rows(src, dst, rows, cols, gain_col, l):
            for rc in range(rows // 128):
                for c0 in range(0, cols, 2048):
                    w = min(2048, cols - c0)
                    a = stg.next()
                    b = stb.next()
                    P.dma("sp", a.t[:, 0:w], src[rc * 128:(rc + 1) * 128, c0:c0 + w], W_=[a.r])
                    eng = ("dve", "act", "dve")[flip[0] % 3]
                    flip[0] += 1
                    if gain_col is None:
                        CP(eng, b.t[:, 0:w], a.t[:, 0:w], [a.r], [b.r])
                    elif eng == "act":
                        g = pp.t[:, l, gain_col + rc:gain_col + rc + 1]
                        P.op("act", lambda e, o_=b.t[:, 0:w], i_=a.t[:, 0:w], g_=g: e.mul(o_, i_, g_), [a.r, pp.r], [b.r])
                    else:
                        g = pp.t[:, l, gain_col + rc:gain_col + rc + 1]
                        TS(eng, b.t[:, 0:w], a.t[:, 0:w], g, None, ALU.mult, None, [a.r, pp.r], [b.r])
                    P.dma("pool", dst[rc * 128:(rc + 1) * 128, c0:c0 + w], b.t[:, 0:w], R_=[b.r], W_=[wres])

        wres = Res("weights")
        for l in range(nlayers):
            cast_rows(w_in_d.ap()[l], win_bf.ap()[l], D, INC, PP_NM, l)
            cast_rows(w_out_d.ap()[l], wout_bf.ap()[l], D, D, None, l)
            cast_rows(w1_d.ap()[l], w1_bf.ap()[l], D, DFF, PP_NMLP, l)
            cast_rows(w2_d.ap()[l], w2_bf.ap()[l], DFF, D, None, l)
        P.barrier()
        P.flush()

    x1r = Res("x1_d")
    x2r = Res("x2_d")

    def load_norm_T(ph_bufs, x_src, b, keep=None, src_res=None, skip_load=False):
        xt_ring, junk, st_ring, hb_ring, banks, hT_ring = ph_bufs
        hT = hT_ring.next()
        for s in range(4):
            tok0 = b * 512 + s * 128
            if keep is None:
                xt = xt_ring.next()
                xv = xt.t[:, :]
                xr = xt.r
            else:
                xv = keep.t[:, s, :]
                xr = keep.r
            if not skip_load:
                P.dma("sp", xv, x_src[tok0:tok0 + 128, :], R_=([src_res] if src_res is not None else []), W_=[xr])
            st = st_ring.next()
            ACT(junk.t[:, :], xv, AF.Square, [xr], [junk.r, st.r], accum_out=st.t[:, 0:1])
            RSQRT(st.t[:, 1:2], st.t[:, 0:1], 1.0 / D, EPS, [st.r], [st.r])
            hb = hb_ring.next()
            TS("dve", hb.t[:, :], xv, st.t[:, 1:2], None, ALU.mult, None, [xr, st.r], [hb.r])
            pt = banks.next()
            ptb = pt.t[:, :].bitcast(BF16)
            TR([(ptb[:, c * 128:(c + 1) * 128], hb.t[:, c * 128:(c + 1) * 128], ident_bf) for c in range(8)],
               [hb.r, cbf.r], [pt.r])
            CP("act", hT.t[:, :, s * 128:(s + 1) * 128], ptb.rearrange("p (c t) -> p c t", c=8), [pt.r], [hT.r])
        return hT

    for l in range(nlayers):
        x_cur = x_in.ap() if l == 0 else x2_d.ap()
        x_next = out_d.ap() if l == nlayers - 1 else x2_d.ap()
        winl = win_bf.ap()[l].rearrange("(c p) n -> p c n", p=128)
        ag1r = Res("ag1_in")
        ag2r = Res("ag2_in")
        scr = Res("scratch_l%d" % l)
        uTr = Res("uText")

        with ExitStack() as ph:
            SB = lambda shape, dt=F32: alloc(ph, "sb", shape, dt)
            banks = Ring([alloc(ph, "ps", [128, 512], F32) for _ in range(8)])
            xt_ring = Ring([SB([128, D]) for _ in range(2)])
            junk = SB([128, D], BF16)
            st_ring = Ring([SB([128, 2]) for _ in range(4)])
            hb_ring = Ring([SB([128, D], BF16) for _ in range(2)])
            hT_ring = Ring([SB([128, 8, 512], BF16) for _ in range(1)])
            lnt = (xt_ring, junk, st_ring, hb_ring, banks, hT_ring)
            wq_ring = Ring([SB([128, 8, 512], BF16) for _ in range(2)])
            wf_ring = Ring([SB([128, 8, 768], BF16) for _ in range(1)])
            z = [[SB([128, 1792]) for _ in range(4)] for _ in range(1)]
            gT_blk = Ring([SB([64, 4, 512], BF16) for _ in range(1)])
            uT_blk = Ring([SB([128, 2, 512]) for _ in range(1)])
            sg_ring = Ring([SB([128, 512]) for _ in range(2)])
            qT_blk = Ring([SB([128, 4, 512], BF16) for _ in range(1)])
            kT_blk = Ring([SB([128, 512], BF16) for _ in range(1)])
            QtT_blk = Ring([SB([64, 2, 4, 512], BF16) for _ in range(1)])
            oiT_blk = Ring([SB([64, 4, 512]) for _ in range(1)])
            W2 = lambda shape, dt=F32: Ring([SB(shape, dt) for _ in range(2)])
            W1 = lambda shape, dt=F32: Ring([SB(shape, dt)])
            sq_r, ssq_r, qn_r, t1_r, t2_r = W1([128, 640]), W2([128, 16]), W1([128, 640]), W1([128, 640]), W1([128, 640])
            qr_r, vb_r = W2([128, 640], BF16), W2([128, 128], BF16)
            sgf_r, sgn_r, fg_r, logf_r, kk_r = W1([128, 512]), W1([128, 512]), W1([128, 512]), W1([128, 512]), W1([128, 512])
            eb_r, enb_r = W1([128, 512]), W1([128, 512])
            Qt_r, Kt_r, Vb_r = W2([128, 2, 256], BF16), W2([128, 512], BF16), W2([128, 256], BF16)
            QtT_r, KtT_r = W2([64, 8, 128], BF16), W2([64, 8, 128], BF16)
            AT_r = [W2([128, 4, 128], BF16), W2([128, 4, 128], BF16)]
            TM_GROUPS = [(0, 512), (512, 256), (768, 512), (1280, 512)]
            gqk = SB([128, 2, 10, 64])
            lbt = SB([128, 2, 512])
            oml = SB([128, 2, 512])
            qkb = SB([128, 2, 128])
            P.dma("sp", qkb.t[:, :, :], qkn_d.ap().rearrange("l o c -> o l c").partition_broadcast(128), W_=[qkb.r])
            CP("dve", gqk.t[:, l, 0:8, :], bc(qkb.t[:, l, 0:64], [128, 8, 64], 1), [qkb.r], [gqk.r])
            CP("dve", gqk.t[:, l, 8:10, :], bc(qkb.t[:, l, 64:128], [128, 2, 64], 1), [qkb.r], [gqk.r])
            lbb = SB([128, 1024])
            P.dma("sp", lbb.t[:, :], lbp_d.ap().partition_broadcast(128), W_=[lbb.r])
            lbv = lbb.t[:, :].rearrange("p (d l c) -> p d l c", d=2, l=2)
            dlt = SB([128, 2, 256])
            TT("dve", dlt.t[:, :, :], lbv[:, :, 1, :], lbv[:, :, 0, :], ALU.subtract, [lbb.r], [dlt.r])
            MEMSET("pool", lbt.t[:, 0, :], 0.0, [lbt.r])
            ACT(lbt.t[:, 1, :], dlt.t[:, :, :].rearrange("p d c -> p (d c)"), AF.Sigmoid, [dlt.r], [lbt.r])
            TS("dve", oml.t[:, :, :], lbt.t[:, :, :], -1.0, 1.0, ALU.mult, ALU.add, [lbt.r], [oml.r])
            ropeC = SB([128, NSUB, 64])
            ropeS = SB([128, NSUB, 64])
            P.dma("sp", ropeC.t[:, :, :], ropeC_d.ap(), W_=[ropeC.r])
            P.dma("sp", ropeS.t[:, :, :], ropeS_d.ap(), W_=[ropeS.r])

            for b in range(NBLK):
                hT = load_norm_T(lnt, x_cur, b, src_res=(x2r if l > 0 else None))
                zb = z[0]
                for gi, (c0, ncol) in enumerate(TM_GROUPS):
                    wq = wq_ring.next()
                    P.dma("sp", wq.t[:, :, 0:ncol], winl[:, :, c0:c0 + ncol], R_=[wres], W_=[wq.r])
                    for s in range(4):
                        pz = banks.next()
                        MM([(pz.t[:, 0:ncol], hT.t[:, c, s * 128:(s + 1) * 128], wq.t[:, c, 0:ncol], c == 0, c == 7)
                            for c in range(8)], [hT.r, wq.r], [pz.r])
                        CP(("act", "dve")[s % 2], zb[s].t[:, c0:c0 + ncol], pz.t[:, 0:ncol], [pz.r], [zb[s].r])
                wf = wf_ring.next()
                P.dma("sp", wf.t[:, :, :], winl[:, :, C_GH:INC], R_=[wres], W_=[wf.r])
                gTb = gT_blk.next()
                for h in range(4):
                    pz = banks.next()
                    MM([(pz.t[0:64, :], wf.t[:, c, h * 64:(h + 1) * 64], hT.t[:, c, :], c == 0, c == 7) for c in range(8)],
                       [hT.r, wf.r], [pz.r])
                    ACT(gTb.t[:, h, :], pz.t[0:64, :], AF.Silu, [pz.r], [gTb.r])
                P.dma("sp", gT_d.ap()[:, :, b * 512:(b + 1) * 512], gTb.t[:, :, :], R_=[gTb.r], W_=[scr])
                uTb = uT_blk.next()
                for c2 in range(2):
                    pa = banks.next()
                    pg = banks.next()
                    MM([(pa.t[:, :], wf.t[:, c, 256 + c2 * 128:256 + (c2 + 1) * 128], hT.t[:, c, :], c == 0, c == 7)
                        for c in range(8)], [hT.r, wf.r], [pa.r])
                    MM([(pg.t[:, :], wf.t[:, c, 512 + c2 * 128:512 + (c2 + 1) * 128], hT.t[:, c, :], c == 0, c == 7)
                        for c in range(8)], [hT.r, wf.r], [pg.r])
                    sg = sg_ring.next()
                    ACT(sg.t[:, :], pg.t[:, :], AF.Sigmoid, [pg.r], [sg.r])
                    TT("dve", uTb.t[:, c2, :], pa.t[:, :], sg.t[:, :], ALU.mult, [pa.r, sg.r], [uTb.r])
                P.dma("sp", uText_d.ap()[:, :, HALO + b * 512:HALO + (b + 1) * 512], uTb.t[:, :, :], R_=[uTb.r], W_=[uTr])

                qTb = qT_blk.next()
                kTb = kT_blk.next()
                QtTb = QtT_blk.next()
                oiTb = oiT_blk.next()
                for s in range(4):
                    zs = zb[s]
                    zt = zs.t
                    ts = b * 4 + s
                    tok0 = ts * 128
                    qk = zt[:, 0:640]
                    qk3 = qk.rearrange("p (h d) -> p h d", d=64)
                    sq, ssq, qn, t1, t2, qr, vb = sq_r.next(), ssq_r.next(), qn_r.next(), t1_r.next(), t2_r.next(), qr_r.next(), vb_r.next()
                    TT("dve", sq.t[:, :], qk, qk, ALU.mult, [zs.r], [sq.r])
                    P.op("dve", lambda e, o=ssq.t[:, 0:10], i=sq.t[:, :].rearrange("p (h d) -> p h d", d=64):
                         e.tensor_reduce(o, i, AX.X, ALU.add), [sq.r], [ssq.r])
                    RSQRT(ssq.t[:, 0:10], ssq.t[:, 0:10], 1.0, 64 * EPS, [ssq.r], [ssq.r])
                    qn3 = qn.t[:, :].rearrange("p (h d) -> p h d", d=64)
                    TT("dve", qn3, qk3, bc(ssq.t[:, 0:10], [128, 10, 64], 2), ALU.mult, [zs.r, ssq.r], [qn.r])
                    TT("dve", qn3, qn3, gqk.t[:, l, :, :], ALU.mult, [qn.r, gqk.r], [qn.r])
                    t13 = t1.t[:, :].rearrange("p (h d) -> p h d", d=64)
                    t23 = t2.t[:, :].rearrange("p (h d) -> p h d", d=64)
                    TT("dve", t13, qn3, bc(ropeC.t[:, ts, :], [128, 10, 64], 1), ALU.mult, [qn.r, ropeC.r], [t1.r])
                    for a in range(2):
                        for hf in range(2):
                            o0 = a * 32 + hf * 16
                            i0 = a * 32 + (1 - hf) * 16
                            TT("dve", t23[:, :, o0:o0 + 16], qn3[:, :, i0:i0 + 16],
                               bc(ropeS.t[:, ts, o0:o0 + 16], [128, 10, 16], 1), ALU.mult, [qn.r, ropeS.r], [t2.r])
                    TT("dve", qr.t[:, 0:512].rearrange("p (j g d) -> p g j d", j=4, g=2),
                       t1.t[:, 0:512].rearrange("p (g j d) -> p g j d", g=2, j=4),
                       t2.t[:, 0:512].rearrange("p (g j d) -> p g j d", g=2, j=4), ALU.add, [t1.r, t2.r], [qr.r])
                    TT("dve", qr.t[:, 512:640], t1.t[:, 512:640], t2.t[:, 512:640], ALU.add, [t1.r, t2.r], [qr.r])
                    pt = banks.next()
                    ptb = pt.t[:, :].bitcast(BF16)
                    TR([(ptb[:, j * 128:(j + 1) * 128], qr.t[:, j * 128:(j + 1) * 128], ident_bf) for j in range(4)]
                       + [(ptb[:, 512:640], qr.t[:, 512:640], ident_bf)], [qr.r, cbf.r], [pt.r])
                    CP("act", qTb.t[:, :, s * 128:(s + 1) * 128], ptb[:, 0:512].rearrange("p (j t) -> p j t", j=4), [pt.r], [qTb.r])
                    CP("dve", kTb.t[:, s * 128:(s + 1) * 128], ptb[:, 512:640], [pt.r], [kTb.r])
                    CP("dve", vb.t[:, :], zt[:, C_V:C_V + 128], [zs.r], [vb.r])
                    P.dma("sp", ag1_in[b].ap()[128:256, :].rearrange("a (b c) -> (a b) c", c=128)[s * 128:(s + 1) * 128, :], vb.t[:, :],
                          R_=[vb.r], W_=[ag1r])
                    sgf, sgn, fg, logf, kk = sgf_r.next(), sgn_r.next(), fg_r.next(), logf_r.next(), kk_r.next()
                    eb, enb, Qt, Kt, Vb = eb_r.next(), enb_r.next(), Qt_r.next(), Kt_r.next(), Vb_r.next()
                    ff = zt[:, C_FF:C_FF + 512]
                    ACT(sgf.t[:, :], ff, AF.Sigmoid, [zs.r], [sgf.r])
                    ACT(sgn.t[:, :], ff, AF.Sigmoid, [zs.r], [sgn.r], scale=-1.0)
                    TT("dve", fg.t[:, :], sgf.t[:, :], oml.t[:, l, :], ALU.mult, [sgf.r, oml.r], [fg.r])
                    TT("dve", fg.t[:, :], fg.t[:, :], lbt.t[:, l, :], ALU.add, [fg.r, lbt.r], [fg.r])
                    ACT(logf.t[:, :], fg.t[:, :], AF.Ln, [fg.r], [logf.r])
                    TT("dve", kk.t[:, :], sgn.t[:, :], oml.t[:, l, :], ALU.mult, [sgn.r, oml.r], [kk.r])
                    pc = banks.next()
                    MM([(pc.t[:, 0:256], Mf[0], logf.t[:, 0:256], True, True),
                        (pc.t[:, 256:512], Mf[1], logf.t[:, 256:512], True, True)], [consts.r, logf.r], [pc.r])
                    ACT(eb.t[:, :], pc.t[:, :], AF.Exp, [pc.r], [eb.r])
                    ACT(enb.t[:, :], pc.t[:, :], AF.Exp, [pc.r], [enb.r], scale=-1.0)
                    TT("dve", Qt.t[:, :, :], bc(zt[:, C_QH:C_QH + 256], [128, 2, 256], 1),
                       eb.t[:, :].rearrange("p (d c) -> p d c", d=2), ALU.mult, [zs.r, eb.r], [Qt.r])
                    TT("dve", Kt.t[:, :], kk.t[:, :], enb.t[:, :], ALU.mult, [kk.r, enb.r], [Kt.r])
                    CP("dve", Vb.t[:, :], zt[:, C_IH:C_IH + 256], [zs.r], [Vb.r])
                    P.dma("sp", kt_d.ap()[tok0:tok0 + 128, :], Kt.t[:, :], R_=[Kt.r], W_=[scr])
                    P.dma("sp", vh_d.ap()[tok0:tok0 + 128, :], Vb.t[:, :], R_=[Vb.r], W_=[scr])
                    pd = banks.next()
                    MM([(pd.t[0:64, (dr * 4 + h) * 4:(dr * 4 + h) * 4 + 4], logf.t[:, dr * 256 + h * 64:dr * 256 + (h + 1) * 64], sel_f, True, True)
                        for dr in range(2) for h in range(4)], [logf.r, consts.r], [pd.r])
                    ACT(d_all.t[:, :, :, ts * 4:ts * 4 + 4].rearrange("p d h n -> p (d h) n"),
                        pd.t[0:64, 0:32].rearrange("p (a n) -> p a n", n=4), AF.Exp, [pd.r], [d_all.r])
                    pq = banks.next()
                    pqb = pq.t[:, :].bitcast(BF16)
                    TR([(pqb[0:64, (dr * 4 + h) * 128:(dr * 4 + h + 1) * 128], Qt.t[:, dr, h * 64:(h + 1) * 64], ident_bf)
                        for dr in range(2) for h in range(4)], [Qt.r, cbf.r], [pq.r])
                    QtT, KtT = QtT_r.next(), KtT_r.next()
                    CP("dve", QtT.t[:, :, :], pqb[0:64, :].rearrange("p (a t) -> p a t", a=8), [pq.r], [QtT.r])
                    CP("act", QtTb.t[:, :, :, s * 128:(s + 1) * 128].rearrange("p d h t -> p (d h) t"),
                       pqb[0:64, :].rearrange("p (a t) -> p a t", a=8), [pq.r], [QtTb.r])
                    pk = banks.next()
                    pkb = pk.t[:, :].bitcast(BF16)
                    TR([(pkb[0:64, (dr * 4 + h) * 128:(dr * 4 + h + 1) * 128], Kt.t[:, dr * 256 + h * 64:dr * 256 + (h + 1) * 64], ident_bf)
                        for dr in range(2) for h in range(4)], [Kt.r, cbf.r], [pk.r])
                    CP("dve", KtT.t[:, :, :], pkb[0:64, :].rearrange("p (a t) -> p a t", a=8), [pk.r], [KtT.r])
                    ATs = []
                    for dr in range(2):
                        pa = banks.next()
                        MM([(pa.t[:, h * 128:(h + 1) * 128], KtT.t[:, dr * 4 + h, :], QtT.t[:, dr * 4 + h, :], True, True) for h in range(4)],
                           [KtT.r, QtT.r], [pa.r])
                        AT = AT_r[dr].next()
                        TT("dve", AT.t[:, :, :], pa.t[:, :].rearrange("p (h t) -> p h t", h=4), bc(Mf[dr], [128, 4, 128], 1), ALU.mult,
                           [pa.r, consts.r], [AT.r])
                        ATs.append(AT)
                    po = banks.next()
                    MM([(po.t[0:64, h * 128:(h + 1) * 128], Vb.t[:, h * 64:(h + 1) * 64], ATs[dr].t[:, h, :], dr == 0, dr == 1)
                        for h in range(4) for dr in range(2)], [Vb.r, ATs[0].r, ATs[1].r], [po.r])
                    CP("act", oiTb.t[:, :, s * 128:(s + 1) * 128], po.t[0:64, :].rearrange("p (h t) -> p h t", h=4), [po.r], [oiTb.r])
                P.dma("sp", qT_d.ap()[:, :, b * 512:(b + 1) * 512], qTb.t[:, :, :], R_=[qTb.r], W_=[scr])
                P.dma("sp", ag1_in[b].ap()[0:128, :], kTb.t[:, :], R_=[kTb.r], W_=[ag1r])
                P.dma("sp", qtT_d.ap()[:, :, :, b * 512:(b + 1) * 512], QtTb.t[:, :, :, :], R_=[QtTb.r], W_=[scr])
                P.dma("sp", oiT_d.ap()[:, :, b * 512:(b + 1) * 512], oiTb.t[:, :, :], R_=[oiTb.r], W_=[scr])
            P.barrier()
            P.flush()
        if "stop_s1" in dbg:
            break

        ofr = [Res("of_d%d" % i) for i in range(NSUB)]
        ag1o = Res("ag1_out")
        ag2o = Res("ag2_out")
        mixr = Res("mix_d")

        ringcache = {}

        def hgrn_gen(final, ph, getbank, deep):
            if True:
                SB = lambda shape, dt=F32: alloc(ph, "sb", shape, dt)
                nb = 4 if deep else 2
                if id(ph) not in ringcache:
                    ringcache[id(ph)] = (
                        Ring([SB([128, 256], BF16) for _ in range(nb + 2)]),
                        Ring([SB([128, 256], BF16) for _ in range(nb + 2)]),
                        Ring([SB([128, 4, 4, 64], BF16) for _ in range(nb)]),
                        Ring([SB([64, 4, 4, 64]) for _ in range(nb)]),
                        Ring([SB([64, 4, 4, 64], BF16) for _ in range(nb)]),
                        Ring([SB([64, 4, 128], BF16) for _ in range(nb)]),
                        Ring([SB([64, 4, 128]) for _ in range(nb)]),
                        Ring([SB([64, 512]) for _ in range(nb // 2)]),
                        Ring([SB([64, 512]) for _ in range(nb // 2)]),
                        Ring([SB([64, 512]) for _ in range(nb // 2)]),
                        Ring([SB([64, 4, 128], BF16) for _ in range(2)]),
                        Ring([SB([64, 4, 128], BF16) for _ in range(2)]),
                        [SB([64, 4, 64]) for _ in range(2)],
                        SB([64, 8]))
                (Kl_ring, Vl_ring, Vx_ring, Us_ring, Sp_ring, Ql_ring, oi_ring, o_ring, sq_ring, rs_ring,
                 gl_ring, mx_ring, Salt, dc) = ringcache[id(ph)]
                if not final:
                    zt_ = SB([128, 640])
                    MEMSET("pool", zt_.t[:, :], 0.0, [zt_.r])
                    P.dma("sp", ag2_in.ap(), zt_.t[:, :], R_=[zt_.r], W_=[ag2r])
                if final:
                    Gt = SB([64, R, 520])
                    Gh = SB([128, R, 60])
                    Pst = SB([64, 4, 64])
                    hp = SB([128, 2, 15])
                    hn = SB([128, 2, 15])
                    g2 = ag2_out[l].ap().rearrange("(r p) c -> p r c", p=128)
                    P.dma("sp", Gt.t[:, :, :], g2[0:64, :, 0:520], R_=[ag2o], W_=[Gt.r])
                    P.dma("sp", Gh.t[:, :, :], g2[:, :, 520:580], R_=[ag2o], W_=[Gh.r])
                    for dr in range(2):
                        MEMSET("pool", Pst.t[:, :, :], 0.0, [Pst.r])
                        MEMSET("pool", Sin[dr].t[:, :, :], 0.0, [Sin[dr].r])
                        for r in (range(R) if dr == 0 else range(R - 1, -1, -1)):
                            STT("dve", Sin[dr].t[:, :, :], Pst.t[:, :, :], oh.t[0:64, r:r + 1], Sin[dr].t[:, :, :], ALU.mult, ALU.add,
                                [Pst.r, oh.r, Sin[dr].r], [Sin[dr].r])
                            Tr = Gt.t[:, r, dr * 260:dr * 260 + 256].rearrange("p (h v) -> p h v", h=4)
                            Dr = Gt.t[:, r, dr * 260 + 256:dr * 260 + 260]
                            TT("dve", Pst.t[:, :, :], Pst.t[:, :, :], bc(Dr, [64, 4, 64], 2), ALU.mult, [Pst.r, Gt.r], [Pst.r])
                            TT("dve", Pst.t[:, :, :], Pst.t[:, :, :], Tr, ALU.add, [Pst.r, Gt.r], [Pst.r])
                    Gh5 = Gh.t[:, :, :].rearrange("p r (c e k) -> p r c e k", c=2, e=2)
                    for (hb_, e_, o0) in ((hp, 1, 4), (hn, 0, 8)):
                        TS("dve", hb_.t[:, :, :], Gh5[:, 0, :, e_, :], oh.t[:, o0:o0 + 1], None, ALU.mult, None, [Gh.r, oh.r], [hb_.r])
                        for r in range(1, R):
                            STT("dve", hb_.t[:, :, :], Gh5[:, r, :, e_, :], oh.t[:, o0 + r:o0 + r + 1], hb_.t[:, :, :], ALU.mult, ALU.add,
                                [Gh.r, oh.r, hb_.r], [hb_.r])
                    P.dma("sp", uText_d.ap()[:, :, 0:HALO], hp.t[:, :, :], R_=[hp.r], W_=[uTr])
                    P.dma("sp", uText_d.ap()[:, :, HALO + NT:2 * HALO + NT], hn.t[:, :, :], R_=[hn.r], W_=[uTr])

                states = {}
                for dr in range(2):
                    cur, nxt = Sst[dr], Salt[dr]
                    if final:
                        CP("dve", cur.t[:, :, :], Sin[dr].t[:, :, :], [Sin[dr].r], [cur.r])
                    else:
                        MEMSET("pool", cur.t[:, :, :], 0.0, [cur.r])
                    states[dr] = (cur, nxt)
                for i in range(NSUB):
                    for dr in range(2):
                        ts = i if dr == 0 else NSUB - 1 - i
                        first = i < NSUB // 2
                        cur, nxt = states[dr]
                        tok0 = ts * 128
                        Kl, Vl, Vx, Usb = Kl_ring.next(), Vl_ring.next(), Vx_ring.next(), Us_ring.next()
                        P.dma("sp", Kl.t[:, :], kt_d.ap()[tok0:tok0 + 128, dr * 256:(dr + 1) * 256], R_=[scr], W_=[Kl.r])
                        P.dma("sp", Vl.t[:, :], vh_d.ap()[tok0:tok0 + 128, :], R_=[scr], W_=[Vl.r])
                        TT("dve", Vx.t[:, :, :, :],
                           Vl.t[:, :].rearrange("p (h v) -> p h v", h=4).unsqueeze(2).to_broadcast([128, 4, 4, 64]),
                           sel_bf.unsqueeze(1).unsqueeze(3).to_broadcast([128, 4, 4, 64]), ALU.mult, [Vl.r, cbf.r], [Vx.r])
                        if final:
                            Ql = Ql_ring.next()
                            P.dma("sp", Ql.t[:, :, :], qtT_d.ap()[:, dr, :, tok0:tok0 + 128], R_=[scr], W_=[Ql.r])
                        yield
                        for hp2 in range(2):
                            pu = getbank()
                            MM([(pu.t[0:64, hh * 256:(hh + 1) * 256], Kl.t[:, (hp2 * 2 + hh) * 64:(hp2 * 2 + hh + 1) * 64],
                                 Vx.t[:, hp2 * 2 + hh, :, :].rearrange("p n v -> p (n v)"), True, True) for hh in range(2)],
                               [Kl.r, Vx.r], [pu.r])
                            CP("dve", Usb.t[:, hp2 * 2:hp2 * 2 + 2, :, :].rearrange("p h n v -> p h (n v)"),
                               pu.t[0:64, :].rearrange("p (h x) -> p h x", h=2), [pu.r], [Usb.r])
                        Sp = Sp_ring.next() if final else None
                        for n in (range(4) if dr == 0 else range(3, -1, -1)):
                            ch = ts * 4 + n
                            if final:
                                CP("dve", Sp.t[:, n, :, :], cur.t[:, :, :], [cur.r], [Sp.r])
                            TT("dve", nxt.t[:, :, :], cur.t[:, :, :], Usb.t[:, :, n, :], ALU.add, [cur.r, Usb.r], [nxt.r])
                            TT("dve", nxt.t[:, :, :], nxt.t[:, :, :], bc(d_all.t[:, dr, :, ch], [64, 4, 64], 2), ALU.mult,
                               [nxt.r, d_all.r], [nxt.r])
                            cur, nxt = nxt, cur
                        states[dr] = (cur, nxt)
                        yield
                        if not final:
                            continue
                        po = getbank()
                        MM([(po.t[0:64, h * 128 + n * 32:h * 128 + (n + 1) * 32], Sp.t[:, n, h, :], Ql.t[:, h, n * 32:(n + 1) * 32], True, True)
                            for n in range(4) for h in range(4)], [Sp.r, Ql.r], [po.r])
                        po3 = po.t[0:64, :].rearrange("p (h t) -> p h t", h=4)
                        oi = oi_ring.next()
                        if first:
                            P.dma("sp", oi.t[:, :, :], oiT_d.ap()[:, :, tok0:tok0 + 128], R_=[scr], W_=[oi.r])
                            TT("dve", oi.t[:, :, :], po3, oi.t[:, :, :], ALU.add, [po.r, oi.r], [oi.r])
                            P.dma("sp", of_d.ap()[:, :, tok0:tok0 + 128], oi.t[:, :, :], R_=[oi.r], W_=[ofr[ts]])
                        else:
                            P.dma("sp", oi.t[:, :, :], of_d.ap()[:, :, tok0:tok0 + 128], R_=[ofr[ts]], W_=[oi.r])
                            o, sq, rs, gl, mx = o_ring.next(), sq_ring.next(), rs_ring.next(), gl_ring.next(), mx_ring.next()
                            o3 = o.t[:, :].rearrange("p (h t) -> p h t", h=4)
                            TT("dve", o3, po3, oi.t[:, :, :], ALU.add, [po.r, oi.r], [o.r])
                            ACT(sq.t[:, :], o.t[:, :], AF.Square, [o.r], [sq.r])
                            pn = getbank()
                            MM([(pn.t[0:64, h * 128:(h + 1) * 128], ones64, sq.t[:, h * 128:(h + 1) * 128], True, True) for h in range(4)],
                               [consts.r, sq.r], [pn.r])
                            RSQRT(rs.t[:, :], pn.t[0:64, :], 1.0 / 64, EPS, [pn.r], [rs.r])
                            TT("dve", o.t[:, :], o.t[:, :], rs.t[:, :], ALU.mult, [o.r, rs.r], [o.r])
                            TT("dve", o3, o3, bc(pp.t[0:64, l, PP_GN:PP_GN + 4], [64, 4, 128], 2), ALU.mult, [o.r, pp.r], [o.r])
                            P.dma("sp", gl.t[:, :, :], gT_d.ap()[:, :, tok0:tok0 + 128], R_=[scr], W_=[gl.r])
                            TT("dve", mx.t[:, :, :], o3, gl.t[:, :, :], ALU.mult, [o.r, gl.r], [mx.r])
                            P.dma("sp", mixhg_d.ap()[:, :, tok0:tok0 + 128], mx.t[:, :, :], R_=[mx.r], W_=[mixr])
                        yield
                if not final:
                    for dr in range(2):
                        cur = states[dr][0]
                        P.dma("sp", ag2_in.ap()[0:64, dr * 260:dr * 260 + 256], cur.t[:, :, :].rearrange("p h v -> p (h v)"),
                              R_=[cur.r], W_=[ag2r])
                        P.op("dve", lambda e, o_=dc.t[:, dr * 4:dr * 4 + 4], i_=d_all.t[:, dr, :, :]: e.tensor_reduce(o_, i_, AX.X, ALU.mult),
                             [d_all.r], [dc.r])
                        P.dma("sp", ag2_in.ap()[0:64, dr * 260 + 256:dr * 260 + 260], dc.t[:, dr * 4:dr * 4 + 4], R_=[dc.r], W_=[ag2r])
                if not final:
                    a2 = ag2_in.ap()[:, 520:580].rearrange("p (c e k) -> p c e k", c=2, e=2)
                    P.dma("sp", a2[:, :, 0, :], uText_d.ap()[:, :, HALO:2 * HALO], R_=[uTr], W_=[ag2r])
                    P.dma("sp", a2[:, :, 1, :], uText_d.ap()[:, :, NT:NT + HALO], R_=[uTr], W_=[ag2r])

        for b in range(NBLK):
            P.allgather(ag1_in[b], ag1_out[l][b], R_=[ag1r], W_=[ag1o])

        def conv_gen(ph, getbank):
            SB = lambda shape, dt=F32: alloc(ph, "sb", shape, dt)
            ub_ring = Ring([SB([128, 2, 512 + 2 * HALO]) for _ in range(1)])
            acc_ring = Ring([SB([128, 2, 512]) for _ in range(1)])
            yc_ring = Ring([SB([128, 2, 512]) for _ in range(1)])
            rs_ring = Ring([SB([128, 512]) for _ in range(1)])
            mc_ring = Ring([SB([128, 2, 512], BF16) for _ in range(1)])
            cw = pp.t[:, l, PP_CW:PP_CW + 62].rearrange("p (c k) -> p c k", c=2)
            for b in range(NBLK):
                ub, acc, yc, rs, mc = ub_ring.next(), acc_ring.next(), yc_ring.next(), rs_ring.next(), mc_ring.next()
                sq = acc
                P.dma("sp", ub.t[:, :, :], uText_d.ap()[:, :, b * 512:b * 512 + 512 + 2 * HALO], R_=[uTr], W_=[ub.r])
                for c in range(2):
                    TS("dve", acc.t[:, c, :], ub.t[:, c, 0:512], cw[:, c, 0:1], pp.t[:, l, PP_CB + c:PP_CB + c + 1], ALU.mult, ALU.add,
                       [ub.r, pp.r], [acc.r])
                    for k in range(1, KW):
                        STT("dve", acc.t[:, c, :], ub.t[:, c, k:k + 512], cw[:, c, k:k + 1], acc.t[:, c, :], ALU.mult, ALU.add,
                            [ub.r, pp.r, acc.r], [acc.r])
                    yield
                pm = getbank()
                MM([(pm.t[:, :], onesq, acc.t[:, c, :], c == 0, c == 1) for c in range(2)], [consts.r, acc.r], [pm.r])
                TT("dve", yc.t[:, :, :], acc.t[:, :, :], bc(pm.t[:, :], [128, 2, 512], 1), ALU.subtract, [acc.r, pm.r], [yc.r])
                ACT(sq.t[:, :, :], yc.t[:, :, :], AF.Square, [yc.r], [sq.r])
                yield
                pv = getbank()
                MM([(pv.t[:, :], onesq, sq.t[:, c, :], c == 0, c == 1) for c in range(2)], [consts.r, sq.r], [pv.r])
                RSQRT(rs.t[:, :], pv.t[:, :], 1.0, EPS, [pv.r], [rs.r])
                TT("dve", yc.t[:, :, :], yc.t[:, :, :], bc(rs.t[:, :], [128, 2, 512], 1), ALU.mult, [yc.r, rs.r], [yc.r])
                for c in range(2):
                    ACT(mc.t[:, c, :], yc.t[:, c, :], AF.Silu, [yc.r, pp.r], [mc.r],
                        scale=pp.t[:, l, PP_LNG + c:PP_LNG + c + 1], bias=pp.t[:, l, PP_LNB + c:PP_LNB + c + 1])
                P.dma("sp", mixcv_d.ap()[:, :, b * 512:(b + 1) * 512], mc.t[:, :, :], R_=[mc.r], W_=[mixr])
                yield

        with ExitStack() as ph:
            SB = lambda shape, dt=F32: alloc(ph, "sb", shape, dt)
            st_ring = Ring([alloc(ph, "ps", [128, 1024], F32) for _ in range(3)])
            obank = [alloc(ph, "ps", [128, 512], F32) for _ in range(2)]
            tbank = st_ring

            class View:
                def __init__(self, t, r):
                    self.t = t
                    self.r = r

            pending = set()

            def take_slot():
                for _ in range(len(st_ring.bufs)):
                    sl_ = st_ring.next()
                    if id(sl_) not in pending:
                        return sl_
                raise RuntimeError("no free score slot")

            def halfbank():
                sl_ = take_slot()
                return View(sl_.t[:, 0:512], sl_.r)

            kT_all = SB([128, L], BF16)
            v_all = SB([128, T, 2, 128], BF16)
            QT_ring = Ring([SB([128, 4, 512], BF16) for _ in range(2)])
            pT_ring = Ring([SB([128, 1024], BF16) for _ in range(3)])
            osb_ring = Ring([SB([128, 512]) for _ in range(2)])
            rec_ring = Ring([SB([64, 512]) for _ in range(1)])
            mixA_ring = Ring([SB([64, 8, 512], BF16) for _ in range(1)])
            MEMSET("pool", v_all.t[:, :, :, 64:128], 1.0, [v_all.r])
            for r in range(R):
                for b in range(NBLK):
                    go = ag1_out[l][b].ap()
                    P.dma("sp", kT_all.t[:, r * NT + b * 512:r * NT + (b + 1) * 512], go[r * 256:r * 256 + 128, :], R_=[ag1o], W_=[kT_all.r])
                    vsrc = go[r * 256 + 128:r * 256 + 256, :].rearrange("a (b c) -> (a b) c", c=128)
                    vsrc = vsrc.rearrange("(t p) (g d) -> p t g d", p=128, g=2)
                    t0 = r * NSUB + b * 4
                    for g in range(2):
                        P.dma("sp", v_all.t[:, t0:t0 + 4, g, 0:64], vsrc[:, :, g, :], R_=[ag1o], W_=[v_all.r])
            flags = {"conv_ok": False}

            def side_chain():
                yield from hgrn_gen(False, ph, halfbank, False)
                P.allgather(ag2_in, ag2_out[l], R_=[ag2r], W_=[ag2o])
                yield
                for _ in hgrn_gen(True, ph, halfbank, False):
                    flags["conv_ok"] = True
                    yield

            hgen = side_chain()
            cgen = conv_gen(ph, halfbank)
            side = {"h": True, "c": True}

            def step(which):
                g_ = hgen if which == "h" else cgen
                if which == "c" and not flags["conv_ok"]:
                    return
                if side[which]:
                    try:
                        next(g_)
                    except StopIteration:
                        side[which] = False

            nsteps = NBLK * 4 * T
            h_every = max(1, nsteps // (11 * NSUB + 16))
            c_every = max(1, nsteps // (8 * NBLK + 4))
            for _ in range(3 * NSUB // 2):
                step("h")
            k = 0
            for b in range(NBLK):
                QT = QT_ring.next()
                P.dma("sp", QT.t[:, :, :], qT_d.ap()[:, :, b * 512:(b + 1) * 512], R_=[scr], W_=[QT.r])
                mixA = mixA_ring.next()
                for j in range(4):
                    def S_mm(t, j=j, QT=QT):
                        st = take_slot()
                        pending.add(id(st))
                        MM([(st.t[:, 0:512], kT_all.t[0:64, t * 128:(t + 1) * 128], QT.t[0:64, j, :], True, True),
                            (st.t[:, 512:1024], kT_all.t[64:128, t * 128:(t + 1) * 128], QT.t[64:128, j, :], True, True)],
                           [kT_all.r, QT.r], [st.r])
                        return st
                    pend = [S_mm(0)]
                    if T > 1:
                        pend.append(S_mm(1))
                    for t in range(T):
                        st = pend.pop(0)
                        pT = pT_ring.next()
                        ACT(pT.t[:, :], st.t[:, :], AF.Exp, [st.r], [pT.r], scale=8.0)
                        pending.discard(id(st))
                        if t + 2 < T:
                            pend.append(S_mm(t + 2))
                        MM([(obank[0].t[:, :], v_all.t[:, t, 0, :], pT.t[:, 0:512], t == 0, t == T - 1),
                            (obank[1].t[:, :], v_all.t[:, t, 1, :], pT.t[:, 512:1024], t == 0, t == T - 1)],
                           [v_all.r, pT.r], [obank[0].r, obank[1].r])
                        k += 1
                        if k % h_every == 0:
                            step("h")
                        if k % c_every == 0:
                            step("c")
                    for g in range(2):
                        osb, rec = osb_ring.next(), rec_ring.next()
                        CP("dve", osb.t[:, :], obank[g].t[:, :], [obank[g].r], [osb.r])
                        pr = take_slot()
                        MM([(pr.t[0:64, 0:512], shiftM, osb.t[:, :], True, True)], [consts.r, osb.r], [pr.r])
                        P.op("dve", lambda e, o_=rec.t[:, :], i_=pr.t[0:64, 0:512]: e.reciprocal(o_, i_), [pr.r], [rec.r])
                        TT("dve", mixA.t[:, g * 4 + j, :], osb.t[0:64, :], rec.t[:, :], ALU.mult, [osb.r, rec.r], [mixA.r])
                P.dma("sp", mixatt_d.ap()[:, :, b * 512:(b + 1) * 512], mixA.t[:, :, :], R_=[mixA.r], W_=[mixr])
            while side["h"]:
                step("h")
            while side["c"]:
                step("c")
            P.barrier()
            P.flush()
        if "stop_s6" in dbg:
            break

        with ExitStack() as ph:
            SB = lambda shape, dt=F32: alloc(ph, "sb", shape, dt)
            banks = Ring([alloc(ph, "ps", [128, 512], F32) for _ in range(4)])
            accb = [alloc(ph, "ps", [128, 512], F32) for _ in range(4)]
            junk = SB([128, D], BF16)
            st_ring = Ring([SB([128, 2]) for _ in range(8)])
            hb_ring = Ring([SB([128, D], BF16) for _ in range(4)])
            hT_ring = Ring([SB([128, 8, 512], BF16) for _ in range(2)])
            xb_ring = Ring([SB([128, 4, D]) for _ in range(2)])
            aT = SB([128, 32, 512], BF16)
            w1q_ring = Ring([SB([128, 8, 512], BF16) for _ in range(2)])
            w2q_ring = Ring([SB([128, 4, 512], BF16) for _ in range(3)])
            rl_ring = Ring([SB([128, 512]) for _ in range(2)])
            w1l = w1_bf.ap()[l].rearrange("(c p) n -> p c n", p=128)
            w2l = w2_bf.ap()[l].rearrange("(j p) n -> p j n", p=128)
            wo_att = SB([64, 8, D], BF16)
            wo_hg = SB([64, 4, D], BF16)
            wo_cv = SB([128, 2, D], BF16)
            ma_ring = Ring([SB([64, 8, 512], BF16) for _ in range(1)])
            mh_ring = Ring([SB([64, 4, 512], BF16) for _ in range(1)])
            mcv_ring = Ring([SB([128, 2, 512], BF16) for _ in range(1)])
            wol = wout_bf.ap()[l]
            P.dma("sp", wo_att.t[:, :, :], wol[0:512, :].rearrange("(h d) n -> d h n", d=64), R_=[wres], W_=[wo_att.r])
            P.dma("sp", wo_hg.t[:, :, :], wol[512:768, :].rearrange("(h d) n -> d h n", d=64), R_=[wres], W_=[wo_hg.r])
            P.dma("sp", wo_cv.t[:, :, :], wol[768:1024, :].rearrange("(c p) n -> p c n", p=128), R_=[wres], W_=[wo_cv.r])
            def wo_norm(b):
                xb = xb_ring.next()
                ma, mh, mcv = ma_ring.next(), mh_ring.next(), mcv_ring.next()
                P.dma("sp", ma.t[:, :, :], mixatt_d.ap()[:, :, b * 512:(b + 1) * 512], R_=[mixr], W_=[ma.r])
                P.dma("sp", mh.t[:, :, :], mixhg_d.ap()[:, :, b * 512:(b + 1) * 512], R_=[mixr], W_=[mh.r])
                P.dma("sp", mcv.t[:, :, :], mixcv_d.ap()[:, :, b * 512:(b + 1) * 512], R_=[mixr], W_=[mcv.r])
                for s in range(4):
                    P.dma("sp", xb.t[:, s, :], x_cur[b * 512 + s * 128:b * 512 + (s + 1) * 128, :], W_=[xb.r])
                hbs = []
                for s in range(4):
                    for n2 in range(2):
                        pw = banks.next()
                        sl = slice(s * 128, (s + 1) * 128)
                        nl = slice(n2 * 512, (n2 + 1) * 512)
                        grp = [(pw.t[:, :], ma.t[:, h, sl], wo_att.t[:, h, nl], h == 0, False) for h in range(8)]
                        grp += [(pw.t[:, :], mh.t[:, h, sl], wo_hg.t[:, h, nl], False, False) for h in range(4)]
                        grp += [(pw.t[:, :], mcv.t[:, c, sl], wo_cv.t[:, c, nl], False, c == 1) for c in range(2)]
                        MM(grp, [ma.r, mh.r, mcv.r, wo_att.r, wo_hg.r, wo_cv.r], [pw.r])
                        TT("dve", xb.t[:, s, nl], pw.t[:, :], xb.t[:, s, nl], ALU.add, [pw.r, xb.r], [xb.r])
                    st = st_ring.next()
                    ACT(junk.t[:, :], xb.t[:, s, :], AF.Square, [xb.r], [junk.r, st.r], accum_out=st.t[:, 0:1])
                    RSQRT(st.t[:, 1:2], st.t[:, 0:1], 1.0 / D, EPS, [st.r], [st.r])
                    hb = hb_ring.next()
                    TS("dve", hb.t[:, :], xb.t[:, s, :], st.t[:, 1:2], None, ALU.mult, None, [xb.r, st.r], [hb.r])
                    hbs.append(hb)
                return xb, hbs

            def transposes(hbs):
                hT = hT_ring.next()
                for s, hb in enumerate(hbs):
                    pt = banks.next()
                    ptb = pt.t[:, :].bitcast(BF16)
                    TR([(ptb[:, c * 128:(c + 1) * 128], hb.t[:, c * 128:(c + 1) * 128], ident_bf) for c in range(8)],
                       [hb.r, cbf.r], [pt.r])
                    CP("act", hT.t[:, :, s * 128:(s + 1) * 128], ptb.rearrange("p (c t) -> p c t", c=8), [pt.r], [hT.r])
                return hT

            nxt_blk = wo_norm(0)
            nxt_hT = transposes(nxt_blk[1])
            for b in range(NBLK):
                xb, hT = nxt_blk[0], nxt_hT
                if b + 1 < NBLK:
                    nxt_blk = wo_norm(b + 1)
                for fg in range(8):
                    w1q = w1q_ring.next()
                    P.dma("sp", w1q.t[:, :, :], w1l[:, :, fg * 512:(fg + 1) * 512], R_=[wres], W_=[w1q.r])
                    for jj in range(4):
                        pa = banks.next()
                        MM([(pa.t[:, :], w1q.t[:, c, jj * 128:(jj + 1) * 128], hT.t[:, c, :], c == 0, c == 7) for c in range(8)],
                           [w1q.r, hT.r], [pa.r])
                        rl = rl_ring.next()
                        ACT(rl.t[:, :], pa.t[:, :], AF.Relu, [pa.r], [rl.r])
                        TT("dve", aT.t[:, fg * 4 + jj, :], rl.t[:, :], rl.t[:, :], ALU.mult, [rl.r], [aT.r])
                if b + 1 < NBLK:
                    nxt_hT = transposes(nxt_blk[1])
                for n2 in range(2):
                    nl = slice(n2 * 512, (n2 + 1) * 512)
                    for pc in range(8):
                        w2q = w2q_ring.next()
                        P.dma("sp", w2q.t[:, :, :], w2l[:, pc * 4:(pc + 1) * 4, nl], R_=[wres], W_=[w2q.r])
                        for s in range(4):
                            MM([(accb[s].t[:, :], aT.t[:, pc * 4 + jj, s * 128:(s + 1) * 128], w2q.t[:, jj, :],
                                 pc == 0 and jj == 0, pc == 7 and jj == 3) for jj in range(4)], [aT.r, w2q.r], [accb[s].r])
                    for s in range(4):
                        TT("dve", xb.t[:, s, nl], accb[s].t[:, :], xb.t[:, s, nl], ALU.add, [accb[s].r, xb.r], [xb.r])
                for s in range(4):
                    P.dma("sp", x_next[b * 512 + s * 128:b * 512 + (s + 1) * 128, :], xb.t[:, s, :], R_=[xb.r], W_=[x2r])
            P.barrier()
            P.flush()

    P.barrier()
    P.flush()
    es.close()
    return nc


def _consts():
    c = np.zeros((128, K_END), np.float32)
    j = np.arange(128)[:, None]
    i = np.arange(128)[None, :]
    c[:, K_ID:K_ID + 128] = (j == i)
    same = (j // CH) == (i // CH)
    c[:, K_MFW:K_MFW + 128] = same & (j <= i)
    c[:, K_MBW:K_MBW + 128] = same & (j >= i)
    c[:, K_SEL:K_SEL + 4] = (j // CH) == np.arange(4)[None, :]
    c[64, K_SHIFT:K_SHIFT + 64] = 1.0
    c[:, K_ONES:K_ONES + 64] = 1.0
    c[:, K_ONESQ:K_ONESQ + 128] = 1.0 / 256.0
    return c


def _rope_tables(pos):
    inv = (10000.0 ** (-np.arange(0, 32, 2, dtype=np.float32) / 32.0)).astype(np.float32)
    row = (pos // 64).astype(np.float32)[:, None] * inv[None, :]
    col = (pos % 64).astype(np.float32)[:, None] * inv[None, :]
    cr, sr, cc, sc = np.cos(row), np.sin(row), np.cos(col), np.sin(col)
    C = np.concatenate([cr, cr, cc, cc], 1).astype(np.float32)
    S = np.concatenate([-sr, sr, -sc, sc], 1).astype(np.float32)
    return C, S


def prepare(inputs, NT):
    f = lambda a: np.ascontiguousarray(np.asarray(a, dtype=np.float32))
    x = f(inputs["x"])
    B = x.shape[0]
    pp = np.zeros((2, 128, PP_END), np.float32)
    for l in range(2):
        pp[l, :, PP_NM:PP_NM + 8] = f(inputs["norm_mix"])[l].reshape(8, 128).T
        pp[l, :, PP_NMLP:PP_NMLP + 8] = f(inputs["norm_mlp"])[l].reshape(8, 128).T
        pp[l, 0:64, PP_GN:PP_GN + 4] = f(inputs["hgrn_norm"])[l].reshape(4, 64).T
        pp[l, :, PP_CW:PP_CW + 62] = f(inputs["conv_w"])[l].T.reshape(2, 128, 31).transpose(1, 0, 2).reshape(128, 62)
        pp[l, :, PP_CB:PP_CB + 2] = f(inputs["conv_b"])[l].reshape(2, 128).T
        pp[l, :, PP_LNG:PP_LNG + 2] = f(inputs["conv_ln_g"])[l].reshape(2, 128).T
        pp[l, :, PP_LNB:PP_LNB + 2] = f(inputs["conv_ln_b"])[l].reshape(2, 128).T
    qkn = np.concatenate([f(inputs["q_norm"]), f(inputs["k_norm"])], 1).reshape(2, 1, 128)
    lbp = np.concatenate([f(inputs["hgrn_lb_fwd"]).reshape(-1), f(inputs["hgrn_lb_bwd"]).reshape(-1)]).reshape(1, 1024)
    consts = _consts()
    shared = {"w_in": f(inputs["w_in"]), "w_out": f(inputs["w_out"]), "w_mlp_in": f(inputs["w_mlp_in"]),
              "w_mlp_out": f(inputs["w_mlp_out"]), "pp": pp, "qkn": np.ascontiguousarray(qkn), "lbp": lbp, "consts": consts}
    maps = []
    for c in range(B * R):
        b, r = c // R, c % R
        pos = np.arange(r * NT, (r + 1) * NT)
        C, S = _rope_tables(pos)
        oh = np.zeros((1, 12), np.float32)
        oh[0, r] = 1.0
        if r > 0:
            oh[0, 4 + r - 1] = 1.0
        if r < R - 1:
            oh[0, 8 + r + 1] = 1.0
        m = dict(shared)
        C = np.ascontiguousarray(C.reshape(NT // 128, 128, 64).transpose(1, 0, 2))
        S = np.ascontiguousarray(S.reshape(NT // 128, 128, 64).transpose(1, 0, 2))
        m.update({"x": np.ascontiguousarray(x[b, r * NT:(r + 1) * NT]), "ropeC": C, "ropeS": S, "oh": oh})
        maps.append(m)
    return maps


_CACHE = {}


def kernel(**inputs):
    x = np.asarray(inputs["x"])
    B, Lfull, _ = x.shape
    NT = Lfull // R
    if NT not in _CACHE:
        _CACHE[NT] = build_program(NT)
    nc = _CACHE[NT]
    maps = prepare(inputs, NT)
    res = run_bass_kernel_spmd(nc, maps, core_ids=list(range(B * R)))
    out = np.empty((B, Lfull, D), np.float32)
    for c in range(B * R):
        out[c // R, (c % R) * NT:(c % R + 1) * NT] = np.asarray(res.results[c]["out"], dtype=np.float32)
    return out
```

```python
from contextlib import ExitStack
import numpy as np
import concourse.bass as bass
import concourse.mybir as mybir
from concourse.bass_utils import run_bass_kernel_spmd

F32 = mybir.dt.float32
BF16 = mybir.dt.bfloat16
AF = mybir.ActivationFunctionType
ALU = mybir.AluOpType
AX = mybir.AxisListType

D = 1024
DFF = 4096
INC = 2560
EPS = 1e-6
R = 4
NCORES = 8
CH = 32
HALO = 15
KW = 31


class Res:
    __slots__ = ("w", "r", "name", "excl")

    def __init__(self, name="", excl=False):
        self.w = None
        self.r = []
        self.name = name
        self.excl = excl


class Prog:
    ENG = ("pe", "act", "dve", "pool", "sp")

    def __init__(self, nc, es):
        self.nc = nc
        self.es = es
        self.q = {e: [] for e in self.ENG}
        self.sem = {e: es.enter_context(nc.semaphore("sem_" + e)) for e in ("pe", "act", "dve", "pool")}
        self.cnt = {e: 0 for e in self.sem}
        self.seen = {e: {} for e in self.ENG}
        self.dsem = {}
        self.dcnt = {}
        self.drr = {}
        for qn, n in (("sp", 12), ("pool", 6), ("act", 4)):
            self.dsem[qn] = [es.enter_context(nc.semaphore("dma_%s_%d" % (qn, i))) for i in range(n)]
            self.dcnt[qn] = [0] * n
            self.drr[qn] = 0
        self.cctoks = []

    def _waits(self, eng, R_, W_):
        need = {}

        def add(tok):
            if tok is None:
                return
            s, v = tok
            k = id(s)
            if k not in need or need[k][1] < v:
                need[k] = (s, v)

        for r in R_:
            add(r.w)
            if r.excl:
                for t in r.r:
                    add(t)
        for w in W_:
            add(w.w)
            for t in w.r:
                add(t)
        out = []
        own = self.sem.get(eng)
        for k, (s, v) in need.items():
            if eng == "pe" and s is own:
                continue
            if self.seen[eng].get(k, 0) >= v:
                continue
            self.seen[eng][k] = v
            out.append((s, v))
        return out

    def _commit(self, tok, R_, W_):
        for r in R_:
            r.r.append(tok)
        for w in W_:
            w.w = tok
            w.r = []

    def op(self, eng, fn, R_=(), W_=()):
        waits = self._waits(eng, R_, W_)
        self.cnt[eng] += 1
        tok = (self.sem[eng], self.cnt[eng])
        sem = self.sem[eng]

        def c(e):
            for s, v in waits:
                e.wait_ge(s, v)
            fn(e).then_inc(sem, 1)

        self.q[eng].append(c)
        self._commit(tok, R_, W_)
        return tok

    def dma(self, qn, out, in_, R_=(), W_=(), **kw):
        i = self.drr[qn]
        self.drr[qn] = (i + 1) % len(self.dsem[qn])
        s = self.dsem[qn][i]
        prev = self.dcnt[qn][i]
        waits = self._waits(qn, R_, W_)
        k = id(s)
        if prev > 0 and self.seen[qn].get(k, 0) < prev:
            self.seen[qn][k] = prev
            waits.append((s, prev))
        self.dcnt[qn][i] = prev + 16
        tok = (s, prev + 16)

        def c(e):
            for s_, v in waits:
                e.wait_ge(s_, v)
            e.dma_start(out=out, in_=in_, **kw).then_inc(s, 16)

        self.q[qn].append(c)
        self._commit(tok, R_, W_)
        return tok

    def allgather(self, in_t, out_t, R_=(), W_=()):
        waits = self._waits("pool", R_, W_)
        s = self.es.enter_context(self.nc.semaphore("cc_sem%d" % len(self.cctoks)))
        tok = (s, 1)
        self.cctoks.append(tok)
        groups = [[0, 1, 2, 3], [4, 5, 6, 7]]

        def c(e):
            for s_, v in waits:
                e.wait_ge(s_, v)
            e.collective_compute("AllGather", ALU.bypass, replica_groups=groups,
                                 ins=[in_t.ap().opt()], outs=[out_t.ap().opt()]).then_inc(s)

        self.q["pool"].append(c)
        self._commit(tok, R_, W_)
        return tok

    def barrier(self):
        toks = [(self.sem[e], self.cnt[e]) for e in self.sem if self.cnt[e] > 0]
        for qn in self.dsem:
            for s, v in zip(self.dsem[qn], self.dcnt[qn]):
                if v > 0:
                    toks.append((s, v))
        toks.extend(self.cctoks)
        for eng in self.ENG:
            wl = []
            for s, v in toks:
                if self.seen[eng].get(id(s), 0) >= v:
                    continue
                self.seen[eng][id(s)] = v
                wl.append((s, v))

            def c(e, wl=wl):
                for s_, v in wl:
                    e.wait_ge(s_, v)

            self.q[eng].append(c)

    def flush(self):
        nc = self.nc
        q = self.q
        with nc.Block() as block:
            @block.tensor
            def _(e):
                for c in q["pe"]:
                    c(e)

            @block.scalar
            def _(e):
                for c in q["act"]:
                    c(e)

            @block.vector
            def _(e):
                for c in q["dve"]:
                    c(e)

            @block.gpsimd
            def _(e):
                for c in q["pool"]:
                    c(e)

            @block.sync
            def _(e):
                for c in q["sp"]:
                    c(e)
        self.q = {e: [] for e in self.ENG}


class Buf:
    __slots__ = ("t", "r")

    def __init__(self, t):
        self.t = t
        self.r = Res()


class Ring:
    def __init__(self, bufs):
        self.bufs = bufs
        self.i = 0

    def next(self):
        b = self.bufs[self.i]
        self.i = (self.i + 1) % len(self.bufs)
        return b


C_Q, C_K, C_V, C_QH, C_IH, C_FF, C_GH, C_AC, C_GC = 0, 512, 640, 768, 1024, 1280, 1792, 2048, 2304
K_ID, K_MFW, K_MBW, K_SEL, K_SHIFT, K_ONES, K_ONESQ, K_END = 0, 128, 256, 384, 388, 452, 516, 644
PP_NM, PP_NMLP, PP_GN, PP_CW, PP_CB, PP_LNG, PP_LNB, PP_END = 0, 8, 16, 20, 82, 84, 86, 88


def build_program(NT, nlayers=2, dbg=()):
    nc = bass.Bass("TRN2", target_bir_lowering=False)
    L = R * NT
    NBLK = NT // 512
    NSUB = NT // 128
    T = L // 128
    NCH = NT // CH

    def din(name, shape, dt=F32):
        return nc.dram_tensor(name, list(shape), dt, kind="ExternalInput")

    x_in = din("x", [NT, D])
    w_in_d = din("w_in", [2, D, INC])
    w_out_d = din("w_out", [2, D, D])
    w1_d = din("w_mlp_in", [2, D, DFF])
    w2_d = din("w_mlp_out", [2, DFF, D])
    pp_d = din("pp", [2, 128, PP_END])
    qkn_d = din("qkn", [2, 1, 128])
    lbp_d = din("lbp", [1, 1024])
    consts_d = din("consts", [128, K_END])
    ropeC_d = din("ropeC", [128, NT // 128, 64])
    ropeS_d = din("ropeS", [128, NT // 128, 64])
    oh_d = din("oh", [1, 12])
    out_d = nc.dram_tensor("out", [NT, D], F32, kind="ExternalOutput")

    def dscr(name, shape, dt=F32):
        if name in dbg:
            return nc.dram_tensor(name, list(shape), dt, kind="ExternalOutput")
        return nc.dram_tensor(name, list(shape), dt)

    win_bf = dscr("win_bf", [2, D, INC], BF16)
    wout_bf = dscr("wout_bf", [2, D, D], BF16)
    w1_bf = dscr("w1_bf", [2, D, DFF], BF16)
    w2_bf = dscr("w2_bf", [2, DFF, D], BF16)
    qT_d = dscr("qT_d", [128, 4, NT], BF16)
    ag1_in = [dscr("ag1_in%d" % b, [256, 512], BF16) for b in range(NBLK)]
    ag1_out = [[dscr("ag1_out%d_%d" % (l, b), [R * 256, 512], BF16) for b in range(NBLK)] for l in range(2)]
    ag2_in = dscr("ag2_in", [128, 640], F32)
    ag2_out = [dscr("ag2_out%d" % l, [R * 128, 640], F32) for l in range(2)]
    qtT_d = dscr("qtT_d", [64, 2, 4, NT], BF16)
    kt_d = dscr("kt_d", [NT, 512], BF16)
    vh_d = dscr("vh_d", [NT, 256], BF16)
    oiT_d = dscr("oiT_d", [64, 4, NT], F32)
    of_d = dscr("of_d", [64, 4, NT], F32)
    gT_d = dscr("gT_d", [64, 4, NT], BF16)
    uText_d = dscr("uText_d", [128, 2, NT + 2 * HALO], F32)
    mixhg_d = dscr("mixhg_d", [64, 4, NT], BF16)
    mixcv_d = dscr("mixcv_d", [128, 2, NT], BF16)
    mixatt_d = dscr("mixatt_d", [64, 8, NT], BF16)
    x1_d = dscr("x1_d", [NT, D], F32)
    x2_d = dscr("x2_d", [NT, D], F32)

    es = ExitStack()
    P = Prog(nc, es)
    cnt = [0]

    def alloc(stack, kind, shape, dt):
        cnt[0] += 1
        f = nc.sbuf_tensor if kind == "sb" else nc.psum_tensor
        bf_ = Buf(stack.enter_context(f("%s%d" % (kind, cnt[0]), list(shape), dt)))
        bf_.r.excl = (kind == "ps")
        return bf_

    G = es
    consts = alloc(G, "sb", [128, K_END], F32)
    cbf = alloc(G, "sb", [128, 388], BF16)
    pp = alloc(G, "sb", [128, 2, PP_END], F32)
    oh = alloc(G, "sb", [128, 12], F32)
    d_all = alloc(G, "sb", [64, 2, 4, NCH], F32)
    Sst = [alloc(G, "sb", [64, 4, 64], F32) for _ in range(2)]
    Sin = [alloc(G, "sb", [64, 4, 64], F32) for _ in range(2)]

    ident_bf = cbf.t[:, 0:128]
    ident_f = consts.t[:, K_ID:K_ID + 128]
    Mf = [consts.t[:, K_MFW:K_MFW + 128], consts.t[:, K_MBW:K_MBW + 128]]
    sel_f = consts.t[:, K_SEL:K_SEL + 4]
    sel_bf = cbf.t[:, 384:388]
    shiftM = consts.t[:, K_SHIFT:K_SHIFT + 64]
    ones64 = consts.t[0:64, K_ONES:K_ONES + 64]
    onesq = consts.t[:, K_ONESQ:K_ONESQ + 128]

    def TT(eng, out, a, b, op, R_, W_):
        return P.op(eng, lambda e: e.tensor_tensor(out, a, b, op), R_, W_)

    def TS(eng, out, a, s1, s2, op0, op1, R_, W_):
        if s2 is None:
            return P.op(eng, lambda e: e.tensor_scalar(out, a, s1, None, op0), R_, W_)
        return P.op(eng, lambda e: e.tensor_scalar(out, a, s1, s2, op0, op1), R_, W_)

    def STT(eng, out, a, s, b, op0, op1, R_, W_):
        return P.op(eng, lambda e: e.scalar_tensor_tensor(out, a, s, b, op0, op1), R_, W_)

    def CP(eng, out, a, R_, W_):
        if eng == "act":
            return P.op(eng, lambda e: e.copy(out, a), R_, W_)
        return P.op(eng, lambda e: e.tensor_copy(out, a), R_, W_)

    def ACT(out, a, func, R_, W_, **kw):
        return P.op("act", lambda e: e.activation(out, a, func, **kw), R_, W_)

    def RSQRT(out, a, scale, bias, R_, W_):
        ACT(out, a, AF.Sqrt, R_, W_, bias=bias, scale=scale)
        P.op("dve", lambda e: e.reciprocal(out, out), W_, W_)

    def MEMSET(eng, out, v, W_):
        return P.op(eng, lambda e: e.memset(out, v), (), W_)

    def MM(groups, R_, W_):
        def fn(e):
            ins = None
            for (o, l, r, st, sp) in groups:
                ins = e.matmul(o, l, r, start=st, stop=sp)
            return ins
        return P.op("pe", fn, R_, W_)

    def TR(items, R_, W_):
        def fn(e):
            ins = None
            for (o, i, idn) in items:
                ins = e.transpose(o, i, idn)
            return ins
        return P.op("pe", fn, R_, W_)

    def bc(ap, shape, axis):
        return ap.unsqueeze(axis).to_broadcast(list(shape))

    with ExitStack() as ph:
        P.dma("sp", consts.t[:, :], consts_d.ap(), W_=[consts.r])
        P.dma("sp", pp.t[:, :, :], pp_d.ap().rearrange("l p c -> p l c"), W_=[pp.r])
        P.dma("sp", oh.t[:, :], oh_d.ap().partition_broadcast(128), W_=[oh.r])
        CP("dve", cbf.t[:, 0:388], consts.t[:, 0:388], [consts.r], [cbf.r])
        stg = Ring([alloc(ph, "sb", [128, 2048], F32) for _ in range(3)])
        stb = Ring([alloc(ph, "sb", [128, 2048], BF16) for _ in range(3)])
        flip = [0]

        def cast_rows(src, dst, rows, cols, gain_col, l):
            for rc in range(rows // 128):
                for c0 in range(0, cols, 2048):
                    w = min(2048, cols - c0)
                    a = stg.next()
                    b = stb.next()
                    P.dma("sp", a.t[:, 0:w], src[rc * 128:(rc + 1) * 128, c0:c0 + w], W_=[a.r])
                    eng = ("dve", "act", "dve")[flip[0] % 3]
                    flip[0] += 1
                    if gain_col is None:
                        CP(eng, b.t[:, 0:w], a.t[:, 0:w], [a.r], [b.r])
                    elif eng == "act":
                        g = pp.t[:, l, gain_col + rc:gain_col + rc + 1]
                        P.op("act", lambda e, o_=b.t[:, 0:w], i_=a.t[:, 0:w], g_=g: e.mul(o_, i_, g_), [a.r, pp.r], [b.r])
                    else:
                        g = pp.t[:, l, gain_col + rc:gain_col + rc + 1]
                        TS(eng, b.t[:, 0:w], a.t[:, 0:w], g, None, ALU.mult, None, [a.r, pp.r], [b.r])
                    P.dma("pool", dst[rc * 128:(rc + 1) * 128, c0:c0 + w], b.t[:, 0:w], R_=[b.r], W_=[wres])

        wres = Res("weights")
        for l in range(nlayers):
            cast_rows(w_in_d.ap()[l], win_bf.ap()[l], D, INC, PP_NM, l)
            cast_rows(w_out_d.ap()[l], wout_bf.ap()[l], D, D, None, l)
            cast_rows(w1_d.ap()[l], w1_bf.ap()[l], D, DFF, PP_NMLP, l)
            cast_rows(w2_d.ap()[l], w2_bf.ap()[l], DFF, D, None, l)
        P.barrier()
        P.flush()

    x1r = Res("x1_d")
    x2r = Res("x2_d")

    def load_norm_T(ph_bufs, x_src, b, keep=None, src_res=None, skip_load=False):
        xt_ring, junk, st_ring, hb_ring, banks, hT_ring = ph_bufs
        hT = hT_ring.next()
        for s in range(4):
            tok0 = b * 512 + s * 128
            if keep is None:
                xt = xt_ring.next()
                xv = xt.t[:, :]
                xr = xt.r
            else:
                xv = keep.t[:, s, :]
                xr = keep.r
            if not skip_load:
                P.dma("sp", xv, x_src[tok0:tok0 + 128, :], R_=([src_res] if src_res is not None else []), W_=[xr])
            st = st_ring.next()
            ACT(junk.t[:, :], xv, AF.Square, [xr], [junk.r, st.r], accum_out=st.t[:, 0:1])
            RSQRT(st.t[:, 1:2], st.t[:, 0:1], 1.0 / D, EPS, [st.r], [st.r])
            hb = hb_ring.next()
            TS("dve", hb.t[:, :], xv, st.t[:, 1:2], None, ALU.mult, None, [xr, st.r], [hb.r])
            pt = banks.next()
            ptb = pt.t[:, :].bitcast(BF16)
            TR([(ptb[:, c * 128:(c + 1) * 128], hb.t[:, c * 128:(c + 1) * 128], ident_bf) for c in range(8)],
               [hb.r, cbf.r], [pt.r])
            CP("act", hT.t[:, :, s * 128:(s + 1) * 128], ptb.rearrange("p (c t) -> p c t", c=8), [pt.r], [hT.r])
        return hT

    for l in range(nlayers):
        x_cur = x_in.ap() if l == 0 else x2_d.ap()
        x_next = out_d.ap() if l == nlayers - 1 else x2_d.ap()
        winl = win_bf.ap()[l].rearrange("(c p) n -> p c n", p=128)
        ag1r = Res("ag1_in")
        ag2r = Res("ag2_in")
        scr = Res("scratch_l%d" % l)
        uTr = Res("uText")

        with ExitStack() as ph:
            SB = lambda shape, dt=F32: alloc(ph, "sb", shape, dt)
            banks = Ring([alloc(ph, "ps", [128, 512], F32) for _ in range(8)])
            xt_ring = Ring([SB([128, D]) for _ in range(2)])
            junk = SB([128, D], BF16)
            st_ring = Ring([SB([128, 2]) for _ in range(4)])
            hb_ring = Ring([SB([128, D], BF16) for _ in range(2)])
            hT_ring = Ring([SB([128, 8, 512], BF16) for _ in range(1)])
            lnt = (xt_ring, junk, st_ring, hb_ring, banks, hT_ring)
            wq_ring = Ring([SB([128, 8, 512], BF16) for _ in range(2)])
            wf_ring = Ring([SB([128, 8, 768], BF16) for _ in range(1)])
            z = [[SB([128, 1792]) for _ in range(4)] for _ in range(1)]
            gT_blk = Ring([SB([64, 4, 512], BF16) for _ in range(1)])
            uT_blk = Ring([SB([128, 2, 512]) for _ in range(1)])
            sg_ring = Ring([SB([128, 512]) for _ in range(2)])
            qT_blk = Ring([SB([128, 4, 512], BF16) for _ in range(1)])
            kT_blk = Ring([SB([128, 512], BF16) for _ in range(1)])
            QtT_blk = Ring([SB([64, 2, 4, 512], BF16) for _ in range(1)])
            oiT_blk = Ring([SB([64, 4, 512]) for _ in range(1)])
            W2 = lambda shape, dt=F32: Ring([SB(shape, dt) for _ in range(2)])
            W1 = lambda shape, dt=F32: Ring([SB(shape, dt)])
            sq_r, ssq_r, qn_r, t1_r, t2_r = W1([128, 640]), W2([128, 16]), W1([128, 640]), W1([128, 640]), W1([128, 640])
            qr_r, vb_r = W2([128, 640], BF16), W2([128, 128], BF16)
            sgf_r, sgn_r, fg_r, logf_r, kk_r = W1([128, 512]), W1([128, 512]), W1([128, 512]), W1([128, 512]), W1([128, 512])
            eb_r, enb_r = W1([128, 512]), W1([128, 512])
            Qt_r, Kt_r, Vb_r = W2([128, 2, 256], BF16), W2([128, 512], BF16), W2([128, 256], BF16)
            QtT_r, KtT_r = W2([64, 8, 128], BF16), W2([64, 8, 128], BF16)
            AT_r = [W2([128, 4, 128], BF16), W2([128, 4, 128], BF16)]
            TM_GROUPS = [(0, 512), (512, 256), (768, 512), (1280, 512)]
            gqk = SB([128, 2, 10, 64])
            lbt = SB([128, 2, 512])
            oml = SB([128, 2, 512])
            qkb = SB([128, 2, 128])
            P.dma("sp", qkb.t[:, :, :], qkn_d.ap().rearrange("l o c -> o l c").partition_broadcast(128), W_=[qkb.r])
            CP("dve", gqk.t[:, l, 0:8, :], bc(qkb.t[:, l, 0:64], [128, 8, 64], 1), [qkb.r], [gqk.r])
            CP("dve", gqk.t[:, l, 8:10, :], bc(qkb.t[:, l, 64:128], [128, 2, 64], 1), [qkb.r], [gqk.r])
            lbb = SB([128, 1024])
            P.dma("sp", lbb.t[:, :], lbp_d.ap().partition_broadcast(128), W_=[lbb.r])
            lbv = lbb.t[:, :].rearrange("p (d l c) -> p d l c", d=2, l=2)
            dlt = SB([128, 2, 256])
            TT("dve", dlt.t[:, :, :], lbv[:, :, 1, :], lbv[:, :, 0, :], ALU.subtract, [lbb.r], [dlt.r])
            MEMSET("pool", lbt.t[:, 0, :], 0.0, [lbt.r])
            ACT(lbt.t[:, 1, :], dlt.t[:, :, :].rearrange("p d c -> p (d c)"), AF.Sigmoid, [dlt.r], [lbt.r])
            TS("dve", oml.t[:, :, :], lbt.t[:, :, :], -1.0, 1.0, ALU.mult, ALU.add, [lbt.r], [oml.r])
            ropeC = SB([128, NSUB, 64])
            ropeS = SB([128, NSUB, 64])
            P.dma("sp", ropeC.t[:, :, :], ropeC_d.ap(), W_=[ropeC.r])
            P.dma("sp", ropeS.t[:, :, :], ropeS_d.ap(), W_=[ropeS.r])

            for b in range(NBLK):
                hT = load_norm_T(lnt, x_cur, b, src_res=(x2r if l > 0 else None))
                zb = z[0]
                for gi, (c0, ncol) in enumerate(TM_GROUPS):
                    wq = wq_ring.next()
                    P.dma("sp", wq.t[:, :, 0:ncol], winl[:, :, c0:c0 + ncol], R_=[wres], W_=[wq.r])
                    for s in range(4):
                        pz = banks.next()
                        MM([(pz.t[:, 0:ncol], hT.t[:, c, s * 128:(s + 1) * 128], wq.t[:, c, 0:ncol], c == 0, c == 7)
                            for c in range(8)], [hT.r, wq.r], [pz.r])
                        CP(("act", "dve")[s % 2], zb[s].t[:, c0:c0 + ncol], pz.t[:, 0:ncol], [pz.r], [zb[s].r])
                wf = wf_ring.next()
                P.dma("sp", wf.t[:, :, :], winl[:, :, C_GH:INC], R_=[wres], W_=[wf.r])
                gTb = gT_blk.next()
                for h in range(4):
                    pz = banks.next()
                    MM([(pz.t[0:64, :], wf.t[:, c, h * 64:(h + 1) * 64], hT.t[:, c, :], c == 0, c == 7) for c in range(8)],
                       [hT.r, wf.r], [pz.r])
                    ACT(gTb.t[:, h, :], pz.t[0:64, :], AF.Silu, [pz.r], [gTb.r])
                P.dma("sp", gT_d.ap()[:, :, b * 512:(b + 1) * 512], gTb.t[:, :, :], R_=[gTb.r], W_=[scr])
                uTb = uT_blk.next()
                for c2 in range(2):
                    pa = banks.next()
                    pg = banks.next()
                    MM([(pa.t[:, :], wf.t[:, c, 256 + c2 * 128:256 + (c2 + 1) * 128], hT.t[:, c, :], c == 0, c == 7)
                        for c in range(8)], [hT.r, wf.r], [pa.r])
                    MM([(pg.t[:, :], wf.t[:, c, 512 + c2 * 128:512 + (c2 + 1) * 128], hT.t[:, c, :], c == 0, c == 7)
                        for c in range(8)], [hT.r, wf.r], [pg.r])
                    sg = sg_ring.next()
                    ACT(sg.t[:, :], pg.t[:, :], AF.Sigmoid, [pg.r], [sg.r])
                    TT("dve", uTb.t[:, c2, :], pa.t[:, :], sg.t[:, :], ALU.mult, [pa.r, sg.r], [uTb.r])
                P.dma("sp", uText_d.ap()[:, :, HALO + b * 512:HALO + (b + 1) * 512], uTb.t[:, :, :], R_=[uTb.r], W_=[uTr])

                qTb = qT_blk.next()
                kTb = kT_blk.next()
                QtTb = QtT_blk.next()
                oiTb = oiT_blk.next()
                for s in range(4):
                    zs = zb[s]
                    zt = zs.t
                    ts = b * 4 + s
                    tok0 = ts * 128
                    qk = zt[:, 0:640]
                    qk3 = qk.rearrange("p (h d) -> p h d", d=64)
                    sq, ssq, qn, t1, t2, qr, vb = sq_r.next(), ssq_r.next(), qn_r.next(), t1_r.next(), t2_r.next(), qr_r.next(), vb_r.next()
                    TT("dve", sq.t[:, :], qk, qk, ALU.mult, [zs.r], [sq.r])
                    P.op("dve", lambda e, o=ssq.t[:, 0:10], i=sq.t[:, :].rearrange("p (h d) -> p h d", d=64):
                         e.tensor_reduce(o, i, AX.X, ALU.add), [sq.r], [ssq.r])
                    RSQRT(ssq.t[:, 0:10], ssq.t[:, 0:10], 1.0, 64 * EPS, [ssq.r], [ssq.r])
                    qn3 = qn.t[:, :].rearrange("p (h d) -> p h d", d=64)
                    TT("dve", qn3, qk3, bc(ssq.t[:, 0:10], [128, 10, 64], 2), ALU.mult, [zs.r, ssq.r], [qn.r])
                    TT("dve", qn3, qn3, gqk.t[:, l, :, :], ALU.mult, [qn.r, gqk.r], [qn.r])
                    t13 = t1.t[:, :].rearrange("p (h d) -> p h d", d=64)
                    t23 = t2.t[:, :].rearrange("p (h d) -> p h d", d=64)
                    TT("dve", t13, qn3, bc(ropeC.t[:, ts, :], [128, 10, 64], 1), ALU.mult, [qn.r, ropeC.r], [t1.r])
                    for a in range(2):
                        for hf in range(2):
                            o0 = a * 32 + hf * 16
                            i0 = a * 32 + (1 - hf) * 16
                            TT("dve", t23[:, :, o0:o0 + 16], qn3[:, :, i0:i0 + 16],
                               bc(ropeS.t[:, ts, o0:o0 + 16], [128, 10, 16], 1), ALU.mult, [qn.r, ropeS.r], [t2.r])
                    TT("dve", qr.t[:, 0:512].rearrange("p (j g d) -> p g j d", j=4, g=2),
                       t1.t[:, 0:512].rearrange("p (g j d) -> p g j d", g=2, j=4),
                       t2.t[:, 0:512].rearrange("p (g j d) -> p g j d", g=2, j=4), ALU.add, [t1.r, t2.r], [qr.r])
                    TT("dve", qr.t[:, 512:640], t1.t[:, 512:640], t2.t[:, 512:640], ALU.add, [t1.r, t2.r], [qr.r])
                    pt = banks.next()
                    ptb = pt.t[:, :].bitcast(BF16)
                    TR([(ptb[:, j * 128:(j + 1) * 128], qr.t[:, j * 128:(j + 1) * 128], ident_bf) for j in range(4)]
                       + [(ptb[:, 512:640], qr.t[:, 512:640], ident_bf)], [qr.r, cbf.r], [pt.r])
                    CP("act", qTb.t[:, :, s * 128:(s + 1) * 128], ptb[:, 0:512].rearrange("p (j t) -> p j t", j=4), [pt.r], [qTb.r])
                    CP("dve", kTb.t[:, s * 128:(s + 1) * 128], ptb[:, 512:640], [pt.r], [kTb.r])
                    CP("dve", vb.t[:, :], zt[:, C_V:C_V + 128], [zs.r], [vb.r])
                    P.dma("sp", ag1_in[b].ap()[128:256, :].rearrange("a (b c) -> (a b) c", c=128)[s * 128:(s + 1) * 128, :], vb.t[:, :],
                          R_=[vb.r], W_=[ag1r])
                    sgf, sgn, fg, logf, kk = sgf_r.next(), sgn_r.next(), fg_r.next(), logf_r.next(), kk_r.next()
                    eb, enb, Qt, Kt, Vb = eb_r.next(), enb_r.next(), Qt_r.next(), Kt_r.next(), Vb_r.next()
                    ff = zt[:, C_FF:C_FF + 512]
                    ACT(sgf.t[:, :], ff, AF.Sigmoid, [zs.r], [sgf.r])
                    ACT(sgn.t[:, :], ff, AF.Sigmoid, [zs.r], [sgn.r], scale=-1.0)
                    TT("dve", fg.t[:, :], sgf.t[:, :], oml.t[:, l, :], ALU.mult, [sgf.r, oml.r], [fg.r])
                    TT("dve", fg.t[:, :], fg.t[:, :], lbt.t[:, l, :], ALU.add, [fg.r, lbt.r], [fg.r])
                    ACT(logf.t[:, :], fg.t[:, :], AF.Ln, [fg.r], [logf.r])
                    TT("dve", kk.t[:, :], sgn.t[:, :], oml.t[:, l, :], ALU.mult, [sgn.r, oml.r], [kk.r])
                    pc = banks.next()
                    MM([(pc.t[:, 0:256], Mf[0], logf.t[:, 0:256], True, True),
                        (pc.t[:, 256:512], Mf[1], logf.t[:, 256:512], True, True)], [consts.r, logf.r], [pc.r])
                    ACT(eb.t[:, :], pc.t[:, :], AF.Exp, [pc.r], [eb.r])
                    ACT(enb.t[:, :], pc.t[:, :], AF.Exp, [pc.r], [enb.r], scale=-1.0)
                    TT("dve", Qt.t[:, :, :], bc(zt[:, C_QH:C_QH + 256], [128, 2, 256], 1),
                       eb.t[:, :].rearrange("p (d c) -> p d c", d=2), ALU.mult, [zs.r, eb.r], [Qt.r])
                    TT("dve", Kt.t[:, :], kk.t[:, :], enb.t[:, :], ALU.mult, [kk.r, enb.r], [Kt.r])
                    CP("dve", Vb.t[:, :], zt[:, C_IH:C_IH + 256], [zs.r], [Vb.r])
                    P.dma("sp", kt_d.ap()[tok0:tok0 + 128, :], Kt.t[:, :], R_=[Kt.r], W_=[scr])
                    P.dma("sp", vh_d.ap()[tok0:tok0 + 128, :], Vb.t[:, :], R_=[Vb.r], W_=[scr])
                    pd = banks.next()
                    MM([(pd.t[0:64, (dr * 4 + h) * 4:(dr * 4 + h) * 4 + 4], logf.t[:, dr * 256 + h * 64:dr * 256 + (h + 1) * 64], sel_f, True, True)
                        for dr in range(2) for h in range(4)], [logf.r, consts.r], [pd.r])
                    ACT(d_all.t[:, :, :, ts * 4:ts * 4 + 4].rearrange("p d h n -> p (d h) n"),
                        pd.t[0:64, 0:32].rearrange("p (a n) -> p a n", n=4), AF.Exp, [pd.r], [d_all.r])
                    pq = banks.next()
                    pqb = pq.t[:, :].bitcast(BF16)
                    TR([(pqb[0:64, (dr * 4 + h) * 128:(dr * 4 + h + 1) * 128], Qt.t[:, dr, h * 64:(h + 1) * 64], ident_bf)
                        for dr in range(2) for h in range(4)], [Qt.r, cbf.r], [pq.r])
                    QtT, KtT = QtT_r.next(), KtT_r.next()
                    CP("dve", QtT.t[:, :, :], pqb[0:64, :].rearrange("p (a t) -> p a t", a=8), [pq.r], [QtT.r])
                    CP("act", QtTb.t[:, :, :, s * 128:(s + 1) * 128].rearrange("p d h t -> p (d h) t"),
                       pqb[0:64, :].rearrange("p (a t) -> p a t", a=8), [pq.r], [QtTb.r])
                    pk = banks.next()
                    pkb = pk.t[:, :].bitcast(BF16)
                    TR([(pkb[0:64, (dr * 4 + h) * 128:(dr * 4 + h + 1) * 128], Kt.t[:, dr * 256 + h * 64:dr * 256 + (h + 1) * 64], ident_bf)
                        for dr in range(2) for h in range(4)], [Kt.r, cbf.r], [pk.r])
                    CP("dve", KtT.t[:, :, :], pkb[0:64, :].rearrange("p (a t) -> p a t", a=8), [pk.r], [KtT.r])
                    ATs = []
                    for dr in range(2):
                        pa = banks.next()
                        MM([(pa.t[:, h * 128:(h + 1) * 128], KtT.t[:, dr * 4 + h, :], QtT.t[:, dr * 4 + h, :], True, True) for h in range(4)],
                           [KtT.r, QtT.r], [pa.r])
                        AT = AT_r[dr].next()
                        TT("dve", AT.t[:, :, :], pa.t[:, :].rearrange("p (h t) -> p h t", h=4), bc(Mf[dr], [128, 4, 128], 1), ALU.mult,
                           [pa.r, consts.r], [AT.r])
                        ATs.append(AT)
                    po = banks.next()
                    MM([(po.t[0:64, h * 128:(h + 1) * 128], Vb.t[:, h * 64:(h + 1) * 64], ATs[dr].t[:, h, :], dr == 0, dr == 1)
                        for h in range(4) for dr in range(2)], [Vb.r, ATs[0].r, ATs[1].r], [po.r])
                    CP("act", oiTb.t[:, :, s * 128:(s + 1) * 128], po.t[0:64, :].rearrange("p (h t) -> p h t", h=4), [po.r], [oiTb.r])
                P.dma("sp", qT_d.ap()[:, :, b * 512:(b + 1) * 512], qTb.t[:, :, :], R_=[qTb.r], W_=[scr])
                P.dma("sp", ag1_in[b].ap()[0:128, :], kTb.t[:, :], R_=[kTb.r], W_=[ag1r])
                P.dma("sp", qtT_d.ap()[:, :, :, b * 512:(b + 1) * 512], QtTb.t[:, :, :, :], R_=[QtTb.r], W_=[scr])
                P.dma("sp", oiT_d.ap()[:, :, b * 512:(b + 1) * 512], oiTb.t[:, :, :], R_=[oiTb.r], W_=[scr])
            P.barrier()
            P.flush()
        if "stop_s1" in dbg:
            break

        ofr = [Res("of_d%d" % i) for i in range(NSUB)]
        ag1o = Res("ag1_out")
        ag2o = Res("ag2_out")
        mixr = Res("mix_d")

        ringcache = {}

        def hgrn_gen(final, ph, getbank, deep):
            if True:
                SB = lambda shape, dt=F32: alloc(ph, "sb", shape, dt)
                nb = 4 if deep else 2
                if id(ph) not in ringcache:
                    ringcache[id(ph)] = (
                        Ring([SB([128, 256], BF16) for _ in range(nb + 2)]),
                        Ring([SB([128, 256], BF16) for _ in range(nb + 2)]),
                        Ring([SB([128, 4, 4, 64], BF16) for _ in range(nb)]),
                        Ring([SB([64, 4, 4, 64]) for _ in range(nb)]),
                        Ring([SB([64, 4, 4, 64], BF16) for _ in range(nb)]),
                        Ring([SB([64, 4, 128], BF16) for _ in range(nb)]),
                        Ring([SB([64, 4, 128]) for _ in range(nb)]),
                        Ring([SB([64, 512]) for _ in range(nb // 2)]),
                        Ring([SB([64, 512]) for _ in range(nb // 2)]),
                        Ring([SB([64, 512]) for _ in range(nb // 2)]),
                        Ring([SB([64, 4, 128], BF16) for _ in range(2)]),
                        Ring([SB([64, 4, 128], BF16) for _ in range(2)]),
                        [SB([64, 4, 64]) for _ in range(2)],
                        SB([64, 8]))
                (Kl_ring, Vl_ring, Vx_ring, Us_ring, Sp_ring, Ql_ring, oi_ring, o_ring, sq_ring, rs_ring,
                 gl_ring, mx_ring, Salt, dc) = ringcache[id(ph)]
                if not final:
                    zt_ = SB([128, 640])
                    MEMSET("pool", zt_.t[:, :], 0.0, [zt_.r])
                    P.dma("sp", ag2_in.ap(), zt_.t[:, :], R_=[zt_.r], W_=[ag2r])
                if final:
                    Gt = SB([64, R, 520])
                    Gh = SB([128, R, 60])
                    Pst = SB([64, 4, 64])
                    hp = SB([128, 2, 15])
                    hn = SB([128, 2, 15])
                    g2 = ag2_out[l].ap().rearrange("(r p) c -> p r c", p=128)
                    P.dma("sp", Gt.t[:, :, :], g2[0:64, :, 0:520], R_=[ag2o], W_=[Gt.r])
                    P.dma("sp", Gh.t[:, :, :], g2[:, :, 520:580], R_=[ag2o], W_=[Gh.r])
                    for dr in range(2):
                        MEMSET("pool", Pst.t[:, :, :], 0.0, [Pst.r])
                        MEMSET("pool", Sin[dr].t[:, :, :], 0.0, [Sin[dr].r])
                        for r in (range(R) if dr == 0 else range(R - 1, -1, -1)):
                            STT("dve", Sin[dr].t[:, :, :], Pst.t[:, :, :], oh.t[0:64, r:r + 1], Sin[dr].t[:, :, :], ALU.mult, ALU.add,
                                [Pst.r, oh.r, Sin[dr].r], [Sin[dr].r])
                            Tr = Gt.t[:, r, dr * 260:dr * 260 + 256].rearrange("p (h v) -> p h v", h=4)
                            Dr = Gt.t[:, r, dr * 260 + 256:dr * 260 + 260]
                            TT("dve", Pst.t[:, :, :], Pst.t[:, :, :], bc(Dr, [64, 4, 64], 2), ALU.mult, [Pst.r, Gt.r], [Pst.r])
                            TT("dve", Pst.t[:, :, :], Pst.t[:, :, :], Tr, ALU.add, [Pst.r, Gt.r], [Pst.r])
                    Gh5 = Gh.t[:, :, :].rearrange("p r (c e k) -> p r c e k", c=2, e=2)
                    for (hb_, e_, o0) in ((hp, 1, 4), (hn, 0, 8)):
                        TS("dve", hb_.t[:, :, :], Gh5[:, 0, :, e_, :], oh.t[:, o0:o0 + 1], None, ALU.mult, None, [Gh.r, oh.r], [hb_.r])
                        for r in range(1, R):
                            STT("dve", hb_.t[:, :, :], Gh5[:, r, :, e_, :], oh.t[:, o0 + r:o0 + r + 1], hb_.t[:, :, :], ALU.mult, ALU.add,
                                [Gh.r, oh.r, hb_.r], [hb_.r])
                    P.dma("sp", uText_d.ap()[:, :, 0:HALO], hp.t[:, :, :], R_=[hp.r], W_=[uTr])
                    P.dma("sp", uText_d.ap()[:, :, HALO + NT:2 * HALO + NT], hn.t[:, :, :], R_=[hn.r], W_=[uTr])

                states = {}
                for dr in range(2):
                    cur, nxt = Sst[dr], Salt[dr]
                    if final:
                        CP("dve", cur.t[:, :, :], Sin[dr].t[:, :, :], [Sin[dr].r], [cur.r])
                    else:
                        MEMSET("pool", cur.t[:, :, :], 0.0, [cur.r])
                    states[dr] = (cur, nxt)
                for i in range(NSUB):
                    for dr in range(2):
                        ts = i if dr == 0 else NSUB - 1 - i
                        first = i < NSUB // 2
                        cur, nxt = states[dr]
                        tok0 = ts * 128
                        Kl, Vl, Vx, Usb = Kl_ring.next(), Vl_ring.next(), Vx_ring.next(), Us_ring.next()
                        P.dma("sp", Kl.t[:, :], kt_d.ap()[tok0:tok0 + 128, dr * 256:(dr + 1) * 256], R_=[scr], W_=[Kl.r])
                        P.dma("sp", Vl.t[:, :], vh_d.ap()[tok0:tok0 + 128, :], R_=[scr], W_=[Vl.r])
                        TT("dve", Vx.t[:, :, :, :],
                           Vl.t[:, :].rearrange("p (h v) -> p h v", h=4).unsqueeze(2).to_broadcast([128, 4, 4, 64]),
                           sel_bf.unsqueeze(1).unsqueeze(3).to_broadcast([128, 4, 4, 64]), ALU.mult, [Vl.r, cbf.r], [Vx.r])
                        if final:
                            Ql = Ql_ring.next()
                            P.dma("sp", Ql.t[:, :, :], qtT_d.ap()[:, dr, :, tok0:tok0 + 128], R_=[scr], W_=[Ql.r])
                        yield
                        for hp2 in range(2):
                            pu = getbank()
                            MM([(pu.t[0:64, hh * 256:(hh + 1) * 256], Kl.t[:, (hp2 * 2 + hh) * 64:(hp2 * 2 + hh + 1) * 64],
                                 Vx.t[:, hp2 * 2 + hh, :, :].rearrange("p n v -> p (n v)"), True, True) for hh in range(2)],
                               [Kl.r, Vx.r], [pu.r])
                            CP("dve", Usb.t[:, hp2 * 2:hp2 * 2 + 2, :, :].rearrange("p h n v -> p h (n v)"),
                               pu.t[0:64, :].rearrange("p (h x) -> p h x", h=2), [pu.r], [Usb.r])
                        Sp = Sp_ring.next() if final else None
                        for n in (range(4) if dr == 0 else range(3, -1, -1)):
                            ch = ts * 4 + n
                            if final:
                                CP("dve", Sp.t[:, n, :, :], cur.t[:, :, :], [cur.r], [Sp.r])
                            TT("dve", nxt.t[:, :, :], cur.t[:, :, :], Usb.t[:, :, n, :], ALU.add, [cur.r, Usb.r], [nxt.r])
                            TT("dve", nxt.t[:, :, :], nxt.t[:, :, :], bc(d_all.t[:, dr, :, ch], [64, 4, 64], 2), ALU.mult,
                               [nxt.r, d_all.r], [nxt.r])
                            cur, nxt = nxt, cur
                        states[dr] = (cur, nxt)
                        yield
                        if not final:
                            continue
                        po = getbank()
                        MM([(po.t[0:64, h * 128 + n * 32:h * 128 + (n + 1) * 32], Sp.t[:, n, h, :], Ql.t[:, h, n * 32:(n + 1) * 32], True, True)
                            for n in range(4) for h in range(4)], [Sp.r, Ql.r], [po.r])
                        po3 = po.t[0:64, :].rearrange("p (h t) -> p h t", h=4)
                        oi = oi_ring.next()
                        if first:
                            P.dma("sp", oi.t[:, :, :], oiT_d.ap()[:, :, tok0:tok0 + 128], R_=[scr], W_=[oi.r])
                            TT("dve", oi.t[:, :, :], po3, oi.t[:, :, :], ALU.add, [po.r, oi.r], [oi.r])
                            P.dma("sp", of_d.ap()[:, :, tok0:tok0 + 128], oi.t[:, :, :], R_=[oi.r], W_=[ofr[ts]])
                        else:
                            P.dma("sp", oi.t[:, :, :], of_d.ap()[:, :, tok0:tok0 + 128], R_=[ofr[ts]], W_=[oi.r])
                            o, sq, rs, gl, mx = o_ring.next(), sq_ring.next(), rs_ring.next(), gl_ring.next(), mx_ring.next()
                            o3 = o.t[:, :].rearrange("p (h t) -> p h t", h=4)
                            TT("dve", o3, po3, oi.t[:, :, :], ALU.add, [po.r, oi.r], [o.r])
                            ACT(sq.t[:, :], o.t[:, :], AF.Square, [o.r], [sq.r])
                            pn = getbank()
                            MM([(pn.t[0:64, h * 128:(h + 1) * 128], ones64, sq.t[:, h * 128:(h + 1) * 128], True, True) for h in range(4)],
                               [consts.r, sq.r], [pn.r])
                            RSQRT(rs.t[:, :], pn.t[0:64, :], 1.0 / 64, EPS, [pn.r], [rs.r])
                            TT("dve", o.t[:, :], o.t[:, :], rs.t[:, :], ALU.mult, [o.r, rs.r], [o.r])
                            TT("dve", o3, o3, bc(pp.t[0:64, l, PP_GN:PP_GN + 4], [64, 4, 128], 2), ALU.mult, [o.r, pp.r], [o.r])
                            P.dma("sp", gl.t[:, :, :], gT_d.ap()[:, :, tok0:tok0 + 128], R_=[scr], W_=[gl.r])
                            TT("dve", mx.t[:, :, :], o3, gl.t[:, :, :], ALU.mult, [o.r, gl.r], [mx.r])
                            P.dma("sp", mixhg_d.ap()[:, :, tok0:tok0 + 128], mx.t[:, :, :], R_=[mx.r], W_=[mixr])
                        yield
                if not final:
                    for dr in range(2):
                        cur = states[dr][0]
                        P.dma("sp", ag2_in.ap()[0:64, dr * 260:dr * 260 + 256], cur.t[:, :, :].rearrange("p h v -> p (h v)"),
                              R_=[cur.r], W_=[ag2r])
                        P.op("dve", lambda e, o_=dc.t[:, dr * 4:dr * 4 + 4], i_=d_all.t[:, dr, :, :]: e.tensor_reduce(o_, i_, AX.X, ALU.mult),
                             [d_all.r], [dc.r])
                        P.dma("sp", ag2_in.ap()[0:64, dr * 260 + 256:dr * 260 + 260], dc.t[:, dr * 4:dr * 4 + 4], R_=[dc.r], W_=[ag2r])
                if not final:
                    a2 = ag2_in.ap()[:, 520:580].rearrange("p (c e k) -> p c e k", c=2, e=2)
                    P.dma("sp", a2[:, :, 0, :], uText_d.ap()[:, :, HALO:2 * HALO], R_=[uTr], W_=[ag2r])
                    P.dma("sp", a2[:, :, 1, :], uText_d.ap()[:, :, NT:NT + HALO], R_=[uTr], W_=[ag2r])

        for b in range(NBLK):
            P.allgather(ag1_in[b], ag1_out[l][b], R_=[ag1r], W_=[ag1o])

        def conv_gen(ph, getbank):
            SB = lambda shape, dt=F32: alloc(ph, "sb", shape, dt)
            ub_ring = Ring([SB([128, 2, 512 + 2 * HALO]) for _ in range(1)])
            acc_ring = Ring([SB([128, 2, 512]) for _ in range(1)])
            yc_ring = Ring([SB([128, 2, 512]) for _ in range(1)])
            rs_ring = Ring([SB([128, 512]) for _ in range(1)])
            mc_ring = Ring([SB([128, 2, 512], BF16) for _ in range(1)])
            cw = pp.t[:, l, PP_CW:PP_CW + 62].rearrange("p (c k) -> p c k", c=2)
            for b in range(NBLK):
                ub, acc, yc, rs, mc = ub_ring.next(), acc_ring.next(), yc_ring.next(), rs_ring.next(), mc_ring.next()
                sq = acc
                P.dma("sp", ub.t[:, :, :], uText_d.ap()[:, :, b * 512:b * 512 + 512 + 2 * HALO], R_=[uTr], W_=[ub.r])
                for c in range(2):
                    TS("dve", acc.t[:, c, :], ub.t[:, c, 0:512], cw[:, c, 0:1], pp.t[:, l, PP_CB + c:PP_CB + c + 1], ALU.mult, ALU.add,
                       [ub.r, pp.r], [acc.r])
                    for k in range(1, KW):
                        STT("dve", acc.t[:, c, :], ub.t[:, c, k:k + 512], cw[:, c, k:k + 1], acc.t[:, c, :], ALU.mult, ALU.add,
                            [ub.r, pp.r, acc.r], [acc.r])
                    yield
                pm = getbank()
                MM([(pm.t[:, :], onesq, acc.t[:, c, :], c == 0, c == 1) for c in range(2)], [consts.r, acc.r], [pm.r])
                TT("dve", yc.t[:, :, :], acc.t[:, :, :], bc(pm.t[:, :], [128, 2, 512], 1), ALU.subtract, [acc.r, pm.r], [yc.r])
                ACT(sq.t[:, :, :], yc.t[:, :, :], AF.Square, [yc.r], [sq.r])
                yield
                pv = getbank()
                MM([(pv.t[:, :], onesq, sq.t[:, c, :], c == 0, c == 1) for c in range(2)], [consts.r, sq.r], [pv.r])
                RSQRT(rs.t[:, :], pv.t[:, :], 1.0, EPS, [pv.r], [rs.r])
                TT("dve", yc.t[:, :, :], yc.t[:, :, :], bc(rs.t[:, :], [128, 2, 512], 1), ALU.mult, [yc.r, rs.r], [yc.r])
                for c in range(2):
                    ACT(mc.t[:, c, :], yc.t[:, c, :], AF.Silu, [yc.r, pp.r], [mc.r],
                        scale=pp.t[:, l, PP_LNG + c:PP_LNG + c + 1], bias=pp.t[:, l, PP_LNB + c:PP_LNB + c + 1])
                P.dma("sp", mixcv_d.ap()[:, :, b * 512:(b + 1) * 512], mc.t[:, :, :], R_=[mc.r], W_=[mixr])
                yield

        with ExitStack() as ph:
            SB = lambda shape, dt=F32: alloc(ph, "sb", shape, dt)
            st_ring = Ring([alloc(ph, "ps", [128, 1024], F32) for _ in range(3)])
            obank = [alloc(ph, "ps", [128, 512], F32) for _ in range(2)]
            tbank = st_ring

            class View:
                def __init__(self, t, r):
                    self.t = t
                    self.r = r

            pending = set()

            def take_slot():
                for _ in range(len(st_ring.bufs)):
                    sl_ = st_ring.next()
                    if id(sl_) not in pending:
                        return sl_
                raise RuntimeError("no free score slot")

            def halfbank():
                sl_ = take_slot()
                return View(sl_.t[:, 0:512], sl_.r)

            kT_all = SB([128, L], BF16)
            v_all = SB([128, T, 2, 128], BF16)
            QT_ring = Ring([SB([128, 4, 512], BF16) for _ in range(2)])
            pT_ring = Ring([SB([128, 1024], BF16) for _ in range(3)])
            osb_ring = Ring([SB([128, 512]) for _ in range(2)])
            rec_ring = Ring([SB([64, 512]) for _ in range(1)])
            mixA_ring = Ring([SB([64, 8, 512], BF16) for _ in range(1)])
            MEMSET("pool", v_all.t[:, :, :, 64:128], 1.0, [v_all.r])
            for r in range(R):
                for b in range(NBLK):
                    go = ag1_out[l][b].ap()
                    P.dma("sp", kT_all.t[:, r * NT + b * 512:r * NT + (b + 1) * 512], go[r * 256:r * 256 + 128, :], R_=[ag1o], W_=[kT_all.r])
                    vsrc = go[r * 256 + 128:r * 256 + 256, :].rearrange("a (b c) -> (a b) c", c=128)
                    vsrc = vsrc.rearrange("(t p) (g d) -> p t g d", p=128, g=2)
                    t0 = r * NSUB + b * 4
                    for g in range(2):
                        P.dma("sp", v_all.t[:, t0:t0 + 4, g, 0:64], vsrc[:, :, g, :], R_=[ag1o], W_=[v_all.r])
            flags = {"conv_ok": False, "ag2": False}

            def side_chain():
                yield from hgrn_gen(False, ph, halfbank, False)
                P.allgather(ag2_in, ag2_out[l], R_=[ag2r], W_=[ag2o])
                flags["ag2"] = True
                yield
                for _ in hgrn_gen(True, ph, halfbank, False):
                    flags["conv_ok"] = True
                    yield

            hgen = side_chain()
            cgen = conv_gen(ph, halfbank)
            side = {"h": True, "c": True}

            def step(which):
                g_ = hgen if which == "h" else cgen
                if which == "c" and not flags["conv_ok"]:
                    return
                if side[which]:
                    try:
                        next(g_)
                    except StopIteration:
                        side[which] = False

            nsteps = NBLK * 4 * T
            h_every = max(1, nsteps // (6 * NSUB + 8))
            c_every = max(1, nsteps // (4 * NBLK + 2))
            while not flags["ag2"]:
                step("h")
            k = 0
            for b in range(NBLK):
                QT = QT_ring.next()
                P.dma("sp", QT.t[:, :, :], qT_d.ap()[:, :, b * 512:(b + 1) * 512], R_=[scr], W_=[QT.r])
                mixA = mixA_ring.next()
                for j in range(4):
                    def S_mm(t, j=j, QT=QT):
                        st = take_slot()
                        pending.add(id(st))
                        MM([(st.t[:, 0:512], kT_all.t[0:64, t * 128:(t + 1) * 128], QT.t[0:64, j, :], True, True),
                            (st.t[:, 512:1024], kT_all.t[64:128, t * 128:(t + 1) * 128], QT.t[64:128, j, :], True, True)],
                           [kT_all.r, QT.r], [st.r])
                        return st
                    pend = [S_mm(0)]
                    if T > 1:
                        pend.append(S_mm(1))
                    for t in range(T):
                        st = pend.pop(0)
                        pT = pT_ring.next()
                        ACT(pT.t[:, :], st.t[:, :], AF.Exp, [st.r], [pT.r], scale=8.0)
                        pending.discard(id(st))
                        if t + 2 < T:
                            pend.append(S_mm(t + 2))
                        MM([(obank[0].t[:, :], v_all.t[:, t, 0, :], pT.t[:, 0:512], t == 0, t == T - 1),
                            (obank[1].t[:, :], v_all.t[:, t, 1, :], pT.t[:, 512:1024], t == 0, t == T - 1)],
                           [v_all.r, pT.r], [obank[0].r, obank[1].r])
                        k += 1
                        if k % h_every == 0:
                            step("h")
                        if k % c_every == 0:
                            step("c")
                    for g in range(2):
                        osb, rec = osb_ring.next(), rec_ring.next()
                        CP("dve", osb.t[:, :], obank[g].t[:, :], [obank[g].r], [osb.r])
                        pr = take_slot()
                        MM([(pr.t[0:64, 0:512], shiftM, osb.t[:, :], True, True)], [consts.r, osb.r], [pr.r])
                        P.op("dve", lambda e, o_=rec.t[:, :], i_=pr.t[0:64, 0:512]: e.reciprocal(o_, i_), [pr.r], [rec.r])
                        TT("dve", mixA.t[:, g * 4 + j, :], osb.t[0:64, :], rec.t[:, :], ALU.mult, [osb.r, rec.r], [mixA.r])
                P.dma("sp", mixatt_d.ap()[:, :, b * 512:(b + 1) * 512], mixA.t[:, :, :], R_=[mixA.r], W_=[mixr])
            while side["h"]:
                step("h")
            while side["c"]:
                step("c")
            P.barrier()
            P.flush()
        if "stop_s6" in dbg:
            break

        with ExitStack() as ph:
            SB = lambda shape, dt=F32: alloc(ph, "sb", shape, dt)
            banks = Ring([alloc(ph, "ps", [128, 512], F32) for _ in range(4)])
            accb = [alloc(ph, "ps", [128, 512], F32) for _ in range(4)]
            junk = SB([128, D], BF16)
            st_ring = Ring([SB([128, 2]) for _ in range(8)])
            hb_ring = Ring([SB([128, D], BF16) for _ in range(4)])
            hT_ring = Ring([SB([128, 8, 512], BF16) for _ in range(2)])
            xb_ring = Ring([SB([128, 4, D]) for _ in range(2)])
            aT = SB([128, 32, 512], BF16)
            w1q_ring = Ring([SB([128, 8, 512], BF16) for _ in range(2)])
            w2q_ring = Ring([SB([128, 4, 512], BF16) for _ in range(3)])
            rl_ring = Ring([SB([128, 512]) for _ in range(2)])
            w1l = w1_bf.ap()[l].rearrange("(c p) n -> p c n", p=128)
            w2l = w2_bf.ap()[l].rearrange("(j p) n -> p j n", p=128)
            wo_att = SB([64, 8, D], BF16)
            wo_hg = SB([64, 4, D], BF16)
            wo_cv = SB([128, 2, D], BF16)
            ma_ring = Ring([SB([64, 8, 512], BF16) for _ in range(1)])
            mh_ring = Ring([SB([64, 4, 512], BF16) for _ in range(1)])
            mcv_ring = Ring([SB([128, 2, 512], BF16) for _ in range(1)])
            wol = wout_bf.ap()[l]
            P.dma("sp", wo_att.t[:, :, :], wol[0:512, :].rearrange("(h d) n -> d h n", d=64), R_=[wres], W_=[wo_att.r])
            P.dma("sp", wo_hg.t[:, :, :], wol[512:768, :].rearrange("(h d) n -> d h n", d=64), R_=[wres], W_=[wo_hg.r])
            P.dma("sp", wo_cv.t[:, :, :], wol[768:1024, :].rearrange("(c p) n -> p c n", p=128), R_=[wres], W_=[wo_cv.r])
            def wo_norm(b):
                xb = xb_ring.next()
                ma, mh, mcv = ma_ring.next(), mh_ring.next(), mcv_ring.next()
                P.dma("sp", ma.t[:, :, :], mixatt_d.ap()[:, :, b * 512:(b + 1) * 512], R_=[mixr], W_=[ma.r])
                P.dma("sp", mh.t[:, :, :], mixhg_d.ap()[:, :, b * 512:(b + 1) * 512], R_=[mixr], W_=[mh.r])
                P.dma("sp", mcv.t[:, :, :], mixcv_d.ap()[:, :, b * 512:(b + 1) * 512], R_=[mixr], W_=[mcv.r])
                for s in range(4):
                    P.dma("sp", xb.t[:, s, :], x_cur[b * 512 + s * 128:b * 512 + (s + 1) * 128, :], W_=[xb.r])
                hbs = []
                for s in range(4):
                    for n2 in range(2):
                        pw = banks.next()
                        sl = slice(s * 128, (s + 1) * 128)
                        nl = slice(n2 * 512, (n2 + 1) * 512)
                        grp = [(pw.t[:, :], ma.t[:, h, sl], wo_att.t[:, h, nl], h == 0, False) for h in range(8)]
                        grp += [(pw.t[:, :], mh.t[:, h, sl], wo_hg.t[:, h, nl], False, False) for h in range(4)]
                        grp += [(pw.t[:, :], mcv.t[:, c, sl], wo_cv.t[:, c, nl], False, c == 1) for c in range(2)]
                        MM(grp, [ma.r, mh.r, mcv.r, wo_att.r, wo_hg.r, wo_cv.r], [pw.r])
                        TT("dve", xb.t[:, s, nl], pw.t[:, :], xb.t[:, s, nl], ALU.add, [pw.r, xb.r], [xb.r])
                    st = st_ring.next()
                    ACT(junk.t[:, :], xb.t[:, s, :], AF.Square, [xb.r], [junk.r, st.r], accum_out=st.t[:, 0:1])
                    RSQRT(st.t[:, 1:2], st.t[:, 0:1], 1.0 / D, EPS, [st.r], [st.r])
                    hb = hb_ring.next()
                    TS("dve", hb.t[:, :], xb.t[:, s, :], st.t[:, 1:2], None, ALU.mult, None, [xb.r, st.r], [hb.r])
                    hbs.append(hb)
                return xb, hbs

            def transposes(hbs):
                hT = hT_ring.next()
                for s, hb in enumerate(hbs):
                    pt = banks.next()
                    ptb = pt.t[:, :].bitcast(BF16)
                    TR([(ptb[:, c * 128:(c + 1) * 128], hb.t[:, c * 128:(c + 1) * 128], ident_bf) for c in range(8)],
                       [hb.r, cbf.r], [pt.r])
                    CP("act", hT.t[:, :, s * 128:(s + 1) * 128], ptb.rearrange("p (c t) -> p c t", c=8), [pt.r], [hT.r])
                return hT

            nxt_blk = wo_norm(0)
            nxt_hT = transposes(nxt_blk[1])
            for b in range(NBLK):
                xb, hT = nxt_blk[0], nxt_hT
                if b + 1 < NBLK:
                    nxt_blk = wo_norm(b + 1)
                for fg in range(8):
                    w1q = w1q_ring.next()
                    P.dma("sp", w1q.t[:, :, :], w1l[:, :, fg * 512:(fg + 1) * 512], R_=[wres], W_=[w1q.r])
                    for jj in range(4):
                        pa = banks.next()
                        MM([(pa.t[:, :], w1q.t[:, c, jj * 128:(jj + 1) * 128], hT.t[:, c, :], c == 0, c == 7) for c in range(8)],
                           [w1q.r, hT.r], [pa.r])
                        rl = rl_ring.next()
                        ACT(rl.t[:, :], pa.t[:, :], AF.Relu, [pa.r], [rl.r])
                        TT("dve", aT.t[:, fg * 4 + jj, :], rl.t[:, :], rl.t[:, :], ALU.mult, [rl.r], [aT.r])
                if b + 1 < NBLK:
                    nxt_hT = transposes(nxt_blk[1])
                for n2 in range(2):
                    nl = slice(n2 * 512, (n2 + 1) * 512)
                    for pc in range(8):
                        w2q = w2q_ring.next()
                        P.dma("sp", w2q.t[:, :, :], w2l[:, pc * 4:(pc + 1) * 4, nl], R_=[wres], W_=[w2q.r])
                        for s in range(4):
                            MM([(accb[s].t[:, :], aT.t[:, pc * 4 + jj, s * 128:(s + 1) * 128], w2q.t[:, jj, :],
                                 pc == 0 and jj == 0, pc == 7 and jj == 3) for jj in range(4)], [aT.r, w2q.r], [accb[s].r])
                    for s in range(4):
                        TT("dve", xb.t[:, s, nl], accb[s].t[:, :], xb.t[:, s, nl], ALU.add, [accb[s].r, xb.r], [xb.r])
                for s in range(4):
                    P.dma("sp", x_next[b * 512 + s * 128:b * 512 + (s + 1) * 128, :], xb.t[:, s, :], R_=[xb.r], W_=[x2r])
            P.barrier()
            P.flush()

    P.barrier()
    P.flush()
    es.close()
    return nc


def _consts():
    c = np.zeros((128, K_END), np.float32)
    j = np.arange(128)[:, None]
    i = np.arange(128)[None, :]
    c[:, K_ID:K_ID + 128] = (j == i)
    same = (j // CH) == (i // CH)
    c[:, K_MFW:K_MFW + 128] = same & (j <= i)
    c[:, K_MBW:K_MBW + 128] = same & (j >= i)
    c[:, K_SEL:K_SEL + 4] = (j // CH) == np.arange(4)[None, :]
    c[64, K_SHIFT:K_SHIFT + 64] = 1.0
    c[:, K_ONES:K_ONES + 64] = 1.0
    c[:, K_ONESQ:K_ONESQ + 128] = 1.0 / 256.0
    return c


def _rope_tables(pos):
    inv = (10000.0 ** (-np.arange(0, 32, 2, dtype=np.float32) / 32.0)).astype(np.float32)
    row = (pos // 64).astype(np.float32)[:, None] * inv[None, :]
    col = (pos % 64).astype(np.float32)[:, None] * inv[None, :]
    cr, sr, cc, sc = np.cos(row), np.sin(row), np.cos(col), np.sin(col)
    C = np.concatenate([cr, cr, cc, cc], 1).astype(np.float32)
    S = np.concatenate([-sr, sr, -sc, sc], 1).astype(np.float32)
    return C, S


def prepare(inputs, NT):
    f = lambda a: np.ascontiguousarray(np.asarray(a, dtype=np.float32))
    x = f(inputs["x"])
    B = x.shape[0]
    pp = np.zeros((2, 128, PP_END), np.float32)
    for l in range(2):
        pp[l, :, PP_NM:PP_NM + 8] = f(inputs["norm_mix"])[l].reshape(8, 128).T
        pp[l, :, PP_NMLP:PP_NMLP + 8] = f(inputs["norm_mlp"])[l].reshape(8, 128).T
        pp[l, 0:64, PP_GN:PP_GN + 4] = f(inputs["hgrn_norm"])[l].reshape(4, 64).T
        pp[l, :, PP_CW:PP_CW + 62] = f(inputs["conv_w"])[l].T.reshape(2, 128, 31).transpose(1, 0, 2).reshape(128, 62)
        pp[l, :, PP_CB:PP_CB + 2] = f(inputs["conv_b"])[l].reshape(2, 128).T
        pp[l, :, PP_LNG:PP_LNG + 2] = f(inputs["conv_ln_g"])[l].reshape(2, 128).T
        pp[l, :, PP_LNB:PP_LNB + 2] = f(inputs["conv_ln_b"])[l].reshape(2, 128).T
    qkn = np.concatenate([f(inputs["q_norm"]), f(inputs["k_norm"])], 1).reshape(2, 1, 128)
    lbp = np.concatenate([f(inputs["hgrn_lb_fwd"]).reshape(-1), f(inputs["hgrn_lb_bwd"]).reshape(-1)]).reshape(1, 1024)
    consts = _consts()
    shared = {"w_in": f(inputs["w_in"]), "w_out": f(inputs["w_out"]), "w_mlp_in": f(inputs["w_mlp_in"]),
              "w_mlp_out": f(inputs["w_mlp_out"]), "pp": pp, "qkn": np.ascontiguousarray(qkn), "lbp": lbp, "consts": consts}
    maps = []
    for c in range(B * R):
        b, r = c // R, c % R
        pos = np.arange(r * NT, (r + 1) * NT)
        C, S = _rope_tables(pos)
        oh = np.zeros((1, 12), np.float32)
        oh[0, r] = 1.0
        if r > 0:
            oh[0, 4 + r - 1] = 1.0
        if r < R - 1:
            oh[0, 8 + r + 1] = 1.0
        m = dict(shared)
        C = np.ascontiguousarray(C.reshape(NT // 128, 128, 64).transpose(1, 0, 2))
        S = np.ascontiguousarray(S.reshape(NT // 128, 128, 64).transpose(1, 0, 2))
        m.update({"x": np.ascontiguousarray(x[b, r * NT:(r + 1) * NT]), "ropeC": C, "ropeS": S, "oh": oh})
        maps.append(m)
    return maps


_CACHE = {}


def kernel(**inputs):
    x = np.asarray(inputs["x"])
    B, Lfull, _ = x.shape
    NT = Lfull // R
    if NT not in _CACHE:
        _CACHE[NT] = build_program(NT)
    nc = _CACHE[NT]
    maps = prepare(inputs, NT)
    res = run_bass_kernel_spmd(nc, maps, core_ids=list(range(B * R)))
    out = np.empty((B, Lfull, D), np.float32)
    for c in range(B * R):
        out[c // R, (c % R) * NT:(c % R + 1) * NT] = np.asarray(res.results[c]["out"], dtype=np.float32)
    return out
```
